# Optimizing a Trainium2 kernel written in Bass

```python
import math
import jax, jax.numpy as jnp
from jax import lax
import numpy as np

D_MODEL = 1024
BATCH = 8
SEQ = 4096
DEPTH = 2

HEAD_DIM = 64
GRID_W = 64
Q_BLOCK = 128
ROPE_THETA = 10000.0
MLA_HEADS = 6
MLA_Q_RANK = 256
MLA_KV_RANK = 128
MLA_NOPE_DIM = 64
MLA_ROPE_DIM = 32
MLA_V_DIM = 64
DIL_HEADS = 6
DIL_BRANCHES = ((128, 1), (512, 4), (2048, 16))
GQA_Q_HEADS = 4
GQA_KV_HEADS = 2
REL_BUCKETS = 32
REL_MAX_DIST = 1024
D_FF = -(-8 * D_MODEL // (3 * 256)) * 256
MLA_IN = MLA_Q_RANK + MLA_KV_RANK + MLA_ROPE_DIM
DIL_IN = 3 * DIL_HEADS * HEAD_DIM
GQA_IN = (GQA_Q_HEADS + 2 * GQA_KV_HEADS) * HEAD_DIM
IN_WIDTH = MLA_IN + DIL_IN + GQA_IN
MIX_WIDTH = MLA_HEADS * MLA_V_DIM + DIL_HEADS * HEAD_DIM + GQA_Q_HEADS * HEAD_DIM
DN_ALPHA = (2.0 * DEPTH) ** 0.25
DN_BETA = (8.0 * DEPTH) ** -0.25
NEG_INF = -1e30

kernel_name = "hybrid_mla_dilated_axialgqa_deepnorm_encoder"


def rms_norm(x, g, eps=1e-6):
    xf = x.astype(jnp.float32)
    y = xf * lax.rsqrt(jnp.mean(xf * xf, axis=-1, keepdims=True) + eps)
    return (y * g.astype(jnp.float32)).astype(x.dtype)


def layer_norm(x, g, b, eps=1e-5):
    xf = x.astype(jnp.float32)
    mu = jnp.mean(xf, axis=-1, keepdims=True)
    var = jnp.mean(jnp.square(xf - mu), axis=-1, keepdims=True)
    y = (xf - mu) * lax.rsqrt(var + eps)
    return (y * g.astype(jnp.float32) + b.astype(jnp.float32)).astype(x.dtype)


def rope(x, pos):
    d = x.shape[-1]
    inv = ROPE_THETA ** (-jnp.arange(0, d, 2, dtype=jnp.float32) / d)
    ang = pos[:, None] * inv[None, :]
    cos = jnp.cos(ang)[None, :, None, :]
    sin = jnp.sin(ang)[None, :, None, :]
    xf = x.astype(jnp.float32)
    x1, x2 = xf[..., : d // 2], xf[..., d // 2:]
    return jnp.concatenate([x1 * cos - x2 * sin, x1 * sin + x2 * cos], axis=-1).astype(x.dtype)


def t5_bucket(rel):
    nb = REL_BUCKETS // 2
    exact = nb // 2
    ret = jnp.where(rel > 0, nb, 0)
    n = jnp.abs(rel)
    nf = jnp.maximum(n, 1).astype(jnp.float32)
    large = exact + (jnp.log(nf / exact) / math.log(REL_MAX_DIST / exact) * (nb - exact)).astype(jnp.int32)
    large = jnp.minimum(large, nb - 1)
    return ret + jnp.where(n < exact, n, large)


def dense_attention(q, k, v, scale):
    B, S, H, Dk = q.shape
    Hkv, Dv = k.shape[2], v.shape[-1]
    G = H // Hkv
    nblk = S // Q_BLOCK
    qb = q.reshape(B, nblk, Q_BLOCK, Hkv, G, Dk).transpose(1, 0, 2, 3, 4, 5)

    def one_block(qblk):
        logits = jnp.einsum('bqkgd,bskd->bkgqs', qblk, k).astype(jnp.float32) * scale
        p = jax.nn.softmax(logits, axis=-1)
        return jnp.einsum('bkgqs,bskd->bqkgd', p.astype(v.dtype), v)

    o = lax.map(one_block, qb)
    return o.transpose(1, 0, 2, 3, 4, 5).reshape(B, S, H, Dv)


def dilated_branch(q, k, v, rel_bias, window, dil):
    B, S, H, D = q.shape
    half = window // (2 * dil)
    L = S // dil
    nb = -(-L // half)
    Lp = nb * half
    BB = B * dil

    def to_sub(t):
        return t.reshape(B, L, dil, H, D).transpose(0, 2, 1, 3, 4).reshape(BB, L, H, D)

    qs, ks, vs = to_sub(q), to_sub(k), to_sub(v)
    qb = jnp.pad(qs, ((0, 0), (0, Lp - L), (0, 0), (0, 0))).reshape(BB, nb, half, H, D)

    def band(t):
        tp = jnp.pad(t, ((0, 0), (half, Lp - L + half), (0, 0), (0, 0))).reshape(BB, nb + 2, half, H, D)
        return jnp.concatenate([tp[:, :-2], tp[:, 1:-1], tp[:, 2:]], axis=2)

    kb, vb = band(ks), band(vs)
    rel = jnp.arange(3 * half)[None, :] - half - jnp.arange(half)[:, None]
    bias = jnp.transpose(rel_bias[t5_bucket(rel * dil)], (2, 0, 1)).astype(jnp.float32)
    key_idx = jnp.arange(nb)[:, None] * half - half + jnp.arange(3 * half)[None, :]
    mask = (jnp.abs(rel) <= half)[None] & ((key_idx >= 0) & (key_idx < L))[:, None, :]

    logits = jnp.einsum('bnqhd,bnkhd->bnhqk', qb, kb).astype(jnp.float32) * (HEAD_DIM ** -0.5)
    logits = jnp.where(mask[None, :, None], logits + bias[None, None], NEG_INF)
    m = jnp.max(logits, axis=-1, keepdims=True)
    e = jnp.exp(logits - m)
    s = jnp.sum(e, axis=-1)
    o = jnp.einsum('bnhqk,bnkhd->bnqhd', e.astype(v.dtype), vb).astype(jnp.float32)
    o = o / jnp.transpose(s, (0, 1, 3, 2))[..., None]
    lse = jnp.transpose(m[..., 0] + jnp.log(s), (0, 1, 3, 2))
    o = o.reshape(BB, Lp, H, D)[:, :L]
    lse = lse.reshape(BB, Lp, H)[:, :L]
    o = o.reshape(B, dil, L, H, D).transpose(0, 2, 1, 3, 4).reshape(B, S, H, D)
    lse = lse.reshape(B, dil, L, H).transpose(0, 2, 1, 3).reshape(B, S, H)
    return o, lse


def dilated_mixture(q, k, v, rel_bias):
    outs, lses = [], []
    for window, dil in DIL_BRANCHES:
        o, lse = dilated_branch(q, k, v, rel_bias, window, dil)
        outs.append(o)
        lses.append(lse)
    w = jax.nn.softmax(jnp.stack(lses, axis=0), axis=0)
    o = jnp.sum(w[..., None] * jnp.stack(outs, axis=0), axis=0)
    return o.astype(q.dtype)


def setup_inputs(seed: int = 0) -> dict:
    key = jax.random.key(seed)
    ks = jax.random.split(key, 20)
    f32 = jnp.float32

    def nrm(k, shape, scale):
        return jax.random.normal(k, shape, f32) * scale

    def gain(k, shape):
        return 1.0 + 0.02 * jax.random.normal(k, shape, f32)

    return {
        "x": jax.random.normal(ks[0], (BATCH, SEQ, D_MODEL), f32),
        "w_in": nrm(ks[1], (DEPTH, D_MODEL, IN_WIDTH), D_MODEL ** -0.5),
        "mla_q_norm": gain(ks[2], (DEPTH, MLA_Q_RANK)),
        "mla_kv_norm": gain(ks[3], (DEPTH, MLA_KV_RANK)),
        "mla_w_uq": nrm(ks[4], (DEPTH, MLA_Q_RANK, MLA_HEADS * (MLA_NOPE_DIM + MLA_ROPE_DIM)), MLA_Q_RANK ** -0.5),
        "mla_w_ukv": nrm(ks[5], (DEPTH, MLA_KV_RANK, MLA_HEADS * (MLA_NOPE_DIM + MLA_V_DIM)), MLA_KV_RANK ** -0.5),
        "gqa_q_norm": gain(ks[6], (DEPTH, HEAD_DIM)),
        "gqa_k_norm": gain(ks[7], (DEPTH, HEAD_DIM)),
        "rel_bias": nrm(ks[8], (REL_BUCKETS, DIL_HEADS), 0.1),
        "w_out": nrm(ks[9], (DEPTH, MIX_WIDTH, D_MODEL), DN_BETA * MIX_WIDTH ** -0.5),
        "ln1_g": gain(ks[10], (DEPTH, D_MODEL)),
        "ln1_b": nrm(ks[11], (DEPTH, D_MODEL), 0.02),
        "ffn_w_gate": nrm(ks[12], (DEPTH, D_MODEL, D_FF), D_MODEL ** -0.5),
        "ffn_w_up": nrm(ks[13], (DEPTH, D_MODEL, D_FF), D_MODEL ** -0.5),
        "ffn_w_down": nrm(ks[14], (DEPTH, D_FF, D_MODEL), DN_BETA * D_FF ** -0.5),
        "ln2_g": gain(ks[15], (DEPTH, D_MODEL)),
        "ln2_b": nrm(ks[16], (DEPTH, D_MODEL), 0.02),
    }


def reference(x, w_in, mla_q_norm, mla_kv_norm, mla_w_uq, mla_w_ukv, gqa_q_norm, gqa_k_norm,
              rel_bias, w_out, ln1_g, ln1_b, ffn_w_gate, ffn_w_up, ffn_w_down, ln2_g, ln2_b):
    B, S, _ = x.shape
    rows = S // GRID_W
    pos = jnp.arange(S, dtype=jnp.float32)
    row_pos = jnp.repeat(jnp.arange(rows), GRID_W).astype(jnp.float32)
    col_pos = jnp.tile(jnp.arange(GRID_W), rows).astype(jnp.float32)
    half_rot = HEAD_DIM // 2

    for l in range(DEPTH):
        h = x @ w_in[l]
        o0 = 0
        cq = rms_norm(h[..., o0:o0 + MLA_Q_RANK], mla_q_norm[l]); o0 += MLA_Q_RANK
        ckv = rms_norm(h[..., o0:o0 + MLA_KV_RANK], mla_kv_norm[l]); o0 += MLA_KV_RANK
        k_rope = h[..., o0:o0 + MLA_ROPE_DIM][:, :, None, :]; o0 += MLA_ROPE_DIM
        qa = (cq @ mla_w_uq[l]).reshape(B, S, MLA_HEADS, MLA_NOPE_DIM + MLA_ROPE_DIM)
        qa = jnp.concatenate([qa[..., :MLA_NOPE_DIM], rope(qa[..., MLA_NOPE_DIM:], pos)], axis=-1)
        kva = (ckv @ mla_w_ukv[l]).reshape(B, S, MLA_HEADS, MLA_NOPE_DIM + MLA_V_DIM)
        k_rope = jnp.broadcast_to(rope(k_rope, pos), (B, S, MLA_HEADS, MLA_ROPE_DIM))
        ka = jnp.concatenate([kva[..., :MLA_NOPE_DIM], k_rope], axis=-1)
        va = kva[..., MLA_NOPE_DIM:]
        out_a = dense_attention(qa, ka, va, (MLA_NOPE_DIM + MLA_ROPE_DIM) ** -0.5)
        out_a = out_a.reshape(B, S, MLA_HEADS * MLA_V_DIM)
        hb = h[..., o0:o0 + DIL_IN].reshape(B, S, 3, DIL_HEADS, HEAD_DIM); o0 += DIL_IN
        out_b = dilated_mixture(hb[:, :, 0], hb[:, :, 1], hb[:, :, 2], rel_bias)
        out_b = out_b.reshape(B, S, DIL_HEADS * HEAD_DIM)
        nq, nkv = GQA_Q_HEADS * HEAD_DIM, GQA_KV_HEADS * HEAD_DIM
        qc = h[..., o0:o0 + nq].reshape(B, S, GQA_Q_HEADS, HEAD_DIM); o0 += nq
        kc = h[..., o0:o0 + nkv].reshape(B, S, GQA_KV_HEADS, HEAD_DIM); o0 += nkv
        vc = h[..., o0:o0 + nkv].reshape(B, S, GQA_KV_HEADS, HEAD_DIM); o0 += nkv
        qc = rms_norm(qc, gqa_q_norm[l])
        kc = rms_norm(kc, gqa_k_norm[l])
        qc = jnp.concatenate([rope(qc[..., :half_rot], row_pos), rope(qc[..., half_rot:], col_pos)], axis=-1)
        kc = jnp.concatenate([rope(kc[..., :half_rot], row_pos), rope(kc[..., half_rot:], col_pos)], axis=-1)
        out_c = dense_attention(qc, kc, vc, HEAD_DIM ** -0.5).reshape(B, S, GQA_Q_HEADS * HEAD_DIM)
        mix = jnp.concatenate([out_a, out_b, out_c], axis=-1) @ w_out[l]
        x = layer_norm(DN_ALPHA * x + mix, ln1_g[l], ln1_b[l])
        ff = (jax.nn.silu(x @ ffn_w_gate[l]) * (x @ ffn_w_up[l])) @ ffn_w_down[l]
        x = layer_norm(DN_ALPHA * x + ff, ln2_g[l], ln2_b[l])
    return x
```

```python
import math
import numpy as np
import ml_dtypes
import concourse.bass as bass
import concourse.mybir as mybir
from concourse.bass_utils import run_bass_kernel_spmd
from concourse.alu_op_type import AluOpType as ALU

AF = mybir.ActivationFunctionType
F32 = mybir.dt.float32
BF16 = mybir.dt.bfloat16
AX = mybir.AxisListType

S = 4096
D = 1024
NT = 32
NS = 8
TS = 512
DEPTH = 2
IN_W = 2080
DFF = 2816
NJ = 22
LR = 3072
TW = 2944
DN_ALPHA = (2.0 * DEPTH) ** 0.25
C_CQ, C_CKV, C_KR, C_DQ, C_DK, C_DV, C_GQ, C_GK, C_GV = 0, 256, 384, 416, 800, 1184, 1568, 1824, 1952


class Buf:
    __slots__ = ("name", "w", "r", "dkey", "excl")

    def __init__(self, name, excl=False):
        self.name = name
        self.w = None
        self.r = []
        self.dkey = None
        self.excl = excl


class Ctx:
    def __init__(self, nc):
        self.nc = nc
        self.engs = {"pe": nc.tensor, "act": nc.scalar, "dve": nc.vector, "pool": nc.gpsimd, "sp": nc.sync}
        self.sem = {}
        self.cnt = {}
        self.seen = {e: {} for e in self.engs}
        self.dma_keys = set()
        for e in ("pe", "act", "dve", "pool"):
            self._mksem(e)

    def _mksem(self, key):
        self.sem[key] = self.nc.alloc_semaphore("s_" + key)
        self.cnt[key] = 0

    def _wait(self, eng, need):
        e = self.engs[eng]
        seen = self.seen[eng]
        for k, v in need.items():
            if k in self.dma_keys:
                v = self.cnt[k]
            if seen.get(k, 0) >= v:
                continue
            e.wait_ge(self.sem[k], v)
            seen[k] = v

    def _deps(self, eng, reads, writes):
        need = {}
        for b in reads:
            if b.w is not None:
                k, v = b.w
                if need.get(k, 0) < v:
                    need[k] = v
            if b.excl:
                for (k, v) in b.r:
                    if k != eng and need.get(k, 0) < v:
                        need[k] = v
        same = eng != "pe"
        for b in writes:
            if b.w is not None:
                k, v = b.w
                if (k != eng or same) and need.get(k, 0) < v:
                    need[k] = v
            for (k, v) in b.r:
                if (k != eng or same) and need.get(k, 0) < v:
                    need[k] = v
        return need

    def _record(self, ev, reads, writes):
        for b in reads:
            b.r.append(ev)
            if len(b.r) > 16:
                m = {}
                for k, v in b.r:
                    if m.get(k, 0) < v:
                        m[k] = v
                b.r = list(m.items())
        for b in writes:
            b.w = ev
            b.r = []

    def op(self, eng, fn, reads=(), writes=(), signal=True):
        self._wait(eng, self._deps(eng, reads, writes))
        ins = fn(self.engs[eng])
        if signal:
            self.cnt[eng] += 1
            ins.then_inc(self.sem[eng], 1)
            ev = (eng, self.cnt[eng])
        else:
            ev = (eng, self.cnt[eng] + 1)
        self._record(ev, reads, writes)
        return ins

    def dma(self, q, out, in_, owner, reads=(), writes=(), **kw):
        kind = "S" if q == "pool" else "H"
        if owner.dkey is None:
            fk = [k for k in getattr(self, "free_keys", []) if k[1] == kind]
            if fk:
                owner.dkey = fk[-1]
                self.free_keys.remove(fk[-1])
            else:
                owner.dkey = "d%s%d" % (kind, len(self.dma_keys))
                self._mksem(owner.dkey)
                self.dma_keys.add(owner.dkey)
        assert owner.dkey[1] == kind, (owner.name, owner.dkey, q)
        k = owner.dkey
        self._wait(q, self._deps(q, reads, writes))
        ins = self.engs[q].dma_start(out=out, in_=in_, **kw)
        self.cnt[k] += 16
        ins.then_inc(self.sem[k], 16)
        self._record((k, self.cnt[k]), reads, writes)
        return ins

    def barrier(self):
        need = {k: v for k, v in self.cnt.items() if v > 0}
        for e in self.engs:
            self._wait(e, {k: v for k, v in need.items() if k != e})
        keep = set()
        for b in getattr(self, "persist", []):
            b.w = None
            b.r = []
            if b.dkey is not None:
                keep.add(b.dkey)
        self.free_keys = [k for k in sorted(self.dma_keys) if k not in keep]


class G:
    pass


_UID = [0]


def _uname(name):
    _UID[0] += 1
    return "%s_%d" % (name, _UID[0])


def sb(es, g, name, shape, dt):
    return es.enter_context(g.nc.sbuf_tensor(_uname(name), list(shape), dt)).ap()


def ps(es, g, name, shape, dt=F32):
    return es.enter_context(g.nc.psum_tensor(_uname(name), list(shape), dt)).ap()


def _rope_tables():
    f32 = np.float32
    t = np.arange(S, dtype=f32)

    def cs(pos, d):
        inv = (np.float32(10000.0) ** (-np.arange(0, d, 2, dtype=f32) / f32(d))).astype(f32)
        ang = (pos[:, None] * inv[None, :]).astype(f32)
        return np.cos(ang).astype(f32), np.sin(ang).astype(f32)

    ca, sa = cs(t, 32)
    ropeA = np.stack([ca, sa], axis=1)
    ropeA = ropeA.reshape(NT, 128, 2, 16).transpose(1, 0, 2, 3).copy()
    row = np.repeat(np.arange(S // 64), 64).astype(f32)
    col = np.tile(np.arange(64), S // 64).astype(f32)
    cr, sr = cs(row, 32)
    cc, sc = cs(col, 32)
    ropeG = np.stack([np.stack([cr, cc], axis=1), np.stack([sr, sc], axis=1)], axis=1)
    ropeG = ropeG.reshape(NT, 128, 2, 2, 16).transpose(1, 0, 2, 3, 4).copy()
    return ropeA, ropeG


def _t5_bucket(rel):
    nb = 16
    exact = 8
    ret = np.where(rel > 0, nb, 0)
    n = np.abs(rel)
    nf = np.maximum(n, 1).astype(np.float32)
    large = exact + (np.log(nf / np.float32(exact)) / np.float32(math.log(1024 / exact)) * np.float32(nb - exact)).astype(np.int32)
    large = np.minimum(large, nb - 1)
    return ret + np.where(n < exact, n, large)


def _dil_tables(rel_bias):
    d = 1535 - np.arange(LR)
    mult = ((np.abs(d) <= 64).astype(np.int32) + ((d % 4 == 0) & (np.abs(d) <= 256)).astype(np.int32)
            + ((d % 16 == 0) & (np.abs(d) <= 1024)).astype(np.int32))
    logm = np.where(mult > 0, np.log(np.maximum(mult, 1).astype(np.float64)), -30000.0).astype(np.float32)
    bucket = _t5_bucket(d)
    bias_g = np.ascontiguousarray(rel_bias[bucket, :].T).astype(np.float32)
    logm6 = np.ascontiguousarray(np.broadcast_to(logm[None, :], (6, LR))).astype(np.float32)
    return bias_g, logm6


from contextlib import ExitStack


CUT = [None]
DBG_NT = [NT]
NOSKEW = [False]


class _Cut(Exception):
    pass


def chk(n):
    if CUT[0] == n:
        raise _Cut()


def cp(c, eng, out, in_, reads, writes):
    if eng == "act":
        return c.op("act", lambda e: e.activation(out, in_, AF.Copy), reads=reads, writes=writes)
    return c.op(eng, lambda e: e.tensor_copy(out, in_), reads=reads, writes=writes)


def phase_consts(g, es):
    c = g.c
    g.ident = sb(es, g, "ident", [128, 128], BF16)
    g.antiI = sb(es, g, "antiI", [128, 128], BF16)
    g.b_const = Buf("const")
    c.dma("pool", g.ident, g.d_ident, g.b_const, writes=[g.b_const])
    c.dma("pool", g.antiI, g.d_antiI, g.b_const, writes=[g.b_const])
    g.mhalf = sb(es, g, "mhalf", [128, 8], F32)
    g.b_mh = Buf("mhalf")
    c.op("pool", lambda e: e.memset(g.mhalf, -0.5), writes=[g.b_mh])


def phase_x0(g):
    c = g.c
    with ExitStack() as es:
        idf = sb(es, g, "x0_idf", [128, 128], F32)
        xf = [sb(es, g, "x0_xf%d" % i, [128, D], F32) for i in range(3)]
        xts = [sb(es, g, "x0_xt%d" % i, [128, 8, TS], BF16) for i in range(2)]
        pT = [ps(es, g, "x0_pT%d" % i, [128, 8, 128], F32) for i in range(2)]
        bidf = Buf("idf")
        bxf = [Buf("xf%d" % i) for i in range(3)]
        bxt = [Buf("xt%d" % i) for i in range(2)]
        bpT = [Buf("pT%d" % i) for i in range(2)]
        c.dma("sp", idf, g.d_ident, bidf, writes=[bidf])
        xT_v = g.xT.rearrange("(kc p) n -> p kc n", p=128)
        for s in range(NS):
            for t in range(4):
                tt = s * 4 + t
                i = tt % 2
                f = tt % 3
                c.dma("sp", xf[f], g.x[tt * 128:(tt + 1) * 128, :], bxf[f], writes=[bxf[f]])
                for kc in range(8):
                    c.op("pe", lambda e: e.transpose(pT[i][:, kc, :], xf[f][:, kc * 128:(kc + 1) * 128], idf),
                         reads=[bxf[f], bidf], writes=[bpT[i]], signal=(kc == 7))
                cp(c, "dve" if tt % 2 == 0 else "act", xts[s % 2][:, :, t * 128:(t + 1) * 128], pT[i],
                   [bpT[i]], [bxt[s % 2]])
            c.dma("act", xT_v[:, :, s * TS:(s + 1) * TS], xts[s % 2], bxt[s % 2], reads=[bxt[s % 2]], writes=[g.b_scr])
        c.barrier()


def phase_w(g):
    c = g.c
    with ExitStack() as es:
        st = [sb(es, g, "w_st%d" % i, [128, 8, 512], F32) for i in range(2)]
        wb = [sb(es, g, "w_wb%d" % i, [128, 8, 512], BF16) for i in range(2)]
        bst = [Buf("wst%d" % i) for i in range(2)]
        bwb = [Buf("wwb%d" % i) for i in range(2)]
        n = 0
        for l in range(DEPTH):
            for gu, w in enumerate((g.wg, g.wu)):
                wv = w[l].rearrange("(kc p) n -> p kc n", p=128)
                for c0 in range(0, DFF, 512):
                    cw = min(512, DFF - c0)
                    i = n % 2
                    c.dma("sp", st[i][:, :, 0:cw], wv[:, :, c0:c0 + cw], bst[i], writes=[bst[i]])
                    eng = ("dve", "pool")[n % 2]
                    cp(c, eng, wb[i][:, :, 0:cw], st[i][:, :, 0:cw], [bst[i]], [bwb[i]])
                    for jj in range(cw // 128):
                        j = c0 // 128 + jj
                        c.dma("sp", g.wgu[l, gu, j], wb[i][:, :, jj * 128:(jj + 1) * 128], bwb[i],
                              reads=[bwb[i]], writes=[g.b_scr])
                    n += 1
        c.barrier()


def phase_b0(g):
    c = g.c
    with ExitStack() as es:
        a = sb(es, g, "b0_a", [6, LR], F32)
        b = sb(es, g, "b0_b", [6, LR], F32)
        o = sb(es, g, "b0_o", [6, LR], BF16)
        ba, bb, bo = Buf("b0a"), Buf("b0b"), Buf("b0o")
        c.dma("sp", a, g.d_biasg, ba, writes=[ba])
        c.dma("sp", b, g.d_logm, bb, writes=[bb])
        c.op("dve", lambda e: e.tensor_tensor(a, a, b, ALU.add), reads=[ba, bb], writes=[ba])
        c.op("dve", lambda e: e.tensor_scalar(o, a, 8.0, None, ALU.mult), reads=[ba], writes=[bo])
        c.dma("sp", g.rtab, o, bo, reads=[bo], writes=[g.b_scr])
        c.barrier()


def skew(stages, tiles, rev=False):
    n = len(tiles)
    if NOSKEW[0]:
        for t_ in tiles:
            for st in stages:
                st(t_)
        return
    order = list(enumerate(stages))
    if rev:
        order = order[::-1]
    for k in range(n + len(stages) - 1):
        for si, st in order:
            idx = k - si
            if 0 <= idx < n:
                st(tiles[idx])


def phase_a1(g, l, after_loads=None):
    c = g.c
    with ExitStack() as es:
        win = sb(es, g, "a1_win", [128, 8, 416], BF16)
        wuq_s = sb(es, g, "a1_wuqs", [128, 2, 576], F32)
        wukv_s = sb(es, g, "a1_wukvs", [128, 768], F32)
        wuq = sb(es, g, "a1_wuq", [128, 2, 576], BF16)
        wukv = sb(es, g, "a1_wukv", [128, 768], BF16)
        gn = sb(es, g, "a1_gn", [128, 3], F32)
        ropeA = sb(es, g, "a1_rope", [128, NT, 2, 16], F32)
        xts = [sb(es, g, "a1_xt%d" % i, [128, 8, TS], BF16) for i in range(2)]
        junk = sb(es, g, "a1_junk", [128, 256], F32)
        qaTs = [sb(es, g, "a1_qaT%d" % i, [128, 6, TS], BF16) for i in range(2)]
        kaTs = [sb(es, g, "a1_kaT%d" % i, [128, 6, TS], BF16) for i in range(2)]

        def slots(name, shape, dt, n):
            return [sb(es, g, "a1_%s%d" % (name, i), shape, dt) for i in range(n)], [Buf("%s%d" % (name, i)) for i in range(n)]

        hsb, bhsb = slots("hsb", [128, 416], F32, 4)
        st, bst = slots("st", [128, 2], F32, 2)
        st2, bst2 = slots("stb", [128, 2], F32, 2)
        rstd, brstd = slots("rstd", [128, 2], F32, 2)
        kr, bkr = slots("kr", [128, 4, 16], F32, 8)
        cn, bcn = slots("cn", [128, 384], BF16, 2)
        cT, bcT = slots("cT", [128, 3, 128], BF16, 2)
        qr, bqr = slots("qr", [128, 4, 6, 16], F32, 2)
        qab, bqab = slots("qab", [128, 6, 96], BF16, 3)
        kab, bkab = slots("kab", [128, 6, 96], BF16, 5)
        vab, bvab = slots("vab", [128, 6, 64], BF16, 2)
        pH = ps(es, g, "a1_pH", [128, 512])
        pX = ps(es, g, "a1_pX", [128, 4, 512])
        pTc = ps(es, g, "a1_pTc", [128, 8, 128], BF16)
        pTq = ps(es, g, "a1_pTq", [128, 8, 128], BF16)
        pTk = ps(es, g, "a1_pTk", [128, 8, 128], BF16)
        B = lambda n: Buf(n)
        bw, bwu, bwus, bwk, bwks, bgn, brope = B("win"), B("wuq"), B("wuqs"), B("wukv"), B("wukvs"), B("gn"), B("rope")
        bxt = [B("xt0"), B("xt1")]
        bjunk, bpH, bpTc, bpTq, bpTk = B("junk"), B("pH"), B("pTc"), B("pTq"), B("pTk")
        bqaT, bkaT = [B("qaT0"), B("qaT1")], [B("kaT0"), B("kaT1")]
        bqa, bkv, bqr2, bvv = B("pqa"), B("pkv"), B("pqr"), B("pvv")
        wv = g.w_in[l].rearrange("(kc p) n -> p kc n", p=128)
        for kc in range(8):
            c.dma("pool", win[:, kc, :], wv[:, kc, 0:416], bw, writes=[bw])
        wq = g.w_uq[l].rearrange("(kc p) (h d) -> p kc h d", p=128, d=96)
        for kc in range(2):
            for (c0, d0, dw) in ((0, 0, 64), (384, 64, 16), (480, 80, 16)):
                c.dma("sp", wuq_s[:, kc, c0:c0 + 6 * dw].rearrange("p (h d) -> p h d", d=dw), wq[:, kc, :, d0:d0 + dw],
                      bwus, writes=[bwus])
        wk = g.w_ukv[l].rearrange("p (h d) -> p h d", d=128)
        for a_ in range(2):
            c.dma("sp", wukv_s[:, a_ * 384:(a_ + 1) * 384].rearrange("p (h d) -> p h d", d=64), wk[:, :, a_ * 64:(a_ + 1) * 64],
                  bwks, writes=[bwks])
        for kc in range(2):
            c.dma("sp", gn[:, kc:kc + 1], g.qn[l][kc * 128:(kc + 1) * 128].rearrange("(p o) -> p o", o=1), bgn, writes=[bgn])
        c.dma("sp", gn[:, 2:3], g.kvn[l].rearrange("(p o) -> p o", o=1), bgn, writes=[bgn])
        c.dma("sp", ropeA, g.d_ropeA, brope, writes=[brope])
        for kc in range(2):
            c.op("dve", lambda e: e.tensor_scalar(wuq[:, kc, :], wuq_s[:, kc, :], gn[:, kc:kc + 1], None, ALU.mult),
                 reads=[bwus, bgn], writes=[bwu])
        c.op("dve", lambda e: e.tensor_scalar(wukv, wukv_s, gn[:, 2:3], None, ALU.mult), reads=[bwks, bgn], writes=[bwk])
        xT_v = g.xT.rearrange("(kc p) n -> p kc n", p=128)
        qaT_v = g.qaT.rearrange("h p n -> p h n")
        kaT_v = g.kaT.rearrange("h p n -> p h n")
        va_v = g.va.rearrange("h p t d -> p t h d")
        v6 = lambda ap, d: ap.rearrange("p (h d) -> p h d", d=d)

        def U1(tt):
            s, t = tt // 4, tt % 4
            si = s % 2
            xs = xts[si]
            if tt == 0:
                c.dma("sp", xs, xT_v[:, :, 0:TS], bxt[0], writes=[bxt[0]])
            if t == 1 and s + 1 < NS:
                c.dma("sp", xts[1 - si], xT_v[:, :, (s + 1) * TS:(s + 2) * TS], bxt[1 - si], writes=[bxt[1 - si]])
            if tt == min(6, DBG_NT[0] - 1) and after_loads is not None:
                after_loads()
            for kc in range(8):
                c.op("pe", lambda e: e.matmul(pH[:, 0:416], xs[:, kc, t * 128:(t + 1) * 128], win[:, kc, :],
                                              start=(kc == 0), stop=(kc == 7)),
                     reads=[bxt[si], bw], writes=[bpH], signal=(kc == 7))

        def U2(tt):
            h_, bh_ = hsb[tt % 4], bhsb[tt % 4]
            s_, bs_ = st[tt % 2], bst[tt % 2]
            cp(c, "act", h_, pH[:, 0:416], [bpH], [bh_])
            c.op("act", lambda e: e.activation(junk, pH[:, 0:256], AF.Square, scale=1.0 / 16.0, accum_out=s_[:, 0:1]),
                 reads=[bpH], writes=[bjunk, bs_])
            c.op("act", lambda e: e.activation(junk[:, 0:128], pH[:, 256:384], AF.Square,
                                               scale=1.0 / math.sqrt(128.0), accum_out=s_[:, 1:2]),
                 reads=[bpH], writes=[bjunk, bs_])

        def U3(tt):
            h_, bh_ = hsb[tt % 4], bhsb[tt % 4]
            c.op("dve", lambda e: e.tensor_scalar(st2[tt % 2], st[tt % 2], 1e-6, None, ALU.add), reads=[bst[tt % 2]], writes=[bst2[tt % 2]])
            cos, sin = ropeA[:, tt, 0, :], ropeA[:, tt, 1, :]
            x1, x2 = h_[:, 384:400], h_[:, 400:416]
            for q_, (xa, tb) in enumerate(((x1, cos), (x2, sin), (x1, sin), (x2, cos))):
                c.op("dve", lambda e: e.tensor_tensor(kr[tt % 8][:, q_, :], xa, tb, ALU.mult),
                     reads=[bh_, brope], writes=[bkr[tt % 8]])

        def U4(tt):
            c.op("pool", lambda e: e.tensor_tensor(rstd[tt % 2], st2[tt % 2], g.mhalf[:, 0:2], ALU.pow),
                 reads=[bst2[tt % 2], g.b_mh], writes=[brstd[tt % 2]])

        def U5(tt):
            h_, bh_ = hsb[tt % 4], bhsb[tt % 4]
            r_, br_ = rstd[tt % 2], brstd[tt % 2]
            c.op("act", lambda e: e.activation(cn[tt % 2][:, 0:256], h_[:, 0:256], AF.Copy, scale=r_[:, 0:1]),
                 reads=[bh_, br_], writes=[bcn[tt % 2]])
            c.op("dve", lambda e: e.tensor_scalar(cn[tt % 2][:, 256:384], h_[:, 256:384], r_[:, 1:2], None, ALU.mult),
                 reads=[bh_, br_], writes=[bcn[tt % 2]])

        def U6(tt):
            for b_ in range(3):
                c.op("pe", lambda e: e.transpose(pTc[:, b_, :], cn[tt % 2][:, b_ * 128:(b_ + 1) * 128], g.ident),
                     reads=[bcn[tt % 2], g.b_const], writes=[bpTc], signal=(b_ == 2))

        def U7(tt):
            cp(c, "act", cT[tt % 2], pTc[:, 0:3, :], [bpTc], [bcT[tt % 2]])

        def U8(tt):
            ct_, bct_ = cT[tt % 2], bcT[tt % 2]
            for kc in range(2):
                c.op("pe", lambda e: e.matmul(pX[:, 0, 0:384], ct_[:, kc, :], wuq[:, kc, 0:384], start=(kc == 0), stop=(kc == 1)),
                     reads=[bct_, bwu], writes=[bqa], signal=(kc == 1))
            for kc in range(2):
                c.op("pe", lambda e: e.matmul(pX[:, 1, 0:192], ct_[:, kc, :], wuq[:, kc, 384:576], start=(kc == 0), stop=(kc == 1)),
                     reads=[bct_, bwu], writes=[bqr2], signal=(kc == 1))
            c.op("pe", lambda e: e.matmul(pX[:, 2, 0:384], ct_[:, 2, :], wukv[:, 0:384], start=True, stop=True),
                 reads=[bct_, bwk], writes=[bkv])
            c.op("pe", lambda e: e.matmul(pX[:, 3, 0:384], ct_[:, 2, :], wukv[:, 384:768], start=True, stop=True),
                 reads=[bct_, bwk], writes=[bvv])

        def U9(tt):
            cos, sin = ropeA[:, tt, 0, :], ropeA[:, tt, 1, :]
            qa_, bqa_ = qab[tt % 3], bqab[tt % 3]
            ka_, bka_ = kab[tt % 5], bkab[tt % 5]
            c.op("act", lambda e: e.activation(qa_[:, :, 0:64], v6(pX[:, 0, 0:384], 64), AF.Copy), reads=[bqa], writes=[bqa_])
            cosb = cos.unsqueeze(1).to_broadcast([128, 6, 16])
            sinb = sin.unsqueeze(1).to_broadcast([128, 6, 16])
            qx1 = v6(pX[:, 1, 0:96], 16)
            qx2 = v6(pX[:, 1, 96:192], 16)
            for q_, (xa, tb) in enumerate(((qx1, cosb), (qx2, sinb), (qx1, sinb), (qx2, cosb))):
                c.op("dve", lambda e: e.tensor_tensor(qr[tt % 2][:, q_], xa, tb, ALU.mult), reads=[bqr2, brope], writes=[bqr[tt % 2]])
            c.op("act", lambda e: e.activation(ka_[:, :, 0:64], v6(pX[:, 2, 0:384], 64), AF.Copy), reads=[bkv], writes=[bka_])
            c.op("dve", lambda e: e.tensor_copy(vab[tt % 2], v6(pX[:, 3, 0:384], 64)), reads=[bvv], writes=[bvab[tt % 2]])
            c.dma("sp", va_v[:, tt], vab[tt % 2], bvab[tt % 2], reads=[bvab[tt % 2]], writes=[g.b_scr])

        def U10(tt):
            qa_, bqa_ = qab[tt % 3], bqab[tt % 3]
            ka_, bka_ = kab[tt % 5], bkab[tt % 5]
            q_ = qr[tt % 2]
            c.op("pool", lambda e: e.tensor_tensor(qa_[:, :, 64:80], q_[:, 0], q_[:, 1], ALU.subtract), reads=[bqr[tt % 2]], writes=[bqa_])
            c.op("pool", lambda e: e.tensor_tensor(qa_[:, :, 80:96], q_[:, 2], q_[:, 3], ALU.add), reads=[bqr[tt % 2]], writes=[bqa_])
            krb = lambda j_: kr[tt % 8][:, j_, :].unsqueeze(1).to_broadcast([128, 6, 16])
            c.op("pool", lambda e: e.tensor_tensor(ka_[:, :, 64:80], krb(0), krb(1), ALU.subtract), reads=[bkr[tt % 8]], writes=[bka_])
            c.op("pool", lambda e: e.tensor_tensor(ka_[:, :, 80:96], krb(2), krb(3), ALU.add), reads=[bkr[tt % 8]], writes=[bka_])

        def U11(tt):
            for h in range(6):
                c.op("pe", lambda e: e.transpose(pTq[0:96, h, :], qab[tt % 3][:, h, :], g.ident),
                     reads=[bqab[tt % 3], g.b_const], writes=[bpTq], signal=(h == 5))

        def U12(tt):
            s, t = tt // 4, tt % 4
            cp(c, "dve", qaTs[s % 2][0:96, :, t * 128:(t + 1) * 128], pTq[0:96, 0:6, :], [bpTq], [bqaT[s % 2]])

        def U13(tt):
            for h in range(6):
                c.op("pe", lambda e: e.transpose(pTk[0:96, h, :], kab[tt % 5][:, h, :], g.ident),
                     reads=[bkab[tt % 5], g.b_const], writes=[bpTk], signal=(h == 5))

        def U14(tt):
            s, t = tt // 4, tt % 4
            si = s % 2
            cp(c, "act", kaTs[si][0:96, :, t * 128:(t + 1) * 128], pTk[0:96, 0:6, :], [bpTk], [bkaT[si]])
            if t == 3:
                c.dma("sp", qaT_v[:, :, s * TS:(s + 1) * TS], qaTs[si][0:96], bqaT[si], reads=[bqaT[si]], writes=[g.b_scr])
                c.dma("sp", kaT_v[:, :, s * TS:(s + 1) * TS], kaTs[si][0:96], bkaT[si], reads=[bkaT[si]], writes=[g.b_scr])

        skew([U1, U2, U3, U4, U5, U6, U7, U8, U9, U10, U11, U12, U13, U14], list(range(DBG_NT[0])), rev=True)
        c.barrier()


def a2_weights(g, es, l):
    c = g.c
    W0 = 416
    aw = G()
    aw.win = sb(es, g, "a2_win", [128, 8, 1664], BF16)
    aw.ggain = sb(es, g, "a2_gg", [128, 384], F32)
    aw.ropeG = sb(es, g, "a2_rope", [128, NT, 2, 2, 16], F32)
    aw.bw, aw.bgg, aw.brope = Buf("a2win"), Buf("a2gg"), Buf("a2rope")

    def load():
        wv = g.w_in[l].rearrange("(kc p) n -> p kc n", p=128)
        for kc in range(8):
            c.dma("pool", aw.win[:, kc, 0:832], wv[:, kc, W0:W0 + 832], aw.bw, writes=[aw.bw])
            c.dma("pool", aw.win[:, kc, 832:1664], wv[:, kc, W0 + 832:W0 + 1664], aw.bw, writes=[aw.bw])
        c.dma("sp", aw.ggain, g.d_ggain[l].partition_broadcast(128), aw.bgg, writes=[aw.bgg])
        c.dma("sp", aw.ropeG, g.d_ropeG, aw.brope, writes=[aw.brope])

    aw.load = load
    aw.bufs = [aw.bw, aw.bgg, aw.brope]
    return aw


def phase_a2(g, l, aw):
    c = g.c
    with ExitStack() as es:
        W0 = 416
        win, ggain, ropeG = aw.win, aw.ggain, aw.ropeG
        xts = [sb(es, g, "a2_xt%d" % i, [128, 8, TS], BF16) for i in range(2)]
        fT = [sb(es, g, "a2_fT%d" % i, [128, 6, TS], BF16) for i in range(2)]
        gTs = [sb(es, g, "a2_gT%d" % i, [128, 4, TS], BF16) for i in range(2)]

        def slots(name, shape, dt, n):
            return [sb(es, g, "a2_%s%d" % (name, i), shape, dt) for i in range(n)], [Buf("%s%d" % (name, i)) for i in range(n)]

        dvb, bdvb = slots("dvb", [128, 6, 64], BF16, 2)
        gvb, bgvb = slots("gvb", [128, 2, 64], BF16, 2)
        gsb, bgsb = slots("gsb", [128, 384], F32, 5)
        sq, bsq = slots("sq", [128, 384], F32, 3)
        ms, bms = slots("ms", [128, 6], F32, 2)
        ms2, bms2 = slots("msb", [128, 6], F32, 3)
        rstd, brstd = slots("rstd", [128, 6], F32, 3)
        gnt, bgn = slots("gn", [128, 384], F32, 4)
        tmp, btmp = slots("tmp", [128, 4, 192], F32, 3)
        gb, bgb = slots("gb", [128, 8, 64], BF16, 3)
        pF = [ps(es, g, "a2_pF%d" % i, [128, 512]) for i in range(2)]
        pDV = [ps(es, g, "a2_pDV%d" % i, [128, 512]) for i in range(2)]
        pG = [ps(es, g, "a2_pG%d" % i, [128, 512]) for i in range(2)]
        pT = [ps(es, g, "a2_pT%d" % i, [128, 8, 128], BF16) for i in range(2)]
        B = lambda n: Buf(n)
        bw, bgg, brope = aw.bw, aw.bgg, aw.brope
        (bxt, bfT, bgT, bpF, bpDV, bpG, bpT) = ([B("z%d" % i) for i in range(2)] for _ in range(7))
        xT_v = g.xT.rearrange("(kc p) n -> p kc n", p=128)
        dqT_v = g.dqT.rearrange("j p n -> p j n")
        dkT_v = g.dkT.rearrange("j p n -> p j n")
        gqT_v = g.gqT.rearrange("j p n -> p j n")
        gkT_v = g.gkT.rearrange("j p n -> p j n")
        dv_v = g.dv.rearrange("h p t d -> p t h d")
        gv_v = g.gv.rearrange("h p t d -> p t h d")
        nf = [0]
        hd = lambda ap: ap.rearrange("p (h d) -> p h d", d=64)

        def T1(tt):
            s, t, i = tt // 4, tt % 4, tt % 2
            si = s % 2
            xs = xts[si]
            if tt == 0:
                c.dma("sp", xs, xT_v[:, :, 0:TS], bxt[0], writes=[bxt[0]])
            if t == 1 and s + 1 < NS:
                c.dma("sp", xts[1 - si], xT_v[:, :, (s + 1) * TS:(s + 2) * TS], bxt[1 - si], writes=[bxt[1 - si]])
            for gi in ((0, 1), (2, 3), (4,), (5,))[t]:
                cb = (C_DQ - W0) + gi * 128
                fi = nf[0] % 2
                nf[0] += 1
                for kc in range(8):
                    c.op("pe", lambda e: e.matmul(pF[fi], win[:, kc, cb:cb + 128], xs[:, kc, :], start=(kc == 0), stop=(kc == 7)),
                         reads=[bxt[si], bw], writes=[bpF[fi]], signal=(kc == 7))
                cp(c, "act" if gi % 2 == 0 else "dve", fT[si][:, gi, :], pF[fi], [bpF[fi]], [bfT[si]])
            if t == 3:
                c.dma("sp", dqT_v[:, :, s * TS:(s + 1) * TS], fT[si][:, 0:3, :], bfT[si], reads=[bfT[si]], writes=[g.b_scr])
                c.dma("sp", dkT_v[:, :, s * TS:(s + 1) * TS], fT[si][:, 3:6, :], bfT[si], reads=[bfT[si]], writes=[g.b_scr])
            lhs = lambda kc: xs[:, kc, t * 128:(t + 1) * 128]
            for kc in range(8):
                c.op("pe", lambda e: e.matmul(pDV[i][:, 0:384], lhs(kc), win[:, kc, C_DV - W0:C_DV - W0 + 384],
                                              start=(kc == 0), stop=(kc == 7)),
                     reads=[bxt[si], bw], writes=[bpDV[i]], signal=(kc == 7))
            for kc in range(8):
                c.op("pe", lambda e: e.matmul(pG[i], lhs(kc), win[:, kc, C_GQ - W0:C_GQ - W0 + 512],
                                              start=(kc == 0), stop=(kc == 7)),
                     reads=[bxt[si], bw], writes=[bpG[i]], signal=(kc == 7))

        def T2(tt):
            i = tt % 2
            cp(c, "act", gsb[tt % 5], pG[i][:, 0:384], [bpG[i]], [bgsb[tt % 5]])
            c.op("act", lambda e: e.activation(sq[tt % 3], pG[i][:, 0:384], AF.Square, scale=0.125), reads=[bpG[i]], writes=[bsq[tt % 3]])
            cp(c, "act", gvb[i].rearrange("p h d -> p (h d)"), pG[i][:, 384:512], [bpG[i]], [bgvb[i]])
            c.dma("sp", gv_v[:, tt], gvb[i], bgvb[i], reads=[bgvb[i]], writes=[g.b_scr])
            cp(c, "act", dvb[i].rearrange("p h d -> p (h d)"), pDV[i][:, 0:384], [bpDV[i]], [bdvb[i]])
            c.dma("sp", dv_v[:, tt], dvb[i], bdvb[i], reads=[bdvb[i]], writes=[g.b_scr])

        def T3(tt):
            c.op("dve", lambda e: e.tensor_reduce(ms[tt % 2], hd(sq[tt % 3]), AX.X, ALU.add), reads=[bsq[tt % 3]], writes=[bms[tt % 2]])
            c.op("dve", lambda e: e.tensor_scalar(ms2[tt % 3], ms[tt % 2], 1e-6, None, ALU.add), reads=[bms[tt % 2]], writes=[bms2[tt % 3]])

        def T4(tt):
            c.op("pool", lambda e: e.tensor_tensor(rstd[tt % 3], ms2[tt % 3], g.mhalf[:, 0:6], ALU.pow),
                 reads=[bms2[tt % 3], g.b_mh], writes=[brstd[tt % 3]])

        def T5(tt):
            c.op("dve", lambda e: e.tensor_tensor(hd(gnt[tt % 4]), hd(gsb[tt % 5]),
                                                  rstd[tt % 3].unsqueeze(2).to_broadcast([128, 6, 64]), ALU.mult),
                 reads=[bgsb[tt % 5], brstd[tt % 3]], writes=[bgn[tt % 4]])

        def T6(tt):
            c.op("dve", lambda e: e.tensor_tensor(gnt[tt % 4], gnt[tt % 4], ggain, ALU.mult), reads=[bgn[tt % 4], bgg], writes=[bgn[tt % 4]])

        def T7(tt):
            v5 = gnt[tt % 4].rearrange("p (h r x d) -> p h r x d", h=6, r=2, x=2)
            x1 = v5[:, :, :, 0, :]
            x2 = v5[:, :, :, 1, :]
            cosb = ropeG[:, tt, 0].unsqueeze(1).to_broadcast([128, 6, 2, 16])
            sinb = ropeG[:, tt, 1].unsqueeze(1).to_broadcast([128, 6, 2, 16])
            tv = tmp[tt % 3].rearrange("p q (h r d) -> p q h r d", h=6, r=2)
            for q_, (xa, tb) in enumerate(((x1, cosb), (x2, sinb), (x1, sinb), (x2, cosb))):
                c.op("dve" if q_ < 2 else "pool", lambda e: e.tensor_tensor(tv[:, q_], xa, tb, ALU.mult),
                     reads=[bgn[tt % 4], brope], writes=[btmp[tt % 3]])

        def T8(tt):
            tv = tmp[tt % 3].rearrange("p q (h r d) -> p q h r d", h=6, r=2)
            gbt, bg_ = gb[tt % 3], bgb[tt % 3]
            gq5 = gbt.rearrange("p h (r x d) -> p h r x d", r=2, x=2)
            gk6 = gbt.rearrange("p (a b) (r x d) -> p a b r x d", b=2, r=2, x=2)
            c.op("pool", lambda e: e.tensor_tensor(gq5[:, 0:4, :, 0, :], tv[:, 0, 0:4], tv[:, 1, 0:4], ALU.subtract),
                 reads=[btmp[tt % 3]], writes=[bg_])
            c.op("pool", lambda e: e.tensor_tensor(gq5[:, 0:4, :, 1, :], tv[:, 2, 0:4], tv[:, 3, 0:4], ALU.add),
                 reads=[btmp[tt % 3]], writes=[bg_])
            for b_ in range(2):
                c.op("dve", lambda e: e.tensor_tensor(gk6[:, 2:4, b_, :, 0, :], tv[:, 0, 4:6], tv[:, 1, 4:6], ALU.subtract),
                     reads=[btmp[tt % 3]], writes=[bg_])
                c.op("dve", lambda e: e.tensor_tensor(gk6[:, 2:4, b_, :, 1, :], tv[:, 2, 4:6], tv[:, 3, 4:6], ALU.add),
                     reads=[btmp[tt % 3]], writes=[bg_])

        def T9(tt):
            i = tt % 2
            for b_ in range(4):
                c.op("pe", lambda e: e.transpose(pT[i][:, b_, :], gb[tt % 3][:, 2 * b_:2 * b_ + 2, :].rearrange("p h d -> p (h d)"), g.ident),
                     reads=[bgb[tt % 3], g.b_const], writes=[bpT[i]], signal=(b_ == 3))

        def T10(tt):
            s, t, i = tt // 4, tt % 4, tt % 2
            si = s % 2
            cp(c, "act", gTs[si][:, :, t * 128:(t + 1) * 128], pT[i][:, 0:4, :], [bpT[i]], [bgT[si]])
            if t == 3:
                c.dma("sp", gqT_v[:, :, s * TS:(s + 1) * TS], gTs[si][:, 0:2, :], bgT[si], reads=[bgT[si]], writes=[g.b_scr])
                c.dma("sp", gkT_v[:, :, s * TS:(s + 1) * TS], gTs[si][:, 2:4, :], bgT[si], reads=[bgT[si]], writes=[g.b_scr])

        skew([T1, T2, T3, T4, T5, T6, T7, T8, T9, T10], list(range(DBG_NT[0])))
        c.barrier()


def phase_attn(g, name, groups, scale, band, side=None):
    c = g.c
    LAG = 2
    with ExitStack() as es:
        qt = [sb(es, g, name + "_q%d" % i, [128, S], BF16) for i in range(4)]
        kt = [sb(es, g, name + "_k%d" % i, [128, S], BF16) for i in range(2)]
        vt = [sb(es, g, name + "_v%d" % i, [128, NT, 128], BF16) for i in range(4)]
        ot = [sb(es, g, name + "_o%d" % i, [64, S], BF16) for i in range(2)]
        pt = [sb(es, g, name + "_p%d" % i, [128, 512], BF16) for i in range(4)]
        rc = [sb(es, g, name + "_r%d" % i, [128, 512], F32) for i in range(2)]
        pS = [ps(es, g, name + "_pS%d" % i, [128, 512]) for i in range(4)]
        pO = [ps(es, g, name + "_pO%d" % i, [128, 512]) for i in range(2)]
        B = lambda n: Buf(n)
        bq, bk = [B("q%d" % i) for i in range(4)], [B("k0"), B("k1")]
        bv = [B("v%d" % i) for i in range(4)]
        bot, brc, bpO = ([B("o%d" % i) for i in range(2)] for _ in range(3))
        bpt, bpS = ([B("p%d" % i) for i in range(4)] for _ in range(2))
        btall = B("tall")
        if band:
            tall = sb(es, g, name + "_tall", [128, 6, TW], BF16)
            for h in range(6):
                src = bass.AP(g.rtab.tensor, h * LR, [[1, 128], [1, TW]])
                c.dma("sp", tall[:, h, :], src, btall, reads=[g.b_scr], writes=[btall])
        for i in range(4):
            c.op("pool", lambda e: e.memset(vt[i][:, :, 64:128], 1.0), writes=[bv[i]])
        padded = groups[0]["heads"][0]["K"] == 64
        if padded:
            for i in range(4):
                z0 = 64 * (1 - (i % 2))
                c.op("pool", lambda e: e.memset(qt[i][z0:z0 + 64, :], 0.0), writes=[bq[i]])

        def load_group(gi):
            gr = groups[gi]
            sl = gi % 2
            kp = gr["kp"]
            c.dma("sp", kt[sl][0:kp, :], gr["k"], bk[sl], reads=[g.b_scr], writes=[bk[sl]])
            for hi, hd in enumerate(gr["heads"]):
                vs = sl * 2 + hi
                p0_, K_ = hd["p0"], hd["K"]
                c.dma("sp", qt[vs][p0_:p0_ + K_, :], gr["q"][p0_:p0_ + K_, :], bq[vs], reads=[g.b_scr], writes=[bq[vs]])
                c.dma("sp", vt[vs][:, :, 0:64], hd["v"], bv[vs], reads=[g.b_scr], writes=[bv[vs]])

        flat = []
        hcount = 0
        for gi, gr in enumerate(groups):
            for hi, hd in enumerate(gr["heads"]):
                for cq in range(NS):
                    if band:
                        kbs = [kb for kb in range(4 * cq - 8, 4 * cq + 12) if 0 <= kb < NT]
                    else:
                        kbs = list(range(NT))
                    for ii, kb in enumerate(kbs):
                        flat.append((gi, hi, hcount, cq, ii, kb, len(kbs)))
                hcount += 1
        sidegen = side(es) if side is not None else None
        load_group(0)
        gstart = {}
        for idx, stp in enumerate(flat):
            gstart.setdefault(stp[0], idx)
        nchunk = 0
        for idx in range(len(flat) + LAG):
            if sidegen is not None and idx % 96 == 48:
                next(sidegen, None)
            if idx < len(flat):
                gi, hi, hc, cq, ii, kb, nkb = flat[idx]
                if idx - gstart[gi] == LAG + 1 and gi + 1 < len(groups):
                    load_group(gi + 1)
                gr = groups[gi]
                hd = gr["heads"][hi]
                sl = gi % 2
                p0, K = hd["p0"], hd["K"]
                sk = idx % 4
                kr_ = 128 if padded else K
                qs_ = sl * 2 + hi
                c.op("pe", lambda e: e.matmul(pS[sk], kt[sl][0:kr_, kb * 128:(kb + 1) * 128],
                                              qt[qs_][0:kr_, cq * TS:(cq + 1) * TS], start=True, stop=(not band)),
                     reads=[bk[sl], bq[qs_]], writes=[bpS[sk]], signal=(not band))
                if band:
                    off = 128 * (11 - (kb - 4 * cq))
                    c.op("pe", lambda e: e.matmul(pS[sk], g.antiI, tall[:, hd["tall"], off:off + TS], start=False, stop=True),
                         reads=[btall, g.b_const], writes=[bpS[sk]])
                c.op("act", lambda e: e.activation(pt[sk], pS[sk], AF.Exp, scale=scale), reads=[bpS[sk]], writes=[bpt[sk]])
            j = idx - LAG
            if j >= 0:
                gi, hi, hc, cq, ii, kb, nkb = flat[j]
                hd = groups[gi]["heads"][hi]
                sk = j % 4
                vs = (gi % 2) * 2 + hi
                if ii == 0:
                    osl = nchunk % 2
                    nchunk += 1
                c.op("pe", lambda e: e.matmul(pO[osl], vt[vs][:, kb, :], pt[sk], start=(ii == 0), stop=(ii == nkb - 1)),
                     reads=[bv[vs], bpt[sk]], writes=[bpO[osl]], signal=(ii == nkb - 1))
                if ii == nkb - 1:
                    hs = hc % 2
                    c.op("dve", lambda e: e.reciprocal(rc[osl][64:128, :], pO[osl][64:128, :]), reads=[bpO[osl]], writes=[brc[osl]])
                    c.op("dve", lambda e: e.tensor_tensor(ot[hs][:, cq * TS:(cq + 1) * TS], pO[osl][0:64, :], rc[osl][64:128, :], ALU.mult),
                         reads=[bpO[osl], brc[osl]], writes=[bot[hs]])
                    if cq == NS - 1:
                        r0 = hd["row0"]
                        c.dma("pool", g.mixT[r0:r0 + 64, :], ot[hs], bot[hs], reads=[bot[hs]], writes=[g.b_scr])
        if sidegen is not None:
            for _ in sidegen:
                pass
        c.barrier()


def layer_norm(c, y, by, out, bout, lnp, blnp, gi, sc, bsc):
    st, mv, rs = sc
    for h in range(2):
        c.op("dve", lambda e: e.bn_stats(st[:, h, :], y[:, h * 512:(h + 1) * 512]), reads=[by], writes=[bsc])
    c.op("dve", lambda e: e.bn_aggr(mv, st.rearrange("p a b -> p (a b)")), reads=[bsc], writes=[bsc])
    c.op("dve", lambda e: e.tensor_scalar(rs[:, 0:1], mv[:, 1:2], 1e-5, None, ALU.add), reads=[bsc], writes=[bsc])
    c.op("act", lambda e: e.activation(rs[:, 1:2], rs[:, 0:1], AF.Sqrt), reads=[bsc], writes=[bsc])
    c.op("dve", lambda e: e.reciprocal(rs[:, 0:1], rs[:, 1:2]), reads=[bsc], writes=[bsc])
    c.op("dve", lambda e: e.tensor_scalar(out, y, mv[:, 0:1], rs[:, 0:1], ALU.subtract, ALU.mult),
         reads=[by, bsc], writes=[bout])
    c.op("pool", lambda e: e.tensor_tensor(out, out, lnp[:, gi, :], ALU.mult), reads=[bout, blnp], writes=[bout])
    c.op("pool", lambda e: e.tensor_tensor(out, out, lnp[:, gi + 1, :], ALU.add), reads=[bout, blnp], writes=[bout])


def c_weights(g, es, l):
    c = g.c
    cw = G()
    cw.wout = sb(es, g, "c_wout", [128, 8, D], BF16)
    cw.wd = sb(es, g, "c_wd", [128, NJ, D], BF16)
    cw.lnp = sb(es, g, "c_lnp", [128, 4, D], F32)
    cw.bwout, cw.bwd, cw.blnp = Buf("cwout"), Buf("cwd"), Buf("clnp")
    def loader(_es=None):
        wov = g.w_out[l].rearrange("(kc p) n -> p kc n", p=128)
        for kc in range(8):
            c.dma("pool", cw.wout[:, kc, :], wov[:, kc, :], cw.bwout, writes=[cw.bwout])
            if kc % 2 == 1:
                yield
        wdv = g.wd[l].rearrange("(j p) n -> p j n", p=128)
        for j in range(NJ):
            c.dma("pool", cw.wd[:, j, :], wdv[:, j, :], cw.bwd, writes=[cw.bwd])
            if j % 2 == 1:
                yield
        for i_, v_ in enumerate((g.ln1g, g.ln1b, g.ln2g, g.ln2b)):
            c.dma("sp", cw.lnp[:, i_, :], v_[l].partition_broadcast(128), cw.blnp, writes=[cw.blnp])
        yield

    cw.loader = loader
    g.c.persist += [cw.bwout, cw.bwd, cw.blnp]
    return cw


def phase_c(g, l, last, cw):
    c = g.c
    xres = g.x if l == 0 else g.xres1
    dst = g.out if last else g.xres1
    wout, wd, lnp = cw.wout, cw.wd, cw.lnp
    bwout, bwd, blnp = cw.bwout, cw.bwd, cw.blnp
    with ExitStack() as es:
        NWGU = 3 if last else 2
        mx = sb(es, g, "c_mx", [128, 8, TS], BF16)
        xr = [sb(es, g, "c_xr%d" % i, [128, D], F32) for i in range(4)]
        x1s = [sb(es, g, "c_x1s%d" % i, [128, 4, D], F32) for i in range(2)]
        xb1 = [sb(es, g, "c_xb1%d" % i, [128, D], BF16) for i in range(4)]
        o = [sb(es, g, "c_o%d" % i, [128, D], F32) for i in range(2)]
        x1T = sb(es, g, "c_x1T", [128, 8, TS], BF16)
        aT = sb(es, g, "c_aT", [128, NJ, TS], BF16)
        wgu = [sb(es, g, "c_wgu%d" % i, [128, 2, 8, 128], BF16) for i in range(NWGU)]
        sg = [sb(es, g, "c_sg%d" % i, [128, TS], F32) for i in range(2)]
        sc1 = [(sb(es, g, "c_st%d" % i, [128, 2, 6], F32), sb(es, g, "c_mv%d" % i, [128, 2], F32),
                sb(es, g, "c_rs%d" % i, [128, 2], F32)) for i in range(4)]
        sc3 = [(sb(es, g, "c_st3%d" % i, [128, 2, 6], F32), sb(es, g, "c_mv3%d" % i, [128, 2], F32),
                sb(es, g, "c_rs3%d" % i, [128, 2], F32)) for i in range(2)]
        if not last:
            xb3 = [sb(es, g, "c_xb3%d" % i, [128, D], BF16) for i in range(4)]
            xTn = sb(es, g, "c_xTn", [128, 8, TS], BF16)
        PP = [ps(es, g, "c_PP%d" % i, [128, 2, 512]) for i in range(3)]
        pT = [ps(es, g, "c_pT%d" % i, [128, 8, 128], BF16) for i in range(2)]
        B = lambda n: Buf(n)
        bmx, bx1T, baT, bxTn = (B("c%d" % i) for i in range(4))
        bx1s = [[B("x1s%d_%d" % (a_, i)) for i in range(4)] for a_ in range(2)]
        bxr, bxb1, bxb3, bsc1 = ([B("d%d" % i) for i in range(4)] for _ in range(4))
        bo, bod, bsg, bpT, bsc3 = ([B("e%d" % i) for i in range(2)] for _ in range(5))
        bwgu = [B("wgu%d" % i) for i in range(NWGU)]
        bPP = [B("PP%d" % i) for i in range(3)]
        mixT_v = g.mixT.rearrange("(kc p) n -> p kc n", p=128)
        xT_v = g.xT.rearrange("(kc p) n -> p kc n", p=128)
        cnt = dict(pp=0, tp=0, nj=0)

        def nxt(k, n):
            v = cnt[k] % n
            cnt[k] += 1
            return v

        def transposes(src_b, bsrc, dstT, bdst, t):
            ti = nxt("tp", 2)
            for kc in range(8):
                c.op("pe", lambda e: e.transpose(pT[ti][:, kc, :], src_b[:, kc * 128:(kc + 1) * 128], g.ident),
                     reads=[bsrc, g.b_const], writes=[bpT[ti]], signal=(kc == 7))
            cp(c, "act", dstT[:, :, t * 128:(t + 1) * 128], pT[ti], [bpT[ti]], [bdst])

        def ln_stats(yv, byv, scr, bscr):
            st, mv, rs = scr
            for h in range(2):
                c.op("dve", lambda e: e.bn_stats(st[:, h, :], yv[:, h * 512:(h + 1) * 512]), reads=[byv], writes=[bscr])
            c.op("dve", lambda e: e.bn_aggr(mv, st.rearrange("p a b -> p (a b)")), reads=[bscr], writes=[bscr])
            c.op("dve", lambda e: e.tensor_scalar(rs[:, 1:2], mv[:, 1:2], 1e-5, None, ALU.add), reads=[bscr], writes=[bscr])
            c.op("pool", lambda e: e.tensor_tensor(rs[:, 0:1], rs[:, 1:2], g.mhalf[:, 0:1], ALU.pow), reads=[bscr, g.b_mh], writes=[bscr])

        def ln_apply(yv, byv, scr, bscr, out, bout, gi):
            st, mv, rs = scr
            c.op("dve", lambda e: e.tensor_scalar(out, yv, mv[:, 0:1], rs[:, 0:1], ALU.subtract, ALU.mult),
                 reads=[byv, bscr], writes=[bout])
            c.op("pool", lambda e: e.tensor_tensor(out, out, lnp[:, gi, :], ALU.mult), reads=[bout, blnp], writes=[bout])
            c.op("pool", lambda e: e.tensor_tensor(out, out, lnp[:, gi + 1, :], ALU.add), reads=[bout, blnp], writes=[bout])

        def load_mx(s1):
            c.dma("sp", mx, mixT_v[:, :, s1 * TS:(s1 + 1) * TS], bmx, reads=[g.b_scr], writes=[bmx])

        def P1(s1, t):
            tt = s1 * 4 + t
            pp = nxt("pp", 3)
            c.dma("sp", xr[t], xres[tt * 128:(tt + 1) * 128, :], bxr[t], reads=[g.b_scr], writes=[bxr[t]])
            for hf in range(2):
                for kc in range(8):
                    c.op("pe", lambda e: e.matmul(PP[pp][:, hf, :], mx[:, kc, t * 128:(t + 1) * 128], wout[:, kc, hf * 512:(hf + 1) * 512],
                                                  start=(kc == 0), stop=(kc == 7)),
                         reads=[bmx, bwout], writes=[bPP[pp]], signal=(kc == 7))
            c.op("dve", lambda e: e.scalar_tensor_tensor(xr[t], xr[t], DN_ALPHA, PP[pp].rearrange("p a n -> p (a n)"), ALU.mult, ALU.add),
                 reads=[bxr[t], bPP[pp]], writes=[bxr[t]])

        def P2(s1, t):
            ln_stats(xr[t], bxr[t], sc1[t], bsc1[t])

        def P3(s1, t):
            a_ = s1 % 2
            ln_apply(xr[t], bxr[t], sc1[t], bsc1[t], x1s[a_][:, t, :], bx1s[a_][t], 0)
            cp(c, "act", xb1[t], x1s[a_][:, t, :], [bx1s[a_][t]], [bxb1[t]])

        def P4(s1, t):
            transposes(xb1[t], bxb1[t], x1T, bx1T, t)

        def Q1(s, t):
            a_ = s % 2
            pp = nxt("pp", 3)
            for hf in range(2):
                for j in range(NJ):
                    c.op("pe", lambda e: e.matmul(PP[pp][:, hf, :], aT[:, j, t * 128:(t + 1) * 128], wd[:, j, hf * 512:(hf + 1) * 512],
                                                  start=(j == 0), stop=(j == NJ - 1)),
                         reads=[baT, bwd], writes=[bPP[pp]], signal=(j == NJ - 1))
            xv = x1s[a_][:, t, :]
            c.op("dve", lambda e: e.scalar_tensor_tensor(xv, xv, DN_ALPHA, PP[pp].rearrange("p a n -> p (a n)"), ALU.mult, ALU.add),
                 reads=[bx1s[a_][t], bPP[pp]], writes=[bx1s[a_][t]])

        def Q2(s, t):
            a_ = s % 2
            ln_stats(x1s[a_][:, t, :], bx1s[a_][t], sc3[t % 2], bsc3[t % 2])

        def Q3(s, t):
            a_ = s % 2
            tt = s * 4 + t
            i = t % 2
            ln_apply(x1s[a_][:, t, :], bx1s[a_][t], sc3[i], bsc3[i], o[i], bo[i], 2)
            c.dma("pool", dst[tt * 128:(tt + 1) * 128, :], o[i], bod[i], reads=[bo[i]], writes=[g.b_out])
            if not last:
                cp(c, "pool", xb3[t], o[i], [bo[i]], [bxb3[t]])

        def Q4(s, t):
            transposes(xb3[t], bxb3[t], xTn, bxTn, t)
            if t == 3:
                c.dma("act", xT_v[:, :, s * TS:(s + 1) * TS], xTn, bxTn, reads=[bxTn], writes=[g.b_scr])

        def stage2(s, hooks):
            for j in range(NJ):
                wi = nxt("nj", NWGU)
                pp = nxt("pp", 3)
                gi = j % 2
                c.dma("sp", wgu[wi][:, 0], g.wgu[l, 0, j], bwgu[wi], reads=[g.b_scr], writes=[bwgu[wi]])
                c.dma("sp", wgu[wi][:, 1], g.wgu[l, 1, j], bwgu[wi], reads=[g.b_scr], writes=[bwgu[wi]])
                for gu in range(2):
                    for kc in range(8):
                        c.op("pe", lambda e: e.matmul(PP[pp][:, gu, :], wgu[wi][:, gu, kc, :], x1T[:, kc, :],
                                                      start=(kc == 0), stop=(kc == 7)),
                             reads=[bwgu[wi], bx1T], writes=[bPP[pp]], signal=(kc == 7))
                c.op("act", lambda e: e.activation(sg[gi], PP[pp][:, 0, :], AF.Silu), reads=[bPP[pp]], writes=[bsg[gi]])
                c.op("dve", lambda e: e.tensor_tensor(aT[:, j, :], sg[gi], PP[pp][:, 1, :], ALU.mult),
                     reads=[bsg[gi], bPP[pp]], writes=[baT])
                for h_ in hooks.get(j, ()):
                    h_()

        T4 = [0, 1, 2, 3]
        load_mx(0)
        skew([lambda t: P1(0, t), lambda t: P2(0, t), lambda t: P3(0, t), lambda t: P4(0, t)], T4)
        for s in range(NS):
            hooks = {}
            if s > 0 and not last:
                for t in T4:
                    hooks.setdefault(2 + t, []).append(lambda t=t: Q4(s - 1, t))
            if s + 1 < NS:
                hooks.setdefault(12, []).append(lambda: load_mx(s + 1))
            stage2(s, hooks)
            if s + 1 < NS:
                skew([lambda t: P1(s + 1, t), lambda t: P2(s + 1, t), lambda t: P3(s + 1, t)], T4)
            nx = s + 1 < NS
            Q1(s, 0)
            Q1(s, 1)
            Q2(s, 0)
            if nx:
                P4(s + 1, 0)
                P4(s + 1, 1)
            Q1(s, 2)
            Q2(s, 1)
            Q3(s, 0)
            if nx:
                P4(s + 1, 2)
                P4(s + 1, 3)
            Q1(s, 3)
            Q2(s, 2)
            Q3(s, 1)
            Q2(s, 3)
            Q3(s, 2)
            Q3(s, 3)
        if not last:
            for t in T4:
                Q4(NS - 1, t)
        c.barrier()
        for b_ in (cw.bwout, cw.bwd, cw.blnp):
            g.c.persist.remove(b_)


def w_gen(g, es, layers):
    c = g.c
    st = [sb(es, g, "w_st%d" % i, [128, 8, 512], F32) for i in range(2)]
    wb = [sb(es, g, "w_wb%d" % i, [128, 8, 512], BF16) for i in range(2)]
    bst = [Buf("wst%d" % i) for i in range(2)]
    bwb = [Buf("wwb%d" % i) for i in range(2)]
    n = 0
    for l in layers:
        for gu, w in enumerate((g.wg, g.wu)):
            wv = w[l].rearrange("(kc p) n -> p kc n", p=128)
            for c0 in range(0, DFF, 512):
                cw_ = min(512, DFF - c0)
                i = n % 2
                c.dma("sp", st[i][:, :, 0:cw_], wv[:, :, c0:c0 + cw_], bst[i], writes=[bst[i]])
                cp(c, "pool", wb[i][:, :, 0:cw_], st[i][:, :, 0:cw_], [bst[i]], [bwb[i]])
                for jj in range(cw_ // 128):
                    j = c0 // 128 + jj
                    c.dma("sp", g.wgu[l, gu, j], wb[i][:, :, jj * 128:(jj + 1) * 128], bwb[i],
                          reads=[bwb[i]], writes=[g.b_scr])
                n += 1
                yield


IN_SPECS = [
    ("x", [S, D]), ("w_in", [DEPTH, D, IN_W]), ("w_uq", [DEPTH, 256, 576]), ("w_ukv", [DEPTH, 128, 768]),
    ("qn", [DEPTH, 256]), ("kvn", [DEPTH, 128]), ("d_ggain", [DEPTH, 384]), ("w_out", [DEPTH, D, D]),
    ("wg", [DEPTH, D, DFF]), ("wu", [DEPTH, D, DFF]), ("wd", [DEPTH, DFF, D]),
    ("ln1g", [DEPTH, D]), ("ln1b", [DEPTH, D]), ("ln2g", [DEPTH, D]), ("ln2b", [DEPTH, D]),
    ("d_ident", [128, 128]), ("d_antiI", [128, 128]), ("d_ropeA", [128, NT, 2, 16]), ("d_ropeG", [128, NT, 2, 2, 16]),
    ("d_biasg", [6, LR]), ("d_logm", [6, LR]),
]
SCR_SPECS = [
    ("xT", [D, S], BF16), ("qaT", [6, 96, S], BF16), ("kaT", [6, 96, S], BF16), ("va", [6, 128, NT, 64], BF16),
    ("dqT", [3, 128, S], BF16), ("dkT", [3, 128, S], BF16), ("dv", [6, 128, NT, 64], BF16),
    ("gqT", [2, 128, S], BF16), ("gkT", [2, 128, S], BF16), ("gv", [2, 128, NT, 64], BF16),
    ("mixT", [D, S], BF16), ("xres1", [S, D], F32), ("wgu", [DEPTH, 2, NJ, 128, 8, 128], BF16), ("rtab", [6, LR], BF16),
]


def attn_groups(g):
    mla = [dict(q=g.qaT[h], k=g.kaT[h], kp=96, heads=[dict(p0=0, K=96, v=g.va[h], row0=h * 64, tall=None)]) for h in range(6)]
    dil = [dict(q=g.dqT[j], k=g.dkT[j], kp=128,
                heads=[dict(p0=64 * e, K=64, v=g.dv[2 * j + e], row0=384 + (2 * j + e) * 64, tall=2 * j + e) for e in range(2)])
           for j in range(3)]
    gqa = [dict(q=g.gqT[j], k=g.gkT[j], kp=128,
                heads=[dict(p0=64 * e, K=64, v=g.gv[j], row0=768 + (2 * j + e) * 64, tall=None) for e in range(2)])
           for j in range(2)]
    return mla, dil, gqa


def build(phases=None, dbg=False):
    nc = bass.Bass("TRN2", target_bir_lowering=False)
    g = G()
    g.nc = nc
    for name, shape in IN_SPECS:
        setattr(g, name, nc.dram_tensor(name, shape, F32, kind="ExternalInput").ap())
    for name, shape, dt in SCR_SPECS:
        setattr(g, name, nc.dram_tensor(name, shape, dt, kind=("ExternalOutput" if dbg else "Internal")).ap())
    g.out = nc.dram_tensor("out", [S, D], F32, kind="ExternalOutput").ap()
    g.c = Ctx(nc)
    g.b_scr = Buf("scr")
    g.b_out = Buf("out")
    g.c.persist = [g.b_scr, g.b_out]
    run = (lambda p: True) if phases is None else (lambda p: p in phases)
    with ExitStack() as es:
        phase_consts(g, es)
        g.c.persist += [g.b_const, g.b_mh]
        if run("x0"):
            phase_x0(g)
        if run("b0"):
            phase_b0(g)
        mla, dil, gqa = attn_groups(g)
        for l in range(DEPTH):
            with ExitStack() as es_a:
                aw = a2_weights(g, es_a, l) if run("a2_%d" % l) else None
                if aw is not None:
                    g.c.persist += aw.bufs
                if run("a1_%d" % l):
                    phase_a1(g, l, after_loads=(aw.load if aw is not None else None))
                elif aw is not None:
                    aw.load()
                if aw is not None:
                    phase_a2(g, l, aw)
                    for b_ in aw.bufs:
                        g.c.persist.remove(b_)
            if run("mla_%d" % l):
                phase_attn(g, "ma", mla, 96.0 ** -0.5, False, side=(lambda es_, l=l: w_gen(g, es_, [l])))
            if run("dil_%d" % l):
                phase_attn(g, "md", dil, 0.125, True)
            with ExitStack() as es_c:
                cw = c_weights(g, es_c, l) if run("c_%d" % l) else None
                if run("gqa_%d" % l):
                    phase_attn(g, "mg", gqa, 0.125, False, side=(cw.loader if cw is not None else None))
                elif cw is not None:
                    for _ in cw.loader():
                        pass
                if run("c_%d" % l):
                    phase_c(g, l, l == DEPTH - 1, cw)
        g.c.barrier()
    return nc


def host_inputs(inputs):
    f = lambda a: np.ascontiguousarray(np.asarray(a, dtype=np.float32))
    ropeA, ropeG = _rope_tables()
    biasg, logm = _dil_tables(f(inputs["rel_bias"]))
    gq, gk = f(inputs["gqa_q_norm"]), f(inputs["gqa_k_norm"])
    ggain = np.concatenate([np.tile(gq, (1, 4)), np.tile(gk, (1, 2))], axis=1)
    common = {
        "w_in": f(inputs["w_in"]), "w_uq": f(inputs["mla_w_uq"]), "w_ukv": f(inputs["mla_w_ukv"]),
        "qn": f(inputs["mla_q_norm"]), "kvn": f(inputs["mla_kv_norm"]), "d_ggain": f(ggain),
        "w_out": f(inputs["w_out"]), "wg": f(inputs["ffn_w_gate"]), "wu": f(inputs["ffn_w_up"]), "wd": f(inputs["ffn_w_down"]),
        "ln1g": f(inputs["ln1_g"]), "ln1b": f(inputs["ln1_b"]), "ln2g": f(inputs["ln2_g"]), "ln2b": f(inputs["ln2_b"]),
        "d_ident": np.eye(128, dtype=np.float32), "d_antiI": np.ascontiguousarray(np.eye(128, dtype=np.float32)[::-1]),
        "d_ropeA": ropeA, "d_ropeG": ropeG, "d_biasg": biasg, "d_logm": logm,
    }
    x = f(inputs["x"])
    return [dict(common, x=np.ascontiguousarray(x[b])) for b in range(x.shape[0])]


_NC_CACHE = {}


def kernel(**inputs):
    in_maps = host_inputs(inputs)
    if "nc" not in _NC_CACHE:
        _NC_CACHE["nc"] = build()
    res = run_bass_kernel_spmd(_NC_CACHE["nc"], in_maps, core_ids=list(range(len(in_maps))))
    return np.stack([np.asarray(r["out"], dtype=np.float32) for r in res.results], axis=0)
```

```python
import math
import numpy as np
import ml_dtypes
import concourse.bass as bass
import concourse.mybir as mybir
from concourse.bass_utils import run_bass_kernel_spmd
from concourse.alu_op_type import AluOpType as ALU

AF = mybir.ActivationFunctionType
F32 = mybir.dt.float32
BF16 = mybir.dt.bfloat16
AX = mybir.AxisListType

S = 4096
D = 1024
NT = 32
NS = 8
TS = 512
DEPTH = 2
IN_W = 2080
DFF = 2816
NJ = 22
LR = 3072
TW = 2944
DN_ALPHA = (2.0 * DEPTH) ** 0.25
C_CQ, C_CKV, C_KR, C_DQ, C_DK, C_DV, C_GQ, C_GK, C_GV = 0, 256, 384, 416, 800, 1184, 1568, 1824, 1952


class Buf:
    __slots__ = ("name", "w", "r", "dkey", "excl")

    def __init__(self, name, excl=False):
        self.name = name
        self.w = None
        self.r = []
        self.dkey = None
        self.excl = excl


class Ctx:
    def __init__(self, nc):
        self.nc = nc
        self.engs = {"pe": nc.tensor, "act": nc.scalar, "dve": nc.vector, "pool": nc.gpsimd, "sp": nc.sync}
        self.sem = {}
        self.cnt = {}
        self.seen = {e: {} for e in self.engs}
        self.dma_keys = set()
        for e in ("pe", "act", "dve", "pool"):
            self._mksem(e)

    def _mksem(self, key):
        self.sem[key] = self.nc.alloc_semaphore("s_" + key)
        self.cnt[key] = 0

    def _wait(self, eng, need):
        e = self.engs[eng]
        seen = self.seen[eng]
        for k, v in need.items():
            if k in self.dma_keys:
                v = self.cnt[k]
            if seen.get(k, 0) >= v:
                continue
            e.wait_ge(self.sem[k], v)
            seen[k] = v

    def _deps(self, eng, reads, writes):
        need = {}
        for b in reads:
            if b.w is not None:
                k, v = b.w
                if need.get(k, 0) < v:
                    need[k] = v
            if b.excl:
                for (k, v) in b.r:
                    if k != eng and need.get(k, 0) < v:
                        need[k] = v
        same = eng != "pe"
        for b in writes:
            if b.w is not None:
                k, v = b.w
                if (k != eng or same) and need.get(k, 0) < v:
                    need[k] = v
            for (k, v) in b.r:
                if (k != eng or same) and need.get(k, 0) < v:
                    need[k] = v
        return need

    def _record(self, ev, reads, writes):
        for b in reads:
            b.r.append(ev)
            if len(b.r) > 16:
                m = {}
                for k, v in b.r:
                    if m.get(k, 0) < v:
                        m[k] = v
                b.r = list(m.items())
        for b in writes:
            b.w = ev
            b.r = []

    def op(self, eng, fn, reads=(), writes=(), signal=True):
        self._wait(eng, self._deps(eng, reads, writes))
        ins = fn(self.engs[eng])
        if signal:
            self.cnt[eng] += 1
            ins.then_inc(self.sem[eng], 1)
            ev = (eng, self.cnt[eng])
        else:
            ev = (eng, self.cnt[eng] + 1)
        self._record(ev, reads, writes)
        return ins

    def dma(self, q, out, in_, owner, reads=(), writes=(), **kw):
        kind = "S" if q == "pool" else "H"
        if owner.dkey is None:
            fk = [k for k in getattr(self, "free_keys", []) if k[1] == kind]
            if fk:
                owner.dkey = fk[-1]
                self.free_keys.remove(fk[-1])
            else:
                owner.dkey = "d%s%d" % (kind, len(self.dma_keys))
                self._mksem(owner.dkey)
                self.dma_keys.add(owner.dkey)
        assert owner.dkey[1] == kind, (owner.name, owner.dkey, q)
        k = owner.dkey
        self._wait(q, self._deps(q, reads, writes))
        ins = self.engs[q].dma_start(out=out, in_=in_, **kw)
        self.cnt[k] += 16
        ins.then_inc(self.sem[k], 16)
        self._record((k, self.cnt[k]), reads, writes)
        return ins

    def barrier(self):
        need = {k: v for k, v in self.cnt.items() if v > 0}
        for e in self.engs:
            self._wait(e, {k: v for k, v in need.items() if k != e})
        keep = set()
        for b in getattr(self, "persist", []):
            b.w = None
            b.r = []
            if b.dkey is not None:
                keep.add(b.dkey)
        self.free_keys = [k for k in sorted(self.dma_keys) if k not in keep]


class G:
    pass


_UID = [0]


def _uname(name):
    _UID[0] += 1
    return "%s_%d" % (name, _UID[0])


def sb(es, g, name, shape, dt):
    return es.enter_context(g.nc.sbuf_tensor(_uname(name), list(shape), dt)).ap()


def ps(es, g, name, shape, dt=F32):
    return es.enter_context(g.nc.psum_tensor(_uname(name), list(shape), dt)).ap()


def _rope_tables():
    f32 = np.float32
    t = np.arange(S, dtype=f32)

    def cs(pos, d):
        inv = (np.float32(10000.0) ** (-np.arange(0, d, 2, dtype=f32) / f32(d))).astype(f32)
        ang = (pos[:, None] * inv[None, :]).astype(f32)
        return np.cos(ang).astype(f32), np.sin(ang).astype(f32)

    ca, sa = cs(t, 32)
    ropeA = np.stack([ca, sa], axis=1)
    ropeA = ropeA.reshape(NT, 128, 2, 16).transpose(1, 0, 2, 3).copy()
    row = np.repeat(np.arange(S // 64), 64).astype(f32)
    col = np.tile(np.arange(64), S // 64).astype(f32)
    cr, sr = cs(row, 32)
    cc, sc = cs(col, 32)
    ropeG = np.stack([np.stack([cr, cc], axis=1), np.stack([sr, sc], axis=1)], axis=1)
    ropeG = ropeG.reshape(NT, 128, 2, 2, 16).transpose(1, 0, 2, 3, 4).copy()
    return ropeA, ropeG


def _t5_bucket(rel):
    nb = 16
    exact = 8
    ret = np.where(rel > 0, nb, 0)
    n = np.abs(rel)
    nf = np.maximum(n, 1).astype(np.float32)
    large = exact + (np.log(nf / np.float32(exact)) / np.float32(math.log(1024 / exact)) * np.float32(nb - exact)).astype(np.int32)
    large = np.minimum(large, nb - 1)
    return ret + np.where(n < exact, n, large)


def _dil_tables(rel_bias):
    d = 1535 - np.arange(LR)
    mult = ((np.abs(d) <= 64).astype(np.int32) + ((d % 4 == 0) & (np.abs(d) <= 256)).astype(np.int32)
            + ((d % 16 == 0) & (np.abs(d) <= 1024)).astype(np.int32))
    logm = np.where(mult > 0, np.log(np.maximum(mult, 1).astype(np.float64)), -30000.0).astype(np.float32)
    bucket = _t5_bucket(d)
    bias_g = np.ascontiguousarray(rel_bias[bucket, :].T).astype(np.float32)
    logm6 = np.ascontiguousarray(np.broadcast_to(logm[None, :], (6, LR))).astype(np.float32)
    return bias_g, logm6


from contextlib import ExitStack


CUT = [None]
DBG_NT = [NT]
NOSKEW = [False]


class _Cut(Exception):
    pass


def chk(n):
    if CUT[0] == n:
        raise _Cut()


def cp(c, eng, out, in_, reads, writes):
    if eng == "act":
        return c.op("act", lambda e: e.activation(out, in_, AF.Copy), reads=reads, writes=writes)
    return c.op(eng, lambda e: e.tensor_copy(out, in_), reads=reads, writes=writes)


def phase_consts(g, es):
    c = g.c
    g.ident = sb(es, g, "ident", [128, 128], BF16)
    g.antiI = sb(es, g, "antiI", [128, 128], BF16)
    g.b_const = Buf("const")
    c.dma("pool", g.ident, g.d_ident, g.b_const, writes=[g.b_const])
    c.dma("pool", g.antiI, g.d_antiI, g.b_const, writes=[g.b_const])
    g.mhalf = sb(es, g, "mhalf", [128, 8], F32)
    g.b_mh = Buf("mhalf")
    c.op("pool", lambda e: e.memset(g.mhalf, -0.5), writes=[g.b_mh])


def phase_x0(g):
    c = g.c
    with ExitStack() as es:
        idf = sb(es, g, "x0_idf", [128, 128], F32)
        xf = [sb(es, g, "x0_xf%d" % i, [128, D], F32) for i in range(3)]
        xts = [sb(es, g, "x0_xt%d" % i, [128, 8, TS], BF16) for i in range(2)]
        pT = [ps(es, g, "x0_pT%d" % i, [128, 8, 128], F32) for i in range(2)]
        bidf = Buf("idf")
        bxf = [Buf("xf%d" % i) for i in range(3)]
        bxt = [Buf("xt%d" % i) for i in range(2)]
        bpT = [Buf("pT%d" % i) for i in range(2)]
        c.dma("sp", idf, g.d_ident, bidf, writes=[bidf])
        xT_v = g.xT.rearrange("(kc p) n -> p kc n", p=128)
        for s in range(NS):
            for t in range(4):
                tt = s * 4 + t
                i = tt % 2
                f = tt % 3
                c.dma("sp", xf[f], g.x[tt * 128:(tt + 1) * 128, :], bxf[f], writes=[bxf[f]])
                for kc in range(8):
                    c.op("pe", lambda e: e.transpose(pT[i][:, kc, :], xf[f][:, kc * 128:(kc + 1) * 128], idf),
                         reads=[bxf[f], bidf], writes=[bpT[i]], signal=(kc == 7))
                cp(c, "dve" if tt % 2 == 0 else "act", xts[s % 2][:, :, t * 128:(t + 1) * 128], pT[i],
                   [bpT[i]], [bxt[s % 2]])
            c.dma("act", xT_v[:, :, s * TS:(s + 1) * TS], xts[s % 2], bxt[s % 2], reads=[bxt[s % 2]], writes=[g.b_scr])
        c.barrier()


def phase_w(g):
    c = g.c
    with ExitStack() as es:
        st = [sb(es, g, "w_st%d" % i, [128, 8, 512], F32) for i in range(2)]
        wb = [sb(es, g, "w_wb%d" % i, [128, 8, 512], BF16) for i in range(2)]
        bst = [Buf("wst%d" % i) for i in range(2)]
        bwb = [Buf("wwb%d" % i) for i in range(2)]
        n = 0
        for l in range(DEPTH):
            for gu, w in enumerate((g.wg, g.wu)):
                wv = w[l].rearrange("(kc p) n -> p kc n", p=128)
                for c0 in range(0, DFF, 512):
                    cw = min(512, DFF - c0)
                    i = n % 2
                    c.dma("sp", st[i][:, :, 0:cw], wv[:, :, c0:c0 + cw], bst[i], writes=[bst[i]])
                    eng = ("dve", "pool")[n % 2]
                    cp(c, eng, wb[i][:, :, 0:cw], st[i][:, :, 0:cw], [bst[i]], [bwb[i]])
                    for jj in range(cw // 128):
                        j = c0 // 128 + jj
                        c.dma("sp", g.wgu[l, gu, j], wb[i][:, :, jj * 128:(jj + 1) * 128], bwb[i],
                              reads=[bwb[i]], writes=[g.b_scr])
                    n += 1
        c.barrier()


def phase_b0(g):
    c = g.c
    with ExitStack() as es:
        a = sb(es, g, "b0_a", [6, LR], F32)
        b = sb(es, g, "b0_b", [6, LR], F32)
        o = sb(es, g, "b0_o", [6, LR], BF16)
        ba, bb, bo = Buf("b0a"), Buf("b0b"), Buf("b0o")
        c.dma("sp", a, g.d_biasg, ba, writes=[ba])
        c.dma("sp", b, g.d_logm, bb, writes=[bb])
        c.op("dve", lambda e: e.tensor_tensor(a, a, b, ALU.add), reads=[ba, bb], writes=[ba])
        c.op("dve", lambda e: e.tensor_scalar(o, a, 8.0, None, ALU.mult), reads=[ba], writes=[bo])
        c.dma("sp", g.rtab, o, bo, reads=[bo], writes=[g.b_scr])
        c.barrier()


def skew(stages, tiles, rev=False):
    n = len(tiles)
    if NOSKEW[0]:
        for t_ in tiles:
            for st in stages:
                st(t_)
        return
    order = list(enumerate(stages))
    if rev:
        order = order[::-1]
    for k in range(n + len(stages) - 1):
        for si, st in order:
            idx = k - si
            if 0 <= idx < n:
                st(tiles[idx])


def phase_a1(g, l, after_loads=None):
    c = g.c
    with ExitStack() as es:
        win = sb(es, g, "a1_win", [128, 8, 416], BF16)
        wuq_s = sb(es, g, "a1_wuqs", [128, 2, 576], F32)
        wukv_s = sb(es, g, "a1_wukvs", [128, 768], F32)
        wuq = sb(es, g, "a1_wuq", [128, 2, 576], BF16)
        wukv = sb(es, g, "a1_wukv", [128, 768], BF16)
        gn = sb(es, g, "a1_gn", [128, 3], F32)
        ropeA = sb(es, g, "a1_rope", [128, NT, 2, 16], F32)
        xts = [sb(es, g, "a1_xt%d" % i, [128, 8, TS], BF16) for i in range(2)]
        junk = sb(es, g, "a1_junk", [128, 256], F32)
        qaTs = [sb(es, g, "a1_qaT%d" % i, [128, 6, TS], BF16) for i in range(2)]
        kaTs = [sb(es, g, "a1_kaT%d" % i, [128, 6, TS], BF16) for i in range(2)]

        def slots(name, shape, dt, n):
            return [sb(es, g, "a1_%s%d" % (name, i), shape, dt) for i in range(n)], [Buf("%s%d" % (name, i)) for i in range(n)]

        hsb, bhsb = slots("hsb", [128, 416], F32, 4)
        st, bst = slots("st", [128, 2], F32, 2)
        st2, bst2 = slots("stb", [128, 2], F32, 2)
        rstd, brstd = slots("rstd", [128, 2], F32, 2)
        kr, bkr = slots("kr", [128, 4, 16], F32, 8)
        cn, bcn = slots("cn", [128, 384], BF16, 2)
        cT, bcT = slots("cT", [128, 3, 128], BF16, 2)
        qr, bqr = slots("qr", [128, 4, 6, 16], F32, 2)
        qab, bqab = slots("qab", [128, 6, 96], BF16, 3)
        kab, bkab = slots("kab", [128, 6, 96], BF16, 5)
        vab, bvab = slots("vab", [128, 6, 64], BF16, 2)
        pH = ps(es, g, "a1_pH", [128, 512])
        pX = ps(es, g, "a1_pX", [128, 4, 512])
        pTc = ps(es, g, "a1_pTc", [128, 8, 128], BF16)
        pTq = ps(es, g, "a1_pTq", [128, 8, 128], BF16)
        pTk = ps(es, g, "a1_pTk", [128, 8, 128], BF16)
        B = lambda n: Buf(n)
        bw, bwu, bwus, bwk, bwks, bgn, brope = B("win"), B("wuq"), B("wuqs"), B("wukv"), B("wukvs"), B("gn"), B("rope")
        bxt = [B("xt0"), B("xt1")]
        bjunk, bpH, bpTc, bpTq, bpTk = B("junk"), B("pH"), B("pTc"), B("pTq"), B("pTk")
        bqaT, bkaT = [B("qaT0"), B("qaT1")], [B("kaT0"), B("kaT1")]
        bqa, bkv, bqr2, bvv = B("pqa"), B("pkv"), B("pqr"), B("pvv")
        wv = g.w_in[l].rearrange("(kc p) n -> p kc n", p=128)
        for kc in range(8):
            c.dma("pool", win[:, kc, :], wv[:, kc, 0:416], bw, writes=[bw])
        wq = g.w_uq[l].rearrange("(kc p) (h d) -> p kc h d", p=128, d=96)
        for kc in range(2):
            for (c0, d0, dw) in ((0, 0, 64), (384, 64, 16), (480, 80, 16)):
                c.dma("sp", wuq_s[:, kc, c0:c0 + 6 * dw].rearrange("p (h d) -> p h d", d=dw), wq[:, kc, :, d0:d0 + dw],
                      bwus, writes=[bwus])
        wk = g.w_ukv[l].rearrange("p (h d) -> p h d", d=128)
        for a_ in range(2):
            c.dma("sp", wukv_s[:, a_ * 384:(a_ + 1) * 384].rearrange("p (h d) -> p h d", d=64), wk[:, :, a_ * 64:(a_ + 1) * 64],
                  bwks, writes=[bwks])
        for kc in range(2):
            c.dma("sp", gn[:, kc:kc + 1], g.qn[l][kc * 128:(kc + 1) * 128].rearrange("(p o) -> p o", o=1), bgn, writes=[bgn])
        c.dma("sp", gn[:, 2:3], g.kvn[l].rearrange("(p o) -> p o", o=1), bgn, writes=[bgn])
        c.dma("sp", ropeA, g.d_ropeA, brope, writes=[brope])
        for kc in range(2):
            c.op("dve", lambda e: e.tensor_scalar(wuq[:, kc, :], wuq_s[:, kc, :], gn[:, kc:kc + 1], None, ALU.mult),
                 reads=[bwus, bgn], writes=[bwu])
        c.op("dve", lambda e: e.tensor_scalar(wukv, wukv_s, gn[:, 2:3], None, ALU.mult), reads=[bwks, bgn], writes=[bwk])
        xT_v = g.xT.rearrange("(kc p) n -> p kc n", p=128)
        qaT_v = g.qaT.rearrange("h p n -> p h n")
        kaT_v = g.kaT.rearrange("h p n -> p h n")
        va_v = g.va.rearrange("h p t d -> p t h d")
        v6 = lambda ap, d: ap.rearrange("p (h d) -> p h d", d=d)

        def U1(tt):
            s, t = tt // 4, tt % 4
            si = s % 2
            xs = xts[si]
            if tt == 0:
                c.dma("sp", xs, xT_v[:, :, 0:TS], bxt[0], writes=[bxt[0]])
            if t == 1 and s + 1 < NS:
                c.dma("sp", xts[1 - si], xT_v[:, :, (s + 1) * TS:(s + 2) * TS], bxt[1 - si], writes=[bxt[1 - si]])
            if tt == min(6, DBG_NT[0] - 1) and after_loads is not None:
                after_loads()
            for kc in range(8):
                c.op("pe", lambda e: e.matmul(pH[:, 0:416], xs[:, kc, t * 128:(t + 1) * 128], win[:, kc, :],
                                              start=(kc == 0), stop=(kc == 7)),
                     reads=[bxt[si], bw], writes=[bpH], signal=(kc == 7))

        def U2(tt):
            h_, bh_ = hsb[tt % 4], bhsb[tt % 4]
            s_, bs_ = st[tt % 2], bst[tt % 2]
            cp(c, "act", h_, pH[:, 0:416], [bpH], [bh_])
            c.op("act", lambda e: e.activation(junk, pH[:, 0:256], AF.Square, scale=1.0 / 16.0, accum_out=s_[:, 0:1]),
                 reads=[bpH], writes=[bjunk, bs_])
            c.op("act", lambda e: e.activation(junk[:, 0:128], pH[:, 256:384], AF.Square,
                                               scale=1.0 / math.sqrt(128.0), accum_out=s_[:, 1:2]),
                 reads=[bpH], writes=[bjunk, bs_])

        def U3(tt):
            h_, bh_ = hsb[tt % 4], bhsb[tt % 4]
            c.op("dve", lambda e: e.tensor_scalar(st2[tt % 2], st[tt % 2], 1e-6, None, ALU.add), reads=[bst[tt % 2]], writes=[bst2[tt % 2]])
            cos, sin = ropeA[:, tt, 0, :], ropeA[:, tt, 1, :]
            x1, x2 = h_[:, 384:400], h_[:, 400:416]
            for q_, (xa, tb) in enumerate(((x1, cos), (x2, sin), (x1, sin), (x2, cos))):
                c.op("dve", lambda e: e.tensor_tensor(kr[tt % 8][:, q_, :], xa, tb, ALU.mult),
                     reads=[bh_, brope], writes=[bkr[tt % 8]])

        def U4(tt):
            c.op("pool", lambda e: e.tensor_tensor(rstd[tt % 2], st2[tt % 2], g.mhalf[:, 0:2], ALU.pow),
                 reads=[bst2[tt % 2], g.b_mh], writes=[brstd[tt % 2]])

        def U5(tt):
            h_, bh_ = hsb[tt % 4], bhsb[tt % 4]
            r_, br_ = rstd[tt % 2], brstd[tt % 2]
            c.op("act", lambda e: e.activation(cn[tt % 2][:, 0:256], h_[:, 0:256], AF.Copy, scale=r_[:, 0:1]),
                 reads=[bh_, br_], writes=[bcn[tt % 2]])
            c.op("dve", lambda e: e.tensor_scalar(cn[tt % 2][:, 256:384], h_[:, 256:384], r_[:, 1:2], None, ALU.mult),
                 reads=[bh_, br_], writes=[bcn[tt % 2]])

        def U6(tt):
            for b_ in range(3):
                c.op("pe", lambda e: e.transpose(pTc[:, b_, :], cn[tt % 2][:, b_ * 128:(b_ + 1) * 128], g.ident),
                     reads=[bcn[tt % 2], g.b_const], writes=[bpTc], signal=(b_ == 2))

        def U7(tt):
            cp(c, "act", cT[tt % 2], pTc[:, 0:3, :], [bpTc], [bcT[tt % 2]])

        def U8(tt):
            ct_, bct_ = cT[tt % 2], bcT[tt % 2]
            for kc in range(2):
                c.op("pe", lambda e: e.matmul(pX[:, 0, 0:384], ct_[:, kc, :], wuq[:, kc, 0:384], start=(kc == 0), stop=(kc == 1)),
                     reads=[bct_, bwu], writes=[bqa], signal=(kc == 1))
            for kc in range(2):
                c.op("pe", lambda e: e.matmul(pX[:, 1, 0:192], ct_[:, kc, :], wuq[:, kc, 384:576], start=(kc == 0), stop=(kc == 1)),
                     reads=[bct_, bwu], writes=[bqr2], signal=(kc == 1))
            c.op("pe", lambda e: e.matmul(pX[:, 2, 0:384], ct_[:, 2, :], wukv[:, 0:384], start=True, stop=True),
                 reads=[bct_, bwk], writes=[bkv])
            c.op("pe", lambda e: e.matmul(pX[:, 3, 0:384], ct_[:, 2, :], wukv[:, 384:768], start=True, stop=True),
                 reads=[bct_, bwk], writes=[bvv])

        def U9(tt):
            cos, sin = ropeA[:, tt, 0, :], ropeA[:, tt, 1, :]
            qa_, bqa_ = qab[tt % 3], bqab[tt % 3]
            ka_, bka_ = kab[tt % 5], bkab[tt % 5]
            c.op("act", lambda e: e.activation(qa_[:, :, 0:64], v6(pX[:, 0, 0:384], 64), AF.Copy), reads=[bqa], writes=[bqa_])
            cosb = cos.unsqueeze(1).to_broadcast([128, 6, 16])
            sinb = sin.unsqueeze(1).to_broadcast([128, 6, 16])
            qx1 = v6(pX[:, 1, 0:96], 16)
            qx2 = v6(pX[:, 1, 96:192], 16)
            for q_, (xa, tb) in enumerate(((qx1, cosb), (qx2, sinb), (qx1, sinb), (qx2, cosb))):
                c.op("dve", lambda e: e.tensor_tensor(qr[tt % 2][:, q_], xa, tb, ALU.mult), reads=[bqr2, brope], writes=[bqr[tt % 2]])
            c.op("act", lambda e: e.activation(ka_[:, :, 0:64], v6(pX[:, 2, 0:384], 64), AF.Copy), reads=[bkv], writes=[bka_])
            c.op("dve", lambda e: e.tensor_copy(vab[tt % 2], v6(pX[:, 3, 0:384], 64)), reads=[bvv], writes=[bvab[tt % 2]])
            c.dma("sp", va_v[:, tt], vab[tt % 2], bvab[tt % 2], reads=[bvab[tt % 2]], writes=[g.b_scr])

        def U10(tt):
            qa_, bqa_ = qab[tt % 3], bqab[tt % 3]
            ka_, bka_ = kab[tt % 5], bkab[tt % 5]
            q_ = qr[tt % 2]
            c.op("pool", lambda e: e.tensor_tensor(qa_[:, :, 64:80], q_[:, 0], q_[:, 1], ALU.subtract), reads=[bqr[tt % 2]], writes=[bqa_])
            c.op("pool", lambda e: e.tensor_tensor(qa_[:, :, 80:96], q_[:, 2], q_[:, 3], ALU.add), reads=[bqr[tt % 2]], writes=[bqa_])
            krb = lambda j_: kr[tt % 8][:, j_, :].unsqueeze(1).to_broadcast([128, 6, 16])
            c.op("pool", lambda e: e.tensor_tensor(ka_[:, :, 64:80], krb(0), krb(1), ALU.subtract), reads=[bkr[tt % 8]], writes=[bka_])
            c.op("pool", lambda e: e.tensor_tensor(ka_[:, :, 80:96], krb(2), krb(3), ALU.add), reads=[bkr[tt % 8]], writes=[bka_])

        def U11(tt):
            for h in range(6):
                c.op("pe", lambda e: e.transpose(pTq[0:96, h, :], qab[tt % 3][:, h, :], g.ident),
                     reads=[bqab[tt % 3], g.b_const], writes=[bpTq], signal=(h == 5))

        def U12(tt):
            s, t = tt // 4, tt % 4
            cp(c, "dve", qaTs[s % 2][0:96, :, t * 128:(t + 1) * 128], pTq[0:96, 0:6, :], [bpTq], [bqaT[s % 2]])

        def U13(tt):
            for h in range(6):
                c.op("pe", lambda e: e.transpose(pTk[0:96, h, :], kab[tt % 5][:, h, :], g.ident),
                     reads=[bkab[tt % 5], g.b_const], writes=[bpTk], signal=(h == 5))

        def U14(tt):
            s, t = tt // 4, tt % 4
            si = s % 2
            cp(c, "act", kaTs[si][0:96, :, t * 128:(t + 1) * 128], pTk[0:96, 0:6, :], [bpTk], [bkaT[si]])
            if t == 3:
                c.dma("sp", qaT_v[:, :, s * TS:(s + 1) * TS], qaTs[si][0:96], bqaT[si], reads=[bqaT[si]], writes=[g.b_scr])
                c.dma("sp", kaT_v[:, :, s * TS:(s + 1) * TS], kaTs[si][0:96], bkaT[si], reads=[bkaT[si]], writes=[g.b_scr])

        skew([U1, U2, U3, U4, U5, U6, U7, U8, U9, U10, U11, U12, U13, U14], list(range(DBG_NT[0])), rev=True)
        c.barrier()


def a2_weights(g, es, l):
    c = g.c
    W0 = 416
    aw = G()
    aw.win = sb(es, g, "a2_win", [128, 8, 1664], BF16)
    aw.ggain = sb(es, g, "a2_gg", [128, 384], F32)
    aw.ropeG = sb(es, g, "a2_rope", [128, NT, 2, 2, 16], F32)
    aw.bw, aw.bgg, aw.brope = Buf("a2win"), Buf("a2gg"), Buf("a2rope")

    def load():
        wv = g.w_in[l].rearrange("(kc p) n -> p kc n", p=128)
        for kc in range(8):
            c.dma("pool", aw.win[:, kc, 0:832], wv[:, kc, W0:W0 + 832], aw.bw, writes=[aw.bw])
            c.dma("pool", aw.win[:, kc, 832:1664], wv[:, kc, W0 + 832:W0 + 1664], aw.bw, writes=[aw.bw])
        c.dma("sp", aw.ggain, g.d_ggain[l].partition_broadcast(128), aw.bgg, writes=[aw.bgg])
        c.dma("sp", aw.ropeG, g.d_ropeG, aw.brope, writes=[aw.brope])

    aw.load = load
    aw.bufs = [aw.bw, aw.bgg, aw.brope]
    return aw


def phase_a2(g, l, aw):
    c = g.c
    with ExitStack() as es:
        W0 = 416
        win, ggain, ropeG = aw.win, aw.ggain, aw.ropeG
        xts = [sb(es, g, "a2_xt%d" % i, [128, 8, TS], BF16) for i in range(2)]
        fT = [sb(es, g, "a2_fT%d" % i, [128, 6, TS], BF16) for i in range(2)]
        gTs = [sb(es, g, "a2_gT%d" % i, [128, 4, TS], BF16) for i in range(2)]

        def slots(name, shape, dt, n):
            return [sb(es, g, "a2_%s%d" % (name, i), shape, dt) for i in range(n)], [Buf("%s%d" % (name, i)) for i in range(n)]

        dvb, bdvb = slots("dvb", [128, 6, 64], BF16, 2)
        gvb, bgvb = slots("gvb", [128, 2, 64], BF16, 2)
        gsb, bgsb = slots("gsb", [128, 384], F32, 5)
        sq, bsq = slots("sq", [128, 384], F32, 3)
        ms, bms = slots("ms", [128, 6], F32, 2)
        ms2, bms2 = slots("msb", [128, 6], F32, 3)
        rstd, brstd = slots("rstd", [128, 6], F32, 3)
        gnt, bgn = slots("gn", [128, 384], F32, 4)
        tmp, btmp = slots("tmp", [128, 4, 192], F32, 3)
        gb, bgb = slots("gb", [128, 8, 64], BF16, 3)
        pF = [ps(es, g, "a2_pF%d" % i, [128, 512]) for i in range(2)]
        pDV = [ps(es, g, "a2_pDV%d" % i, [128, 512]) for i in range(2)]
        pG = [ps(es, g, "a2_pG%d" % i, [128, 512]) for i in range(2)]
        pT = [ps(es, g, "a2_pT%d" % i, [128, 8, 128], BF16) for i in range(2)]
        B = lambda n: Buf(n)
        bw, bgg, brope = aw.bw, aw.bgg, aw.brope
        (bxt, bfT, bgT, bpF, bpDV, bpG, bpT) = ([B("z%d" % i) for i in range(2)] for _ in range(7))
        xT_v = g.xT.rearrange("(kc p) n -> p kc n", p=128)
        dqT_v = g.dqT.rearrange("j p n -> p j n")
        dkT_v = g.dkT.rearrange("j p n -> p j n")
        gqT_v = g.gqT.rearrange("j p n -> p j n")
        gkT_v = g.gkT.rearrange("j p n -> p j n")
        dv_v = g.dv.rearrange("h p t d -> p t h d")
        gv_v = g.gv.rearrange("h p t d -> p t h d")
        nf = [0]
        hd = lambda ap: ap.rearrange("p (h d) -> p h d", d=64)

        def T1(tt):
            s, t, i = tt // 4, tt % 4, tt % 2
            si = s % 2
            xs = xts[si]
            if tt == 0:
                c.dma("sp", xs, xT_v[:, :, 0:TS], bxt[0], writes=[bxt[0]])
            if t == 1 and s + 1 < NS:
                c.dma("sp", xts[1 - si], xT_v[:, :, (s + 1) * TS:(s + 2) * TS], bxt[1 - si], writes=[bxt[1 - si]])
            for gi in ((0, 1), (2, 3), (4,), (5,))[t]:
                cb = (C_DQ - W0) + gi * 128
                fi = nf[0] % 2
                nf[0] += 1
                for kc in range(8):
                    c.op("pe", lambda e: e.matmul(pF[fi], win[:, kc, cb:cb + 128], xs[:, kc, :], start=(kc == 0), stop=(kc == 7)),
                         reads=[bxt[si], bw], writes=[bpF[fi]], signal=(kc == 7))
                cp(c, "act" if gi % 2 == 0 else "dve", fT[si][:, gi, :], pF[fi], [bpF[fi]], [bfT[si]])
            if t == 3:
                c.dma("sp", dqT_v[:, :, s * TS:(s + 1) * TS], fT[si][:, 0:3, :], bfT[si], reads=[bfT[si]], writes=[g.b_scr])
                c.dma("sp", dkT_v[:, :, s * TS:(s + 1) * TS], fT[si][:, 3:6, :], bfT[si], reads=[bfT[si]], writes=[g.b_scr])
            lhs = lambda kc: xs[:, kc, t * 128:(t + 1) * 128]
            for kc in range(8):
                c.op("pe", lambda e: e.matmul(pDV[i][:, 0:384], lhs(kc), win[:, kc, C_DV - W0:C_DV - W0 + 384],
                                              start=(kc == 0), stop=(kc == 7)),
                     reads=[bxt[si], bw], writes=[bpDV[i]], signal=(kc == 7))
            for kc in range(8):
                c.op("pe", lambda e: e.matmul(pG[i], lhs(kc), win[:, kc, C_GQ - W0:C_GQ - W0 + 512],
                                              start=(kc == 0), stop=(kc == 7)),
                     reads=[bxt[si], bw], writes=[bpG[i]], signal=(kc == 7))

        def T2(tt):
            i = tt % 2
            cp(c, "act", gsb[tt % 5], pG[i][:, 0:384], [bpG[i]], [bgsb[tt % 5]])
            c.op("act", lambda e: e.activation(sq[tt % 3], pG[i][:, 0:384], AF.Square, scale=0.125), reads=[bpG[i]], writes=[bsq[tt % 3]])
            cp(c, "act", gvb[i].rearrange("p h d -> p (h d)"), pG[i][:, 384:512], [bpG[i]], [bgvb[i]])
            c.dma("sp", gv_v[:, tt], gvb[i], bgvb[i], reads=[bgvb[i]], writes=[g.b_scr])
            cp(c, "act", dvb[i].rearrange("p h d -> p (h d)"), pDV[i][:, 0:384], [bpDV[i]], [bdvb[i]])
            c.dma("sp", dv_v[:, tt], dvb[i], bdvb[i], reads=[bdvb[i]], writes=[g.b_scr])

        def T3(tt):
            c.op("dve", lambda e: e.tensor_reduce(ms[tt % 2], hd(sq[tt % 3]), AX.X, ALU.add), reads=[bsq[tt % 3]], writes=[bms[tt % 2]])
            c.op("dve", lambda e: e.tensor_scalar(ms2[tt % 3], ms[tt % 2], 1e-6, None, ALU.add), reads=[bms[tt % 2]], writes=[bms2[tt % 3]])

        def T4(tt):
            c.op("pool", lambda e: e.tensor_tensor(rstd[tt % 3], ms2[tt % 3], g.mhalf[:, 0:6], ALU.pow),
                 reads=[bms2[tt % 3], g.b_mh], writes=[brstd[tt % 3]])

        def T5(tt):
            c.op("dve", lambda e: e.tensor_tensor(hd(gnt[tt % 4]), hd(gsb[tt % 5]),
                                                  rstd[tt % 3].unsqueeze(2).to_broadcast([128, 6, 64]), ALU.mult),
                 reads=[bgsb[tt % 5], brstd[tt % 3]], writes=[bgn[tt % 4]])

        def T6(tt):
            c.op("dve", lambda e: e.tensor_tensor(gnt[tt % 4], gnt[tt % 4], ggain, ALU.mult), reads=[bgn[tt % 4], bgg], writes=[bgn[tt % 4]])

        def T7(tt):
            v5 = gnt[tt % 4].rearrange("p (h r x d) -> p h r x d", h=6, r=2, x=2)
            x1 = v5[:, :, :, 0, :]
            x2 = v5[:, :, :, 1, :]
            cosb = ropeG[:, tt, 0].unsqueeze(1).to_broadcast([128, 6, 2, 16])
            sinb = ropeG[:, tt, 1].unsqueeze(1).to_broadcast([128, 6, 2, 16])
            tv = tmp[tt % 3].rearrange("p q (h r d) -> p q h r d", h=6, r=2)
            for q_, (xa, tb) in enumerate(((x1, cosb), (x2, sinb), (x1, sinb), (x2, cosb))):
                c.op("dve" if q_ < 2 else "pool", lambda e: e.tensor_tensor(tv[:, q_], xa, tb, ALU.mult),
                     reads=[bgn[tt % 4], brope], writes=[btmp[tt % 3]])

        def T8(tt):
            tv = tmp[tt % 3].rearrange("p q (h r d) -> p q h r d", h=6, r=2)
            gbt, bg_ = gb[tt % 3], bgb[tt % 3]
            gq5 = gbt.rearrange("p h (r x d) -> p h r x d", r=2, x=2)
            gk6 = gbt.rearrange("p (a b) (r x d) -> p a b r x d", b=2, r=2, x=2)
            c.op("pool", lambda e: e.tensor_tensor(gq5[:, 0:4, :, 0, :], tv[:, 0, 0:4], tv[:, 1, 0:4], ALU.subtract),
                 reads=[btmp[tt % 3]], writes=[bg_])
            c.op("pool", lambda e: e.tensor_tensor(gq5[:, 0:4, :, 1, :], tv[:, 2, 0:4], tv[:, 3, 0:4], ALU.add),
                 reads=[btmp[tt % 3]], writes=[bg_])
            for b_ in range(2):
                c.op("dve", lambda e: e.tensor_tensor(gk6[:, 2:4, b_, :, 0, :], tv[:, 0, 4:6], tv[:, 1, 4:6], ALU.subtract),
                     reads=[btmp[tt % 3]], writes=[bg_])
                c.op("dve", lambda e: e.tensor_tensor(gk6[:, 2:4, b_, :, 1, :], tv[:, 2, 4:6], tv[:, 3, 4:6], ALU.add),
                     reads=[btmp[tt % 3]], writes=[bg_])

        def T9(tt):
            i = tt % 2
            for b_ in range(4):
                c.op("pe", lambda e: e.transpose(pT[i][:, b_, :], gb[tt % 3][:, 2 * b_:2 * b_ + 2, :].rearrange("p h d -> p (h d)"), g.ident),
                     reads=[bgb[tt % 3], g.b_const], writes=[bpT[i]], signal=(b_ == 3))

        def T10(tt):
            s, t, i = tt // 4, tt % 4, tt % 2
            si = s % 2
            cp(c, "act", gTs[si][:, :, t * 128:(t + 1) * 128], pT[i][:, 0:4, :], [bpT[i]], [bgT[si]])
            if t == 3:
                c.dma("sp", gqT_v[:, :, s * TS:(s + 1) * TS], gTs[si][:, 0:2, :], bgT[si], reads=[bgT[si]], writes=[g.b_scr])
                c.dma("sp", gkT_v[:, :, s * TS:(s + 1) * TS], gTs[si][:, 2:4, :], bgT[si], reads=[bgT[si]], writes=[g.b_scr])

        skew([T1, T2, T3, T4, T5, T6, T7, T8, T9, T10], list(range(DBG_NT[0])))
        c.barrier()


def phase_attn(g, name, groups, scale, band, side=None):
    c = g.c
    LAG = 2
    with ExitStack() as es:
        qt = [sb(es, g, name + "_q%d" % i, [128, S], BF16) for i in range(4)]
        kt = [sb(es, g, name + "_k%d" % i, [128, S], BF16) for i in range(2)]
        vt = [sb(es, g, name + "_v%d" % i, [128, NT, 128], BF16) for i in range(4)]
        ot = [sb(es, g, name + "_o%d" % i, [64, S], BF16) for i in range(2)]
        pt = [sb(es, g, name + "_p%d" % i, [128, 512], BF16) for i in range(4)]
        rc = [sb(es, g, name + "_r%d" % i, [128, 512], F32) for i in range(2)]
        pS = [ps(es, g, name + "_pS%d" % i, [128, 512]) for i in range(4)]
        pO = [ps(es, g, name + "_pO%d" % i, [128, 512]) for i in range(2)]
        B = lambda n: Buf(n)
        bq, bk = [B("q%d" % i) for i in range(4)], [B("k0"), B("k1")]
        bv = [B("v%d" % i) for i in range(4)]
        bot, brc, bpO = ([B("o%d" % i) for i in range(2)] for _ in range(3))
        bpt, bpS = ([B("p%d" % i) for i in range(4)] for _ in range(2))
        btall = B("tall")
        if band:
            tall = sb(es, g, name + "_tall", [128, 6, TW], BF16)
            for h in range(6):
                src = bass.AP(g.rtab.tensor, h * LR, [[1, 128], [1, TW]])
                c.dma("sp", tall[:, h, :], src, btall, reads=[g.b_scr], writes=[btall])
        for i in range(4):
            c.op("pool", lambda e: e.memset(vt[i][:, :, 64:128], 1.0), writes=[bv[i]])
        padded = groups[0]["heads"][0]["K"] == 64
        if padded:
            for i in range(4):
                z0 = 64 * (1 - (i % 2))
                c.op("pool", lambda e: e.memset(qt[i][z0:z0 + 64, :], 0.0), writes=[bq[i]])

        def load_group(gi):
            gr = groups[gi]
            sl = gi % 2
            kp = gr["kp"]
            c.dma("sp", kt[sl][0:kp, :], gr["k"], bk[sl], reads=[g.b_scr], writes=[bk[sl]])
            for hi, hd in enumerate(gr["heads"]):
                vs = sl * 2 + hi
                p0_, K_ = hd["p0"], hd["K"]
                c.dma("sp", qt[vs][p0_:p0_ + K_, :], gr["q"][p0_:p0_ + K_, :], bq[vs], reads=[g.b_scr], writes=[bq[vs]])
                c.dma("sp", vt[vs][:, :, 0:64], hd["v"], bv[vs], reads=[g.b_scr], writes=[bv[vs]])

        flat = []
        hcount = 0
        for gi, gr in enumerate(groups):
            for hi, hd in enumerate(gr["heads"]):
                for cq in range(NS):
                    if band:
                        kbs = [kb for kb in range(4 * cq - 8, 4 * cq + 12) if 0 <= kb < NT]
                    else:
                        kbs = list(range(NT))
                    for ii, kb in enumerate(kbs):
                        flat.append((gi, hi, hcount, cq, ii, kb, len(kbs)))
                hcount += 1
        sidegen = side(es) if side is not None else None
        load_group(0)
        gstart = {}
        for idx, stp in enumerate(flat):
            gstart.setdefault(stp[0], idx)
        nchunk = 0
        for idx in range(len(flat) + LAG):
            if sidegen is not None and idx % 96 == 48:
                next(sidegen, None)
            if idx < len(flat):
                gi, hi, hc, cq, ii, kb, nkb = flat[idx]
                if idx - gstart[gi] == LAG + 1 and gi + 1 < len(groups):
                    load_group(gi + 1)
                gr = groups[gi]
                hd = gr["heads"][hi]
                sl = gi % 2
                p0, K = hd["p0"], hd["K"]
                sk = idx % 4
                kr_ = 128 if padded else K
                qs_ = sl * 2 + hi
                c.op("pe", lambda e: e.matmul(pS[sk], kt[sl][0:kr_, kb * 128:(kb + 1) * 128],
                                              qt[qs_][0:kr_, cq * TS:(cq + 1) * TS], start=True, stop=(not band)),
                     reads=[bk[sl], bq[qs_]], writes=[bpS[sk]], signal=(not band))
                if band:
                    off = 128 * (11 - (kb - 4 * cq))
                    c.op("pe", lambda e: e.matmul(pS[sk], g.antiI, tall[:, hd["tall"], off:off + TS], start=False, stop=True),
                         reads=[btall, g.b_const], writes=[bpS[sk]])
                c.op("act", lambda e: e.activation(pt[sk], pS[sk], AF.Exp, scale=scale), reads=[bpS[sk]], writes=[bpt[sk]])
            j = idx - LAG
            if j >= 0:
                gi, hi, hc, cq, ii, kb, nkb = flat[j]
                hd = groups[gi]["heads"][hi]
                sk = j % 4
                vs = (gi % 2) * 2 + hi
                if ii == 0:
                    osl = nchunk % 2
                    nchunk += 1
                c.op("pe", lambda e: e.matmul(pO[osl], vt[vs][:, kb, :], pt[sk], start=(ii == 0), stop=(ii == nkb - 1)),
                     reads=[bv[vs], bpt[sk]], writes=[bpO[osl]], signal=(ii == nkb - 1))
                if ii == nkb - 1:
                    hs = hc % 2
                    c.op("dve", lambda e: e.reciprocal(rc[osl][64:128, :], pO[osl][64:128, :]), reads=[bpO[osl]], writes=[brc[osl]])
                    c.op("dve", lambda e: e.tensor_tensor(ot[hs][:, cq * TS:(cq + 1) * TS], pO[osl][0:64, :], rc[osl][64:128, :], ALU.mult),
                         reads=[bpO[osl], brc[osl]], writes=[bot[hs]])
                    if cq == NS - 1:
                        r0 = hd["row0"]
                        c.dma("pool", g.mixT[r0:r0 + 64, :], ot[hs], bot[hs], reads=[bot[hs]], writes=[g.b_scr])
        if sidegen is not None:
            for _ in sidegen:
                pass
        c.barrier()


def layer_norm(c, y, by, out, bout, lnp, blnp, gi, sc, bsc):
    st, mv, rs = sc
    for h in range(2):
        c.op("dve", lambda e: e.bn_stats(st[:, h, :], y[:, h * 512:(h + 1) * 512]), reads=[by], writes=[bsc])
    c.op("dve", lambda e: e.bn_aggr(mv, st.rearrange("p a b -> p (a b)")), reads=[bsc], writes=[bsc])
    c.op("dve", lambda e: e.tensor_scalar(rs[:, 0:1], mv[:, 1:2], 1e-5, None, ALU.add), reads=[bsc], writes=[bsc])
    c.op("act", lambda e: e.activation(rs[:, 1:2], rs[:, 0:1], AF.Sqrt), reads=[bsc], writes=[bsc])
    c.op("dve", lambda e: e.reciprocal(rs[:, 0:1], rs[:, 1:2]), reads=[bsc], writes=[bsc])
    c.op("dve", lambda e: e.tensor_scalar(out, y, mv[:, 0:1], rs[:, 0:1], ALU.subtract, ALU.mult),
         reads=[by, bsc], writes=[bout])
    c.op("pool", lambda e: e.tensor_tensor(out, out, lnp[:, gi, :], ALU.mult), reads=[bout, blnp], writes=[bout])
    c.op("pool", lambda e: e.tensor_tensor(out, out, lnp[:, gi + 1, :], ALU.add), reads=[bout, blnp], writes=[bout])


def c_weights(g, es, l):
    c = g.c
    cw = G()
    cw.wout = sb(es, g, "c_wout", [128, 8, D], BF16)
    cw.wd = sb(es, g, "c_wd", [128, NJ, D], BF16)
    cw.lnp = sb(es, g, "c_lnp", [128, 4, D], F32)
    cw.bwout, cw.bwd, cw.blnp = Buf("cwout"), Buf("cwd"), Buf("clnp")
    def loader(_es=None):
        wov = g.w_out[l].rearrange("(kc p) n -> p kc n", p=128)
        for kc in range(8):
            c.dma("pool", cw.wout[:, kc, :], wov[:, kc, :], cw.bwout, writes=[cw.bwout])
            if kc % 2 == 1:
                yield
        wdv = g.wd[l].rearrange("(j p) n -> p j n", p=128)
        for j in range(NJ):
            c.dma("pool", cw.wd[:, j, :], wdv[:, j, :], cw.bwd, writes=[cw.bwd])
            if j % 2 == 1:
                yield
        for i_, v_ in enumerate((g.ln1g, g.ln1b, g.ln2g, g.ln2b)):
            c.dma("sp", cw.lnp[:, i_, :], v_[l].partition_broadcast(128), cw.blnp, writes=[cw.blnp])
        yield

    cw.loader = loader
    g.c.persist += [cw.bwout, cw.bwd, cw.blnp]
    return cw


def phase_c(g, l, last, cw):
    c = g.c
    xres = g.x if l == 0 else g.xres1
    dst = g.out if last else g.xres1
    wout, wd, lnp = cw.wout, cw.wd, cw.lnp
    bwout, bwd, blnp = cw.bwout, cw.bwd, cw.blnp
    with ExitStack() as es:
        NWGU = 3
        mx = sb(es, g, "c_mx", [128, 8, TS], BF16)
        xr = [sb(es, g, "c_xr%d" % i, [128, D], F32) for i in range(4)]
        x1s = [sb(es, g, "c_x1s%d" % i, [128, 4, D], F32) for i in range(2)]
        xb1 = [sb(es, g, "c_xb1%d" % i, [128, D], BF16) for i in range(4)]
        o = [sb(es, g, "c_o%d" % i, [128, D], F32) for i in range(2)]
        x1T = sb(es, g, "c_x1T", [128, 8, TS], BF16)
        aT = sb(es, g, "c_aT", [128, NJ, TS], BF16)
        wgu = [sb(es, g, "c_wgu%d" % i, [128, 2, 8, 128], BF16) for i in range(NWGU)]
        sg = [sb(es, g, "c_sg%d" % i, [128, TS], F32) for i in range(2)]
        sc1 = [(sb(es, g, "c_st%d" % i, [128, 2, 6], F32), sb(es, g, "c_mv%d" % i, [128, 2], F32),
                sb(es, g, "c_rs%d" % i, [128, 2], F32)) for i in range(4)]
        sc3 = [(sb(es, g, "c_st3%d" % i, [128, 2, 6], F32), sb(es, g, "c_mv3%d" % i, [128, 2], F32),
                sb(es, g, "c_rs3%d" % i, [128, 2], F32)) for i in range(2)]
        if not last:
            xb3 = [sb(es, g, "c_xb3%d" % i, [128, D], BF16) for i in range(2)]
            xTn = sb(es, g, "c_xTn", [128, 8, TS], BF16)
        PP = [ps(es, g, "c_PP%d" % i, [128, 2, 512]) for i in range(3)]
        pT = [ps(es, g, "c_pT%d" % i, [128, 8, 128], BF16) for i in range(2)]
        B = lambda n: Buf(n)
        bmx, bx1T, baT, bxTn = (B("c%d" % i) for i in range(4))
        bx1s = [[B("x1s%d_%d" % (a_, i)) for i in range(4)] for a_ in range(2)]
        bxr, bxb1, bxb3, bsc1 = ([B("d%d" % i) for i in range(4)] for _ in range(4))
        bo, bod, bsg, bpT, bsc3 = ([B("e%d" % i) for i in range(2)] for _ in range(5))
        bwgu = [B("wgu%d" % i) for i in range(NWGU)]
        bPP = [B("PP%d" % i) for i in range(3)]
        mixT_v = g.mixT.rearrange("(kc p) n -> p kc n", p=128)
        xT_v = g.xT.rearrange("(kc p) n -> p kc n", p=128)
        cnt = dict(pp=0, tp=0, nj=0)

        def nxt(k, n):
            v = cnt[k] % n
            cnt[k] += 1
            return v

        def transposes(src_b, bsrc, dstT, bdst, t):
            ti = nxt("tp", 2)
            for kc in range(8):
                c.op("pe", lambda e: e.transpose(pT[ti][:, kc, :], src_b[:, kc * 128:(kc + 1) * 128], g.ident),
                     reads=[bsrc, g.b_const], writes=[bpT[ti]], signal=(kc == 7))
            cp(c, "act", dstT[:, :, t * 128:(t + 1) * 128], pT[ti], [bpT[ti]], [bdst])

        def ln_stats(yv, byv, scr, bscr):
            st, mv, rs = scr
            for h in range(2):
                c.op("dve", lambda e: e.bn_stats(st[:, h, :], yv[:, h * 512:(h + 1) * 512]), reads=[byv], writes=[bscr])
            c.op("dve", lambda e: e.bn_aggr(mv, st.rearrange("p a b -> p (a b)")), reads=[bscr], writes=[bscr])
            c.op("dve", lambda e: e.tensor_scalar(rs[:, 1:2], mv[:, 1:2], 1e-5, None, ALU.add), reads=[bscr], writes=[bscr])
            c.op("pool", lambda e: e.tensor_tensor(rs[:, 0:1], rs[:, 1:2], g.mhalf[:, 0:1], ALU.pow), reads=[bscr, g.b_mh], writes=[bscr])

        def ln_apply(yv, byv, scr, bscr, out, bout, gi):
            st, mv, rs = scr
            c.op("dve", lambda e: e.tensor_scalar(out, yv, mv[:, 0:1], rs[:, 0:1], ALU.subtract, ALU.mult),
                 reads=[byv, bscr], writes=[bout])
            c.op("pool", lambda e: e.tensor_tensor(out, out, lnp[:, gi, :], ALU.mult), reads=[bout, blnp], writes=[bout])
            c.op("pool", lambda e: e.tensor_tensor(out, out, lnp[:, gi + 1, :], ALU.add), reads=[bout, blnp], writes=[bout])

        def load_mx(s1):
            c.dma("sp", mx, mixT_v[:, :, s1 * TS:(s1 + 1) * TS], bmx, reads=[g.b_scr], writes=[bmx])

        def P1(s1, t):
            tt = s1 * 4 + t
            pp = nxt("pp", 3)
            c.dma("sp", xr[t], xres[tt * 128:(tt + 1) * 128, :], bxr[t], reads=[g.b_scr], writes=[bxr[t]])
            for hf in range(2):
                for kc in range(8):
                    c.op("pe", lambda e: e.matmul(PP[pp][:, hf, :], mx[:, kc, t * 128:(t + 1) * 128], wout[:, kc, hf * 512:(hf + 1) * 512],
                                                  start=(kc == 0), stop=(kc == 7)),
                         reads=[bmx, bwout], writes=[bPP[pp]], signal=(kc == 7))
            c.op("dve", lambda e: e.scalar_tensor_tensor(xr[t], xr[t], DN_ALPHA, PP[pp].rearrange("p a n -> p (a n)"), ALU.mult, ALU.add),
                 reads=[bxr[t], bPP[pp]], writes=[bxr[t]])

        def P2(s1, t):
            ln_stats(xr[t], bxr[t], sc1[t], bsc1[t])

        def P3(s1, t):
            a_ = s1 % 2
            ln_apply(xr[t], bxr[t], sc1[t], bsc1[t], x1s[a_][:, t, :], bx1s[a_][t], 0)
            cp(c, "act", xb1[t], x1s[a_][:, t, :], [bx1s[a_][t]], [bxb1[t]])

        def P4(s1, t):
            transposes(xb1[t], bxb1[t], x1T, bx1T, t)

        def Q1(s, t):
            a_ = s % 2
            pp = nxt("pp", 3)
            for hf in range(2):
                for j in range(NJ):
                    c.op("pe", lambda e: e.matmul(PP[pp][:, hf, :], aT[:, j, t * 128:(t + 1) * 128], wd[:, j, hf * 512:(hf + 1) * 512],
                                                  start=(j == 0), stop=(j == NJ - 1)),
                         reads=[baT, bwd], writes=[bPP[pp]], signal=(j == NJ - 1))
            xv = x1s[a_][:, t, :]
            c.op("dve", lambda e: e.scalar_tensor_tensor(xv, xv, DN_ALPHA, PP[pp].rearrange("p a n -> p (a n)"), ALU.mult, ALU.add),
                 reads=[bx1s[a_][t], bPP[pp]], writes=[bx1s[a_][t]])

        def Q2(s, t):
            a_ = s % 2
            ln_stats(x1s[a_][:, t, :], bx1s[a_][t], sc3[t % 2], bsc3[t % 2])

        def Q3(s, t):
            a_ = s % 2
            tt = s * 4 + t
            i = t % 2
            ln_apply(x1s[a_][:, t, :], bx1s[a_][t], sc3[i], bsc3[i], o[i], bo[i], 2)
            c.dma("pool", dst[tt * 128:(tt + 1) * 128, :], o[i], bod[i], reads=[bo[i]], writes=[g.b_out])
            if not last:
                cp(c, "pool", xb3[t % 2], o[i], [bo[i]], [bxb3[t % 2]])

        def Q4(s, t):
            transposes(xb3[t % 2], bxb3[t % 2], xTn, bxTn, t)
            if t == 3:
                c.dma("act", xT_v[:, :, s * TS:(s + 1) * TS], xTn, bxTn, reads=[bxTn], writes=[g.b_scr])

        def stage2(s, hooks):
            for j in range(NJ):
                wi = nxt("nj", NWGU)
                pp = nxt("pp", 3)
                gi = j % 2
                c.dma("sp", wgu[wi][:, 0], g.wgu[l, 0, j], bwgu[wi], reads=[g.b_scr], writes=[bwgu[wi]])
                c.dma("sp", wgu[wi][:, 1], g.wgu[l, 1, j], bwgu[wi], reads=[g.b_scr], writes=[bwgu[wi]])
                for gu in range(2):
                    for kc in range(8):
                        c.op("pe", lambda e: e.matmul(PP[pp][:, gu, :], wgu[wi][:, gu, kc, :], x1T[:, kc, :],
                                                      start=(kc == 0), stop=(kc == 7)),
                             reads=[bwgu[wi], bx1T], writes=[bPP[pp]], signal=(kc == 7))
                c.op("act", lambda e: e.activation(sg[gi], PP[pp][:, 0, :], AF.Silu), reads=[bPP[pp]], writes=[bsg[gi]])
                c.op("dve", lambda e: e.tensor_tensor(aT[:, j, :], sg[gi], PP[pp][:, 1, :], ALU.mult),
                     reads=[bsg[gi], bPP[pp]], writes=[baT])
                for h_ in hooks.get(j, ()):
                    h_()

        T4 = [0, 1, 2, 3]
        load_mx(0)
        skew([lambda t: P1(0, t), lambda t: P2(0, t), lambda t: P3(0, t), lambda t: P4(0, t)], T4)
        for s in range(NS):
            hooks = {}
            if s > 0 and not last:
                for t in (2, 3):
                    hooks.setdefault(t, []).append(lambda t=t: Q4(s - 1, t))
            if s + 1 < NS:
                hooks.setdefault(12, []).append(lambda: load_mx(s + 1))
            stage2(s, hooks)
            if s + 1 < NS:
                skew([lambda t: P1(s + 1, t), lambda t: P2(s + 1, t), lambda t: P3(s + 1, t)], T4)
            nx = s + 1 < NS
            Q1(s, 0)
            Q1(s, 1)
            Q2(s, 0)
            if nx:
                P4(s + 1, 0)
                P4(s + 1, 1)
            Q1(s, 2)
            Q2(s, 1)
            Q3(s, 0)
            if nx:
                P4(s + 1, 2)
                P4(s + 1, 3)
            Q1(s, 3)
            Q2(s, 2)
            Q3(s, 1)
            if not last:
                Q4(s, 0)
                Q4(s, 1)
            Q2(s, 3)
            Q3(s, 2)
            Q3(s, 3)
        if not last:
            for t in (2, 3):
                Q4(NS - 1, t)
        c.barrier()
        for b_ in (cw.bwout, cw.bwd, cw.blnp):
            g.c.persist.remove(b_)


def w_gen(g, es, layers):
    c = g.c
    st = [sb(es, g, "w_st%d" % i, [128, 8, 512], F32) for i in range(2)]
    wb = [sb(es, g, "w_wb%d" % i, [128, 8, 512], BF16) for i in range(2)]
    bst = [Buf("wst%d" % i) for i in range(2)]
    bwb = [Buf("wwb%d" % i) for i in range(2)]
    n = 0
    for l in layers:
        for gu, w in enumerate((g.wg, g.wu)):
            wv = w[l].rearrange("(kc p) n -> p kc n", p=128)
            for c0 in range(0, DFF, 512):
                cw_ = min(512, DFF - c0)
                i = n % 2
                c.dma("sp", st[i][:, :, 0:cw_], wv[:, :, c0:c0 + cw_], bst[i], writes=[bst[i]])
                cp(c, "pool", wb[i][:, :, 0:cw_], st[i][:, :, 0:cw_], [bst[i]], [bwb[i]])
                for jj in range(cw_ // 128):
                    j = c0 // 128 + jj
                    c.dma("sp", g.wgu[l, gu, j], wb[i][:, :, jj * 128:(jj + 1) * 128], bwb[i],
                          reads=[bwb[i]], writes=[g.b_scr])
                n += 1
                yield


IN_SPECS = [
    ("x", [S, D]), ("w_in", [DEPTH, D, IN_W]), ("w_uq", [DEPTH, 256, 576]), ("w_ukv", [DEPTH, 128, 768]),
    ("qn", [DEPTH, 256]), ("kvn", [DEPTH, 128]), ("d_ggain", [DEPTH, 384]), ("w_out", [DEPTH, D, D]),
    ("wg", [DEPTH, D, DFF]), ("wu", [DEPTH, D, DFF]), ("wd", [DEPTH, DFF, D]),
    ("ln1g", [DEPTH, D]), ("ln1b", [DEPTH, D]), ("ln2g", [DEPTH, D]), ("ln2b", [DEPTH, D]),
    ("d_ident", [128, 128]), ("d_antiI", [128, 128]), ("d_ropeA", [128, NT, 2, 16]), ("d_ropeG", [128, NT, 2, 2, 16]),
    ("d_biasg", [6, LR]), ("d_logm", [6, LR]),
]
SCR_SPECS = [
    ("xT", [D, S], BF16), ("qaT", [6, 96, S], BF16), ("kaT", [6, 96, S], BF16), ("va", [6, 128, NT, 64], BF16),
    ("dqT", [3, 128, S], BF16), ("dkT", [3, 128, S], BF16), ("dv", [6, 128, NT, 64], BF16),
    ("gqT", [2, 128, S], BF16), ("gkT", [2, 128, S], BF16), ("gv", [2, 128, NT, 64], BF16),
    ("mixT", [D, S], BF16), ("xres1", [S, D], F32), ("wgu", [DEPTH, 2, NJ, 128, 8, 128], BF16), ("rtab", [6, LR], BF16),
]


def attn_groups(g):
    mla = [dict(q=g.qaT[h], k=g.kaT[h], kp=96, heads=[dict(p0=0, K=96, v=g.va[h], row0=h * 64, tall=None)]) for h in range(6)]
    dil = [dict(q=g.dqT[j], k=g.dkT[j], kp=128,
                heads=[dict(p0=64 * e, K=64, v=g.dv[2 * j + e], row0=384 + (2 * j + e) * 64, tall=2 * j + e) for e in range(2)])
           for j in range(3)]
    gqa = [dict(q=g.gqT[j], k=g.gkT[j], kp=128,
                heads=[dict(p0=64 * e, K=64, v=g.gv[j], row0=768 + (2 * j + e) * 64, tall=None) for e in range(2)])
           for j in range(2)]
    return mla, dil, gqa


def build(phases=None, dbg=False):
    nc = bass.Bass("TRN2", target_bir_lowering=False)
    g = G()
    g.nc = nc
    for name, shape in IN_SPECS:
        setattr(g, name, nc.dram_tensor(name, shape, F32, kind="ExternalInput").ap())
    for name, shape, dt in SCR_SPECS:
        setattr(g, name, nc.dram_tensor(name, shape, dt, kind=("ExternalOutput" if dbg else "Internal")).ap())
    g.out = nc.dram_tensor("out", [S, D], F32, kind="ExternalOutput").ap()
    g.c = Ctx(nc)
    g.b_scr = Buf("scr")
    g.b_out = Buf("out")
    g.c.persist = [g.b_scr, g.b_out]
    run = (lambda p: True) if phases is None else (lambda p: p in phases)
    with ExitStack() as es:
        phase_consts(g, es)
        g.c.persist += [g.b_const, g.b_mh]
        if run("x0"):
            phase_x0(g)
        if run("b0"):
            phase_b0(g)
        mla, dil, gqa = attn_groups(g)
        for l in range(DEPTH):
            with ExitStack() as es_a:
                aw = a2_weights(g, es_a, l) if run("a2_%d" % l) else None
                if aw is not None:
                    g.c.persist += aw.bufs
                if run("a1_%d" % l):
                    phase_a1(g, l, after_loads=(aw.load if aw is not None else None))
                elif aw is not None:
                    aw.load()
                if aw is not None:
                    phase_a2(g, l, aw)
                    for b_ in aw.bufs:
                        g.c.persist.remove(b_)
            if run("mla_%d" % l):
                phase_attn(g, "ma", mla, 96.0 ** -0.5, False, side=(lambda es_, l=l: w_gen(g, es_, [l])))
            if run("dil_%d" % l):
                phase_attn(g, "md", dil, 0.125, True)
            with ExitStack() as es_c:
                cw = c_weights(g, es_c, l) if run("c_%d" % l) else None
                if run("gqa_%d" % l):
                    phase_attn(g, "mg", gqa, 0.125, False, side=(cw.loader if cw is not None else None))
                elif cw is not None:
                    for _ in cw.loader():
                        pass
                if run("c_%d" % l):
                    phase_c(g, l, l == DEPTH - 1, cw)
        g.c.barrier()
    return nc


def host_inputs(inputs):
    f = lambda a: np.ascontiguousarray(np.asarray(a, dtype=np.float32))
    ropeA, ropeG = _rope_tables()
    biasg, logm = _dil_tables(f(inputs["rel_bias"]))
    gq, gk = f(inputs["gqa_q_norm"]), f(inputs["gqa_k_norm"])
    ggain = np.concatenate([np.tile(gq, (1, 4)), np.tile(gk, (1, 2))], axis=1)
    common = {
        "w_in": f(inputs["w_in"]), "w_uq": f(inputs["mla_w_uq"]), "w_ukv": f(inputs["mla_w_ukv"]),
        "qn": f(inputs["mla_q_norm"]), "kvn": f(inputs["mla_kv_norm"]), "d_ggain": f(ggain),
        "w_out": f(inputs["w_out"]), "wg": f(inputs["ffn_w_gate"]), "wu": f(inputs["ffn_w_up"]), "wd": f(inputs["ffn_w_down"]),
        "ln1g": f(inputs["ln1_g"]), "ln1b": f(inputs["ln1_b"]), "ln2g": f(inputs["ln2_g"]), "ln2b": f(inputs["ln2_b"]),
        "d_ident": np.eye(128, dtype=np.float32), "d_antiI": np.ascontiguousarray(np.eye(128, dtype=np.float32)[::-1]),
        "d_ropeA": ropeA, "d_ropeG": ropeG, "d_biasg": biasg, "d_logm": logm,
    }
    x = f(inputs["x"])
    return [dict(common, x=np.ascontiguousarray(x[b])) for b in range(x.shape[0])]


_NC_CACHE = {}


def kernel(**inputs):
    in_maps = host_inputs(inputs)
    if "nc" not in _NC_CACHE:
        _NC_CACHE["nc"] = build()
    res = run_bass_kernel_spmd(_NC_CACHE["nc"], in_maps, core_ids=list(range(len(in_maps))))
    return np.stack([np.asarray(r["out"], dtype=np.float32) for r in res.results], axis=0)
```

```python
import math
import numpy as np
import ml_dtypes
import concourse.bass as bass
import concourse.mybir as mybir
from concourse.bass_utils import run_bass_kernel_spmd
from concourse.alu_op_type import AluOpType as ALU

AF = mybir.ActivationFunctionType
F32 = mybir.dt.float32
BF16 = mybir.dt.bfloat16
AX = mybir.AxisListType

S = 4096
D = 1024
NT = 32
NS = 8
TS = 512
DEPTH = 2
IN_W = 2080
DFF = 2816
NJ = 22
LR = 3072
TW = 2944
DN_ALPHA = (2.0 * DEPTH) ** 0.25
C_CQ, C_CKV, C_KR, C_DQ, C_DK, C_DV, C_GQ, C_GK, C_GV = 0, 256, 384, 416, 800, 1184, 1568, 1824, 1952


class Buf:
    __slots__ = ("name", "w", "r", "dkey", "excl")

    def __init__(self, name, excl=False):
        self.name = name
        self.w = None
        self.r = []
        self.dkey = None
        self.excl = excl


class Ctx:
    def __init__(self, nc):
        self.nc = nc
        self.engs = {"pe": nc.tensor, "act": nc.scalar, "dve": nc.vector, "pool": nc.gpsimd, "sp": nc.sync}
        self.sem = {}
        self.cnt = {}
        self.seen = {e: {} for e in self.engs}
        self.dma_keys = set()
        for e in ("pe", "act", "dve", "pool"):
            self._mksem(e)

    def _mksem(self, key):
        self.sem[key] = self.nc.alloc_semaphore("s_" + key)
        self.cnt[key] = 0

    def _wait(self, eng, need):
        e = self.engs[eng]
        seen = self.seen[eng]
        for k, v in need.items():
            if k in self.dma_keys:
                v = self.cnt[k]
            if seen.get(k, 0) >= v:
                continue
            e.wait_ge(self.sem[k], v)
            seen[k] = v

    def _deps(self, eng, reads, writes):
        need = {}
        for b in reads:
            if b.w is not None:
                k, v = b.w
                if need.get(k, 0) < v:
                    need[k] = v
            if b.excl:
                for (k, v) in b.r:
                    if k != eng and need.get(k, 0) < v:
                        need[k] = v
        same = eng != "pe"
        for b in writes:
            if b.w is not None:
                k, v = b.w
                if (k != eng or same) and need.get(k, 0) < v:
                    need[k] = v
            for (k, v) in b.r:
                if (k != eng or same) and need.get(k, 0) < v:
                    need[k] = v
        return need

    def _record(self, ev, reads, writes):
        for b in reads:
            b.r.append(ev)
            if len(b.r) > 16:
                m = {}
                for k, v in b.r:
                    if m.get(k, 0) < v:
                        m[k] = v
                b.r = list(m.items())
        for b in writes:
            b.w = ev
            b.r = []

    def op(self, eng, fn, reads=(), writes=(), signal=True):
        self._wait(eng, self._deps(eng, reads, writes))
        ins = fn(self.engs[eng])
        if signal:
            self.cnt[eng] += 1
            ins.then_inc(self.sem[eng], 1)
            ev = (eng, self.cnt[eng])
        else:
            ev = (eng, self.cnt[eng] + 1)
        self._record(ev, reads, writes)
        return ins

    def dma(self, q, out, in_, owner, reads=(), writes=(), **kw):
        kind = "S" if q == "pool" else "H"
        if owner.dkey is None:
            fk = [k for k in getattr(self, "free_keys", []) if k[1] == kind]
            if fk:
                owner.dkey = fk[-1]
                self.free_keys.remove(fk[-1])
            else:
                owner.dkey = "d%s%d" % (kind, len(self.dma_keys))
                self._mksem(owner.dkey)
                self.dma_keys.add(owner.dkey)
        assert owner.dkey[1] == kind, (owner.name, owner.dkey, q)
        k = owner.dkey
        self._wait(q, self._deps(q, reads, writes))
        ins = self.engs[q].dma_start(out=out, in_=in_, **kw)
        self.cnt[k] += 16
        ins.then_inc(self.sem[k], 16)
        self._record((k, self.cnt[k]), reads, writes)
        return ins

    def barrier(self):
        need = {k: v for k, v in self.cnt.items() if v > 0}
        for e in self.engs:
            self._wait(e, {k: v for k, v in need.items() if k != e})
        keep = set()
        for b in getattr(self, "persist", []):
            b.w = None
            b.r = []
            if b.dkey is not None:
                keep.add(b.dkey)
        self.free_keys = [k for k in sorted(self.dma_keys) if k not in keep]


class G:
    pass


_UID = [0]


def _uname(name):
    _UID[0] += 1
    return "%s_%d" % (name, _UID[0])


def sb(es, g, name, shape, dt):
    return es.enter_context(g.nc.sbuf_tensor(_uname(name), list(shape), dt)).ap()


def ps(es, g, name, shape, dt=F32):
    return es.enter_context(g.nc.psum_tensor(_uname(name), list(shape), dt)).ap()


def _rope_tables():
    f32 = np.float32
    t = np.arange(S, dtype=f32)

    def cs(pos, d):
        inv = (np.float32(10000.0) ** (-np.arange(0, d, 2, dtype=f32) / f32(d))).astype(f32)
        ang = (pos[:, None] * inv[None, :]).astype(f32)
        return np.cos(ang).astype(f32), np.sin(ang).astype(f32)

    ca, sa = cs(t, 32)
    ropeA = np.stack([ca, sa], axis=1)
    ropeA = ropeA.reshape(NT, 128, 2, 16).transpose(1, 0, 2, 3).copy()
    row = np.repeat(np.arange(S // 64), 64).astype(f32)
    col = np.tile(np.arange(64), S // 64).astype(f32)
    cr, sr = cs(row, 32)
    cc, sc = cs(col, 32)
    ropeG = np.stack([np.stack([cr, cc], axis=1), np.stack([sr, sc], axis=1)], axis=1)
    ropeG = ropeG.reshape(NT, 128, 2, 2, 16).transpose(1, 0, 2, 3, 4).copy()
    return ropeA, ropeG


def _t5_bucket(rel):
    nb = 16
    exact = 8
    ret = np.where(rel > 0, nb, 0)
    n = np.abs(rel)
    nf = np.maximum(n, 1).astype(np.float32)
    large = exact + (np.log(nf / np.float32(exact)) / np.float32(math.log(1024 / exact)) * np.float32(nb - exact)).astype(np.int32)
    large = np.minimum(large, nb - 1)
    return ret + np.where(n < exact, n, large)


def _dil_tables(rel_bias):
    d = 1535 - np.arange(LR)
    mult = ((np.abs(d) <= 64).astype(np.int32) + ((d % 4 == 0) & (np.abs(d) <= 256)).astype(np.int32)
            + ((d % 16 == 0) & (np.abs(d) <= 1024)).astype(np.int32))
    logm = np.where(mult > 0, np.log(np.maximum(mult, 1).astype(np.float64)), -30000.0).astype(np.float32)
    bucket = _t5_bucket(d)
    bias_g = np.ascontiguousarray(rel_bias[bucket, :].T).astype(np.float32)
    logm6 = np.ascontiguousarray(np.broadcast_to(logm[None, :], (6, LR))).astype(np.float32)
    return bias_g, logm6


from contextlib import ExitStack


CUT = [None]
DBG_NT = [NT]
NOSKEW = [False]


class _Cut(Exception):
    pass


def chk(n):
    if CUT[0] == n:
        raise _Cut()


def cp(c, eng, out, in_, reads, writes):
    if eng == "act":
        return c.op("act", lambda e: e.activation(out, in_, AF.Copy), reads=reads, writes=writes)
    return c.op(eng, lambda e: e.tensor_copy(out, in_), reads=reads, writes=writes)


def phase_consts(g, es):
    c = g.c
    g.ident = sb(es, g, "ident", [128, 128], BF16)
    g.antiI = sb(es, g, "antiI", [128, 128], BF16)
    g.b_const = Buf("const")
    c.dma("pool", g.ident, g.d_ident, g.b_const, writes=[g.b_const])
    c.dma("pool", g.antiI, g.d_antiI, g.b_const, writes=[g.b_const])
    g.mhalf = sb(es, g, "mhalf", [128, 8], F32)
    g.b_mh = Buf("mhalf")
    c.op("pool", lambda e: e.memset(g.mhalf, -0.5), writes=[g.b_mh])


def phase_x0(g):
    c = g.c
    with ExitStack() as es:
        idf = sb(es, g, "x0_idf", [128, 128], F32)
        xf = [sb(es, g, "x0_xf%d" % i, [128, D], F32) for i in range(3)]
        xts = [sb(es, g, "x0_xt%d" % i, [128, 8, TS], BF16) for i in range(2)]
        pT = [ps(es, g, "x0_pT%d" % i, [128, 8, 128], F32) for i in range(2)]
        bidf = Buf("idf")
        bxf = [Buf("xf%d" % i) for i in range(3)]
        bxt = [Buf("xt%d" % i) for i in range(2)]
        bpT = [Buf("pT%d" % i) for i in range(2)]
        c.dma("sp", idf, g.d_ident, bidf, writes=[bidf])
        xT_v = g.xT.rearrange("(kc p) n -> p kc n", p=128)
        for s in range(NS):
            for t in range(4):
                tt = s * 4 + t
                i = tt % 2
                f = tt % 3
                c.dma("sp", xf[f], g.x[tt * 128:(tt + 1) * 128, :], bxf[f], writes=[bxf[f]])
                for kc in range(8):
                    c.op("pe", lambda e: e.transpose(pT[i][:, kc, :], xf[f][:, kc * 128:(kc + 1) * 128], idf),
                         reads=[bxf[f], bidf], writes=[bpT[i]], signal=(kc == 7))
                cp(c, "dve" if tt % 2 == 0 else "act", xts[s % 2][:, :, t * 128:(t + 1) * 128], pT[i],
                   [bpT[i]], [bxt[s % 2]])
            c.dma("act", xT_v[:, :, s * TS:(s + 1) * TS], xts[s % 2], bxt[s % 2], reads=[bxt[s % 2]], writes=[g.b_scr])
        c.barrier()


def phase_w(g):
    c = g.c
    with ExitStack() as es:
        st = [sb(es, g, "w_st%d" % i, [128, 8, 512], F32) for i in range(2)]
        wb = [sb(es, g, "w_wb%d" % i, [128, 8, 512], BF16) for i in range(2)]
        bst = [Buf("wst%d" % i) for i in range(2)]
        bwb = [Buf("wwb%d" % i) for i in range(2)]
        n = 0
        for l in range(DEPTH):
            for gu, w in enumerate((g.wg, g.wu)):
                wv = w[l].rearrange("(kc p) n -> p kc n", p=128)
                for c0 in range(0, DFF, 512):
                    cw = min(512, DFF - c0)
                    i = n % 2
                    c.dma("sp", st[i][:, :, 0:cw], wv[:, :, c0:c0 + cw], bst[i], writes=[bst[i]])
                    eng = ("dve", "pool")[n % 2]
                    cp(c, eng, wb[i][:, :, 0:cw], st[i][:, :, 0:cw], [bst[i]], [bwb[i]])
                    for jj in range(cw // 128):
                        j = c0 // 128 + jj
                        c.dma("sp", g.wgu[l, gu, j], wb[i][:, :, jj * 128:(jj + 1) * 128], bwb[i],
                              reads=[bwb[i]], writes=[g.b_scr])
                    n += 1
        c.barrier()


def phase_b0(g):
    c = g.c
    with ExitStack() as es:
        a = sb(es, g, "b0_a", [6, LR], F32)
        b = sb(es, g, "b0_b", [6, LR], F32)
        o = sb(es, g, "b0_o", [6, LR], BF16)
        ba, bb, bo = Buf("b0a"), Buf("b0b"), Buf("b0o")
        c.dma("sp", a, g.d_biasg, ba, writes=[ba])
        c.dma("sp", b, g.d_logm, bb, writes=[bb])
        c.op("dve", lambda e: e.tensor_tensor(a, a, b, ALU.add), reads=[ba, bb], writes=[ba])
        c.op("dve", lambda e: e.tensor_scalar(o, a, 8.0, None, ALU.mult), reads=[ba], writes=[bo])
        c.dma("sp", g.rtab, o, bo, reads=[bo], writes=[g.b_scr])
        c.barrier()


def skew(stages, tiles, rev=False):
    n = len(tiles)
    if NOSKEW[0]:
        for t_ in tiles:
            for st in stages:
                st(t_)
        return
    order = list(enumerate(stages))
    if rev:
        order = order[::-1]
    for k in range(n + len(stages) - 1):
        for si, st in order:
            idx = k - si
            if 0 <= idx < n:
                st(tiles[idx])


def a1_weights(g, es, l):
    c = g.c
    w = G()
    w.win = sb(es, g, "a1_win", [128, 8, 416], BF16)
    wuq_s = sb(es, g, "a1_wuqs", [128, 2, 576], F32)
    wukv_s = sb(es, g, "a1_wukvs", [128, 768], F32)
    w.wuq = sb(es, g, "a1_wuq", [128, 2, 576], BF16)
    w.wukv = sb(es, g, "a1_wukv", [128, 768], BF16)
    gn = sb(es, g, "a1_gn", [128, 3], F32)
    w.ropeA = sb(es, g, "a1_rope", [128, NT, 2, 16], F32)
    w.bw, w.bwu, bwus, w.bwk, bwks, bgn, w.brope = (Buf("a1w%d" % i) for i in range(7))
    w.bufs = [w.bw, w.bwu, bwus, w.bwk, bwks, bgn, w.brope]

    def load():
        wv = g.w_in[l].rearrange("(kc p) n -> p kc n", p=128)
        for kc in range(8):
            c.dma("pool", w.win[:, kc, :], wv[:, kc, 0:416], w.bw, writes=[w.bw])
        wq = g.w_uq[l].rearrange("(kc p) (h d) -> p kc h d", p=128, d=96)
        for kc in range(2):
            for (c0, d0, dw) in ((0, 0, 64), (384, 64, 16), (480, 80, 16)):
                c.dma("sp", wuq_s[:, kc, c0:c0 + 6 * dw].rearrange("p (h d) -> p h d", d=dw), wq[:, kc, :, d0:d0 + dw],
                      bwus, writes=[bwus])
        wk = g.w_ukv[l].rearrange("p (h d) -> p h d", d=128)
        for a_ in range(2):
            c.dma("sp", wukv_s[:, a_ * 384:(a_ + 1) * 384].rearrange("p (h d) -> p h d", d=64), wk[:, :, a_ * 64:(a_ + 1) * 64],
                  bwks, writes=[bwks])
        for kc in range(2):
            c.dma("sp", gn[:, kc:kc + 1], g.qn[l][kc * 128:(kc + 1) * 128].rearrange("(p o) -> p o", o=1), bgn, writes=[bgn])
        c.dma("sp", gn[:, 2:3], g.kvn[l].rearrange("(p o) -> p o", o=1), bgn, writes=[bgn])
        c.dma("sp", w.ropeA, g.d_ropeA, w.brope, writes=[w.brope])
        for kc in range(2):
            c.op("dve", lambda e: e.tensor_scalar(w.wuq[:, kc, :], wuq_s[:, kc, :], gn[:, kc:kc + 1], None, ALU.mult),
                 reads=[bwus, bgn], writes=[w.bwu])
        c.op("dve", lambda e: e.tensor_scalar(w.wukv, wukv_s, gn[:, 2:3], None, ALU.mult), reads=[bwks, bgn], writes=[w.bwk])

    w.load = load
    return w


def phase_a1(g, l, w, after_loads=None):
    c = g.c
    with ExitStack() as es:
        win, wuq, wukv, ropeA = w.win, w.wuq, w.wukv, w.ropeA
        bw, bwu, bwk, brope = w.bw, w.bwu, w.bwk, w.brope
        xts = [sb(es, g, "a1_xt%d" % i, [128, 8, TS], BF16) for i in range(2)]
        junk = sb(es, g, "a1_junk", [128, 256], F32)
        qaTs = [sb(es, g, "a1_qaT%d" % i, [128, 6, TS], BF16) for i in range(2)]
        kaTs = [sb(es, g, "a1_kaT%d" % i, [128, 6, TS], BF16) for i in range(2)]

        def slots(name, shape, dt, n):
            return [sb(es, g, "a1_%s%d" % (name, i), shape, dt) for i in range(n)], [Buf("%s%d" % (name, i)) for i in range(n)]

        hsb, bhsb = slots("hsb", [128, 416], F32, 4)
        st, bst = slots("st", [128, 2], F32, 2)
        st2, bst2 = slots("stb", [128, 2], F32, 2)
        rstd, brstd = slots("rstd", [128, 2], F32, 2)
        kr, bkr = slots("kr", [128, 4, 16], F32, 8)
        cn, bcn = slots("cn", [128, 384], BF16, 2)
        cT, bcT = slots("cT", [128, 3, 128], BF16, 2)
        qr, bqr = slots("qr", [128, 4, 6, 16], F32, 2)
        qab, bqab = slots("qab", [128, 6, 96], BF16, 3)
        kab, bkab = slots("kab", [128, 6, 96], BF16, 5)
        vab, bvab = slots("vab", [128, 6, 64], BF16, 2)
        pH = ps(es, g, "a1_pH", [128, 512])
        pX = ps(es, g, "a1_pX", [128, 4, 512])
        pTc = ps(es, g, "a1_pTc", [128, 8, 128], BF16)
        pTq = ps(es, g, "a1_pTq", [128, 8, 128], BF16)
        pTk = ps(es, g, "a1_pTk", [128, 8, 128], BF16)
        B = lambda n: Buf(n)
        bxt = [B("xt0"), B("xt1")]
        bjunk, bpH, bpTc, bpTq, bpTk = B("junk"), B("pH"), B("pTc"), B("pTq"), B("pTk")
        bqaT, bkaT = [B("qaT0"), B("qaT1")], [B("kaT0"), B("kaT1")]
        bqa, bkv, bqr2, bvv = B("pqa"), B("pkv"), B("pqr"), B("pvv")
        xT_v = g.xT.rearrange("(kc p) n -> p kc n", p=128)
        qaT_v = g.qaT.rearrange("h p n -> p h n")
        kaT_v = g.kaT.rearrange("h p n -> p h n")
        va_v = g.va.rearrange("h p t d -> p t h d")
        v6 = lambda ap, d: ap.rearrange("p (h d) -> p h d", d=d)

        def U1(tt):
            s, t = tt // 4, tt % 4
            si = s % 2
            xs = xts[si]
            if tt == 0:
                c.dma("sp", xs, xT_v[:, :, 0:TS], bxt[0], writes=[bxt[0]])
            if t == 1 and s + 1 < NS:
                c.dma("sp", xts[1 - si], xT_v[:, :, (s + 1) * TS:(s + 2) * TS], bxt[1 - si], writes=[bxt[1 - si]])
            if tt == min(6, DBG_NT[0] - 1) and after_loads is not None:
                after_loads()
            for kc in range(8):
                c.op("pe", lambda e: e.matmul(pH[:, 0:416], xs[:, kc, t * 128:(t + 1) * 128], win[:, kc, :],
                                              start=(kc == 0), stop=(kc == 7)),
                     reads=[bxt[si], bw], writes=[bpH], signal=(kc == 7))

        def U2(tt):
            h_, bh_ = hsb[tt % 4], bhsb[tt % 4]
            s_, bs_ = st[tt % 2], bst[tt % 2]
            cp(c, "act", h_, pH[:, 0:416], [bpH], [bh_])
            c.op("act", lambda e: e.activation(junk, pH[:, 0:256], AF.Square, scale=1.0 / 16.0, accum_out=s_[:, 0:1]),
                 reads=[bpH], writes=[bjunk, bs_])
            c.op("act", lambda e: e.activation(junk[:, 0:128], pH[:, 256:384], AF.Square,
                                               scale=1.0 / math.sqrt(128.0), accum_out=s_[:, 1:2]),
                 reads=[bpH], writes=[bjunk, bs_])

        def U3(tt):
            h_, bh_ = hsb[tt % 4], bhsb[tt % 4]
            c.op("dve", lambda e: e.tensor_scalar(st2[tt % 2], st[tt % 2], 1e-6, None, ALU.add), reads=[bst[tt % 2]], writes=[bst2[tt % 2]])
            cos, sin = ropeA[:, tt, 0, :], ropeA[:, tt, 1, :]
            x1, x2 = h_[:, 384:400], h_[:, 400:416]
            for q_, (xa, tb) in enumerate(((x1, cos), (x2, sin), (x1, sin), (x2, cos))):
                c.op("dve", lambda e: e.tensor_tensor(kr[tt % 8][:, q_, :], xa, tb, ALU.mult),
                     reads=[bh_, brope], writes=[bkr[tt % 8]])

        def U4(tt):
            c.op("pool", lambda e: e.tensor_tensor(rstd[tt % 2], st2[tt % 2], g.mhalf[:, 0:2], ALU.pow),
                 reads=[bst2[tt % 2], g.b_mh], writes=[brstd[tt % 2]])

        def U5(tt):
            h_, bh_ = hsb[tt % 4], bhsb[tt % 4]
            r_, br_ = rstd[tt % 2], brstd[tt % 2]
            c.op("act", lambda e: e.activation(cn[tt % 2][:, 0:256], h_[:, 0:256], AF.Copy, scale=r_[:, 0:1]),
                 reads=[bh_, br_], writes=[bcn[tt % 2]])
            c.op("dve", lambda e: e.tensor_scalar(cn[tt % 2][:, 256:384], h_[:, 256:384], r_[:, 1:2], None, ALU.mult),
                 reads=[bh_, br_], writes=[bcn[tt % 2]])

        def U6(tt):
            for b_ in range(3):
                c.op("pe", lambda e: e.transpose(pTc[:, b_, :], cn[tt % 2][:, b_ * 128:(b_ + 1) * 128], g.ident),
                     reads=[bcn[tt % 2], g.b_const], writes=[bpTc], signal=(b_ == 2))

        def U7(tt):
            cp(c, "act", cT[tt % 2], pTc[:, 0:3, :], [bpTc], [bcT[tt % 2]])

        def U8(tt):
            ct_, bct_ = cT[tt % 2], bcT[tt % 2]
            for kc in range(2):
                c.op("pe", lambda e: e.matmul(pX[:, 0, 0:384], ct_[:, kc, :], wuq[:, kc, 0:384], start=(kc == 0), stop=(kc == 1)),
                     reads=[bct_, bwu], writes=[bqa], signal=(kc == 1))
            for kc in range(2):
                c.op("pe", lambda e: e.matmul(pX[:, 1, 0:192], ct_[:, kc, :], wuq[:, kc, 384:576], start=(kc == 0), stop=(kc == 1)),
                     reads=[bct_, bwu], writes=[bqr2], signal=(kc == 1))
            c.op("pe", lambda e: e.matmul(pX[:, 2, 0:384], ct_[:, 2, :], wukv[:, 0:384], start=True, stop=True),
                 reads=[bct_, bwk], writes=[bkv])
            c.op("pe", lambda e: e.matmul(pX[:, 3, 0:384], ct_[:, 2, :], wukv[:, 384:768], start=True, stop=True),
                 reads=[bct_, bwk], writes=[bvv])

        def U9(tt):
            cos, sin = ropeA[:, tt, 0, :], ropeA[:, tt, 1, :]
            qa_, bqa_ = qab[tt % 3], bqab[tt % 3]
            ka_, bka_ = kab[tt % 5], bkab[tt % 5]
            c.op("act", lambda e: e.activation(qa_[:, :, 0:64], v6(pX[:, 0, 0:384], 64), AF.Copy), reads=[bqa], writes=[bqa_])
            cosb = cos.unsqueeze(1).to_broadcast([128, 6, 16])
            sinb = sin.unsqueeze(1).to_broadcast([128, 6, 16])
            qx1 = v6(pX[:, 1, 0:96], 16)
            qx2 = v6(pX[:, 1, 96:192], 16)
            for q_, (xa, tb) in enumerate(((qx1, cosb), (qx2, sinb), (qx1, sinb), (qx2, cosb))):
                c.op("dve", lambda e: e.tensor_tensor(qr[tt % 2][:, q_], xa, tb, ALU.mult), reads=[bqr2, brope], writes=[bqr[tt % 2]])
            c.op("act", lambda e: e.activation(ka_[:, :, 0:64], v6(pX[:, 2, 0:384], 64), AF.Copy), reads=[bkv], writes=[bka_])
            c.op("dve", lambda e: e.tensor_copy(vab[tt % 2], v6(pX[:, 3, 0:384], 64)), reads=[bvv], writes=[bvab[tt % 2]])
            c.dma("sp", va_v[:, tt], vab[tt % 2], bvab[tt % 2], reads=[bvab[tt % 2]], writes=[g.b_scr])

        def U10(tt):
            qa_, bqa_ = qab[tt % 3], bqab[tt % 3]
            ka_, bka_ = kab[tt % 5], bkab[tt % 5]
            q_ = qr[tt % 2]
            c.op("pool", lambda e: e.tensor_tensor(qa_[:, :, 64:80], q_[:, 0], q_[:, 1], ALU.subtract), reads=[bqr[tt % 2]], writes=[bqa_])
            c.op("pool", lambda e: e.tensor_tensor(qa_[:, :, 80:96], q_[:, 2], q_[:, 3], ALU.add), reads=[bqr[tt % 2]], writes=[bqa_])
            krb = lambda j_: kr[tt % 8][:, j_, :].unsqueeze(1).to_broadcast([128, 6, 16])
            c.op("pool", lambda e: e.tensor_tensor(ka_[:, :, 64:80], krb(0), krb(1), ALU.subtract), reads=[bkr[tt % 8]], writes=[bka_])
            c.op("pool", lambda e: e.tensor_tensor(ka_[:, :, 80:96], krb(2), krb(3), ALU.add), reads=[bkr[tt % 8]], writes=[bka_])

        def U11(tt):
            for h in range(6):
                c.op("pe", lambda e: e.transpose(pTq[0:96, h, :], qab[tt % 3][:, h, :], g.ident),
                     reads=[bqab[tt % 3], g.b_const], writes=[bpTq], signal=(h == 5))

        def U12(tt):
            s, t = tt // 4, tt % 4
            cp(c, "dve", qaTs[s % 2][0:96, :, t * 128:(t + 1) * 128], pTq[0:96, 0:6, :], [bpTq], [bqaT[s % 2]])

        def U13(tt):
            for h in range(6):
                c.op("pe", lambda e: e.transpose(pTk[0:96, h, :], kab[tt % 5][:, h, :], g.ident),
                     reads=[bkab[tt % 5], g.b_const], writes=[bpTk], signal=(h == 5))

        def U14(tt):
            s, t = tt // 4, tt % 4
            si = s % 2
            cp(c, "act", kaTs[si][0:96, :, t * 128:(t + 1) * 128], pTk[0:96, 0:6, :], [bpTk], [bkaT[si]])
            if t == 3:
                c.dma("sp", qaT_v[:, :, s * TS:(s + 1) * TS], qaTs[si][0:96], bqaT[si], reads=[bqaT[si]], writes=[g.b_scr])
                c.dma("sp", kaT_v[:, :, s * TS:(s + 1) * TS], kaTs[si][0:96], bkaT[si], reads=[bkaT[si]], writes=[g.b_scr])

        skew([U1, U2, U3, U4, U5, U6, U7, U8, U9, U10, U11, U12, U13, U14], list(range(DBG_NT[0])), rev=True)
        c.barrier()


def a2_weights(g, es, l):
    c = g.c
    W0 = 416
    aw = G()
    aw.win = sb(es, g, "a2_win", [128, 8, 1664], BF16)
    aw.ggain = sb(es, g, "a2_gg", [128, 384], F32)
    aw.ropeG = sb(es, g, "a2_rope", [128, NT, 2, 2, 16], F32)
    aw.bw, aw.bgg, aw.brope = Buf("a2win"), Buf("a2gg"), Buf("a2rope")

    def load():
        wv = g.w_in[l].rearrange("(kc p) n -> p kc n", p=128)
        for kc in range(8):
            c.dma("pool", aw.win[:, kc, 0:832], wv[:, kc, W0:W0 + 832], aw.bw, writes=[aw.bw])
            c.dma("pool", aw.win[:, kc, 832:1664], wv[:, kc, W0 + 832:W0 + 1664], aw.bw, writes=[aw.bw])
        c.dma("sp", aw.ggain, g.d_ggain[l].partition_broadcast(128), aw.bgg, writes=[aw.bgg])
        c.dma("sp", aw.ropeG, g.d_ropeG, aw.brope, writes=[aw.brope])

    aw.load = load
    aw.bufs = [aw.bw, aw.bgg, aw.brope]
    return aw


def phase_a2(g, l, aw):
    c = g.c
    with ExitStack() as es:
        W0 = 416
        win, ggain, ropeG = aw.win, aw.ggain, aw.ropeG
        xts = [sb(es, g, "a2_xt%d" % i, [128, 8, TS], BF16) for i in range(2)]
        fT = [sb(es, g, "a2_fT%d" % i, [128, 6, TS], BF16) for i in range(2)]
        gTs = [sb(es, g, "a2_gT%d" % i, [128, 4, TS], BF16) for i in range(2)]

        def slots(name, shape, dt, n):
            return [sb(es, g, "a2_%s%d" % (name, i), shape, dt) for i in range(n)], [Buf("%s%d" % (name, i)) for i in range(n)]

        dvb, bdvb = slots("dvb", [128, 6, 64], BF16, 2)
        gvb, bgvb = slots("gvb", [128, 2, 64], BF16, 2)
        gsb, bgsb = slots("gsb", [128, 384], F32, 5)
        sq, bsq = slots("sq", [128, 384], F32, 3)
        ms, bms = slots("ms", [128, 6], F32, 2)
        ms2, bms2 = slots("msb", [128, 6], F32, 3)
        rstd, brstd = slots("rstd", [128, 6], F32, 3)
        gnt, bgn = slots("gn", [128, 384], F32, 4)
        tmp, btmp = slots("tmp", [128, 4, 192], F32, 3)
        gb, bgb = slots("gb", [128, 8, 64], BF16, 3)
        pF = [ps(es, g, "a2_pF%d" % i, [128, 512]) for i in range(2)]
        pDV = [ps(es, g, "a2_pDV%d" % i, [128, 512]) for i in range(2)]
        pG = [ps(es, g, "a2_pG%d" % i, [128, 512]) for i in range(2)]
        pT = [ps(es, g, "a2_pT%d" % i, [128, 8, 128], BF16) for i in range(2)]
        B = lambda n: Buf(n)
        bw, bgg, brope = aw.bw, aw.bgg, aw.brope
        (bxt, bfT, bgT, bpF, bpDV, bpG, bpT) = ([B("z%d" % i) for i in range(2)] for _ in range(7))
        xT_v = g.xT.rearrange("(kc p) n -> p kc n", p=128)
        dqT_v = g.dqT.rearrange("j p n -> p j n")
        dkT_v = g.dkT.rearrange("j p n -> p j n")
        gqT_v = g.gqT.rearrange("j p n -> p j n")
        gkT_v = g.gkT.rearrange("j p n -> p j n")
        dv_v = g.dv.rearrange("h p t d -> p t h d")
        gv_v = g.gv.rearrange("h p t d -> p t h d")
        nf = [0]
        hd = lambda ap: ap.rearrange("p (h d) -> p h d", d=64)

        def T1(tt):
            s, t, i = tt // 4, tt % 4, tt % 2
            si = s % 2
            xs = xts[si]
            if tt == 0:
                c.dma("sp", xs, xT_v[:, :, 0:TS], bxt[0], writes=[bxt[0]])
            if t == 1 and s + 1 < NS:
                c.dma("sp", xts[1 - si], xT_v[:, :, (s + 1) * TS:(s + 2) * TS], bxt[1 - si], writes=[bxt[1 - si]])
            for gi in ((0, 1), (2, 3), (4,), (5,))[t]:
                cb = (C_DQ - W0) + gi * 128
                fi = nf[0] % 2
                nf[0] += 1
                for kc in range(8):
                    c.op("pe", lambda e: e.matmul(pF[fi], win[:, kc, cb:cb + 128], xs[:, kc, :], start=(kc == 0), stop=(kc == 7)),
                         reads=[bxt[si], bw], writes=[bpF[fi]], signal=(kc == 7))
                cp(c, "act" if gi % 2 == 0 else "dve", fT[si][:, gi, :], pF[fi], [bpF[fi]], [bfT[si]])
            if t == 3:
                c.dma("sp", dqT_v[:, :, s * TS:(s + 1) * TS], fT[si][:, 0:3, :], bfT[si], reads=[bfT[si]], writes=[g.b_scr])
                c.dma("sp", dkT_v[:, :, s * TS:(s + 1) * TS], fT[si][:, 3:6, :], bfT[si], reads=[bfT[si]], writes=[g.b_scr])
            lhs = lambda kc: xs[:, kc, t * 128:(t + 1) * 128]
            for kc in range(8):
                c.op("pe", lambda e: e.matmul(pDV[i][:, 0:384], lhs(kc), win[:, kc, C_DV - W0:C_DV - W0 + 384],
                                              start=(kc == 0), stop=(kc == 7)),
                     reads=[bxt[si], bw], writes=[bpDV[i]], signal=(kc == 7))
            for kc in range(8):
                c.op("pe", lambda e: e.matmul(pG[i], lhs(kc), win[:, kc, C_GQ - W0:C_GQ - W0 + 512],
                                              start=(kc == 0), stop=(kc == 7)),
                     reads=[bxt[si], bw], writes=[bpG[i]], signal=(kc == 7))

        def T2(tt):
            i = tt % 2
            cp(c, "act", gsb[tt % 5], pG[i][:, 0:384], [bpG[i]], [bgsb[tt % 5]])
            c.op("act", lambda e: e.activation(sq[tt % 3], pG[i][:, 0:384], AF.Square, scale=0.125), reads=[bpG[i]], writes=[bsq[tt % 3]])
            cp(c, "act", gvb[i].rearrange("p h d -> p (h d)"), pG[i][:, 384:512], [bpG[i]], [bgvb[i]])
            c.dma("sp", gv_v[:, tt], gvb[i], bgvb[i], reads=[bgvb[i]], writes=[g.b_scr])
            cp(c, "act", dvb[i].rearrange("p h d -> p (h d)"), pDV[i][:, 0:384], [bpDV[i]], [bdvb[i]])
            c.dma("sp", dv_v[:, tt], dvb[i], bdvb[i], reads=[bdvb[i]], writes=[g.b_scr])

        def T3(tt):
            c.op("dve", lambda e: e.tensor_reduce(ms[tt % 2], hd(sq[tt % 3]), AX.X, ALU.add), reads=[bsq[tt % 3]], writes=[bms[tt % 2]])
            c.op("dve", lambda e: e.tensor_scalar(ms2[tt % 3], ms[tt % 2], 1e-6, None, ALU.add), reads=[bms[tt % 2]], writes=[bms2[tt % 3]])

        def T4(tt):
            c.op("pool", lambda e: e.tensor_tensor(rstd[tt % 3], ms2[tt % 3], g.mhalf[:, 0:6], ALU.pow),
                 reads=[bms2[tt % 3], g.b_mh], writes=[brstd[tt % 3]])

        def T5(tt):
            c.op("dve", lambda e: e.tensor_tensor(hd(gnt[tt % 4]), hd(gsb[tt % 5]),
                                                  rstd[tt % 3].unsqueeze(2).to_broadcast([128, 6, 64]), ALU.mult),
                 reads=[bgsb[tt % 5], brstd[tt % 3]], writes=[bgn[tt % 4]])

        def T6(tt):
            c.op("dve", lambda e: e.tensor_tensor(gnt[tt % 4], gnt[tt % 4], ggain, ALU.mult), reads=[bgn[tt % 4], bgg], writes=[bgn[tt % 4]])

        def T7(tt):
            v5 = gnt[tt % 4].rearrange("p (h r x d) -> p h r x d", h=6, r=2, x=2)
            x1 = v5[:, :, :, 0, :]
            x2 = v5[:, :, :, 1, :]
            cosb = ropeG[:, tt, 0].unsqueeze(1).to_broadcast([128, 6, 2, 16])
            sinb = ropeG[:, tt, 1].unsqueeze(1).to_broadcast([128, 6, 2, 16])
            tv = tmp[tt % 3].rearrange("p q (h r d) -> p q h r d", h=6, r=2)
            for q_, (xa, tb) in enumerate(((x1, cosb), (x2, sinb), (x1, sinb), (x2, cosb))):
                c.op("dve" if q_ < 2 else "pool", lambda e: e.tensor_tensor(tv[:, q_], xa, tb, ALU.mult),
                     reads=[bgn[tt % 4], brope], writes=[btmp[tt % 3]])

        def T8(tt):
            tv = tmp[tt % 3].rearrange("p q (h r d) -> p q h r d", h=6, r=2)
            gbt, bg_ = gb[tt % 3], bgb[tt % 3]
            gq5 = gbt.rearrange("p h (r x d) -> p h r x d", r=2, x=2)
            gk6 = gbt.rearrange("p (a b) (r x d) -> p a b r x d", b=2, r=2, x=2)
            c.op("pool", lambda e: e.tensor_tensor(gq5[:, 0:4, :, 0, :], tv[:, 0, 0:4], tv[:, 1, 0:4], ALU.subtract),
                 reads=[btmp[tt % 3]], writes=[bg_])
            c.op("pool", lambda e: e.tensor_tensor(gq5[:, 0:4, :, 1, :], tv[:, 2, 0:4], tv[:, 3, 0:4], ALU.add),
                 reads=[btmp[tt % 3]], writes=[bg_])
            for b_ in range(2):
                c.op("dve", lambda e: e.tensor_tensor(gk6[:, 2:4, b_, :, 0, :], tv[:, 0, 4:6], tv[:, 1, 4:6], ALU.subtract),
                     reads=[btmp[tt % 3]], writes=[bg_])
                c.op("dve", lambda e: e.tensor_tensor(gk6[:, 2:4, b_, :, 1, :], tv[:, 2, 4:6], tv[:, 3, 4:6], ALU.add),
                     reads=[btmp[tt % 3]], writes=[bg_])

        def T9(tt):
            i = tt % 2
            for b_ in range(4):
                c.op("pe", lambda e: e.transpose(pT[i][:, b_, :], gb[tt % 3][:, 2 * b_:2 * b_ + 2, :].rearrange("p h d -> p (h d)"), g.ident),
                     reads=[bgb[tt % 3], g.b_const], writes=[bpT[i]], signal=(b_ == 3))

        def T10(tt):
            s, t, i = tt // 4, tt % 4, tt % 2
            si = s % 2
            cp(c, "act", gTs[si][:, :, t * 128:(t + 1) * 128], pT[i][:, 0:4, :], [bpT[i]], [bgT[si]])
            if t == 3:
                c.dma("sp", gqT_v[:, :, s * TS:(s + 1) * TS], gTs[si][:, 0:2, :], bgT[si], reads=[bgT[si]], writes=[g.b_scr])
                c.dma("sp", gkT_v[:, :, s * TS:(s + 1) * TS], gTs[si][:, 2:4, :], bgT[si], reads=[bgT[si]], writes=[g.b_scr])

        skew([T1, T2, T3, T4, T5, T6, T7, T8, T9, T10], list(range(DBG_NT[0])))
        c.barrier()


def phase_attn(g, name, groups, scale, band, side=None):
    c = g.c
    LAG = 2
    with ExitStack() as es:
        qt = [sb(es, g, name + "_q%d" % i, [128, S], BF16) for i in range(4)]
        kt = [sb(es, g, name + "_k%d" % i, [128, S], BF16) for i in range(2)]
        vt = [sb(es, g, name + "_v%d" % i, [128, NT, 128], BF16) for i in range(4)]
        ot = [sb(es, g, name + "_o%d" % i, [64, S], BF16) for i in range(2)]
        pt = [sb(es, g, name + "_p%d" % i, [128, 512], BF16) for i in range(4)]
        rc = [sb(es, g, name + "_r%d" % i, [128, 512], F32) for i in range(2)]
        pS = [ps(es, g, name + "_pS%d" % i, [128, 512]) for i in range(4)]
        pO = [ps(es, g, name + "_pO%d" % i, [128, 512]) for i in range(2)]
        B = lambda n: Buf(n)
        bq, bk = [B("q%d" % i) for i in range(4)], [B("k0"), B("k1")]
        bv = [B("v%d" % i) for i in range(4)]
        bot, brc, bpO = ([B("o%d" % i) for i in range(2)] for _ in range(3))
        bpt, bpS = ([B("p%d" % i) for i in range(4)] for _ in range(2))
        btall = B("tall")
        if band:
            tall = sb(es, g, name + "_tall", [128, 6, TW], BF16)
            for h in range(6):
                src = bass.AP(g.rtab.tensor, h * LR, [[1, 128], [1, TW]])
                c.dma("sp", tall[:, h, :], src, btall, reads=[g.b_scr], writes=[btall])
        for i in range(4):
            c.op("pool", lambda e: e.memset(vt[i][:, :, 64:128], 1.0), writes=[bv[i]])
        padded = groups[0]["heads"][0]["K"] == 64
        if padded:
            for i in range(4):
                z0 = 64 * (1 - (i % 2))
                c.op("pool", lambda e: e.memset(qt[i][z0:z0 + 64, :], 0.0), writes=[bq[i]])

        def load_group(gi):
            gr = groups[gi]
            sl = gi % 2
            kp = gr["kp"]
            c.dma("sp", kt[sl][0:kp, :], gr["k"], bk[sl], reads=[g.b_scr], writes=[bk[sl]])
            for hi, hd in enumerate(gr["heads"]):
                vs = sl * 2 + hi
                p0_, K_ = hd["p0"], hd["K"]
                c.dma("sp", qt[vs][p0_:p0_ + K_, :], gr["q"][p0_:p0_ + K_, :], bq[vs], reads=[g.b_scr], writes=[bq[vs]])
                c.dma("sp", vt[vs][:, :, 0:64], hd["v"], bv[vs], reads=[g.b_scr], writes=[bv[vs]])

        flat = []
        hcount = 0
        for gi, gr in enumerate(groups):
            for hi, hd in enumerate(gr["heads"]):
                for cq in range(NS):
                    if band:
                        kbs = [kb for kb in range(4 * cq - 8, 4 * cq + 12) if 0 <= kb < NT]
                    else:
                        kbs = list(range(NT))
                    for ii, kb in enumerate(kbs):
                        flat.append((gi, hi, hcount, cq, ii, kb, len(kbs)))
                hcount += 1
        sidegen = side(es) if side is not None else None
        load_group(0)
        gstart = {}
        for idx, stp in enumerate(flat):
            gstart.setdefault(stp[0], idx)
        nchunk = 0
        for idx in range(len(flat) + LAG):
            if sidegen is not None and idx % 96 == 48:
                next(sidegen, None)
            if idx < len(flat):
                gi, hi, hc, cq, ii, kb, nkb = flat[idx]
                if idx - gstart[gi] == LAG + 1 and gi + 1 < len(groups):
                    load_group(gi + 1)
                gr = groups[gi]
                hd = gr["heads"][hi]
                sl = gi % 2
                p0, K = hd["p0"], hd["K"]
                sk = idx % 4
                kr_ = 128 if padded else K
                qs_ = sl * 2 + hi
                c.op("pe", lambda e: e.matmul(pS[sk], kt[sl][0:kr_, kb * 128:(kb + 1) * 128],
                                              qt[qs_][0:kr_, cq * TS:(cq + 1) * TS], start=True, stop=(not band)),
                     reads=[bk[sl], bq[qs_]], writes=[bpS[sk]], signal=(not band))
                if band:
                    off = 128 * (11 - (kb - 4 * cq))
                    c.op("pe", lambda e: e.matmul(pS[sk], g.antiI, tall[:, hd["tall"], off:off + TS], start=False, stop=True),
                         reads=[btall, g.b_const], writes=[bpS[sk]])
                c.op("act", lambda e: e.activation(pt[sk], pS[sk], AF.Exp, scale=scale), reads=[bpS[sk]], writes=[bpt[sk]])
            j = idx - LAG
            if j >= 0:
                gi, hi, hc, cq, ii, kb, nkb = flat[j]
                hd = groups[gi]["heads"][hi]
                sk = j % 4
                vs = (gi % 2) * 2 + hi
                if ii == 0:
                    osl = nchunk % 2
                    nchunk += 1
                c.op("pe", lambda e: e.matmul(pO[osl], vt[vs][:, kb, :], pt[sk], start=(ii == 0), stop=(ii == nkb - 1)),
                     reads=[bv[vs], bpt[sk]], writes=[bpO[osl]], signal=(ii == nkb - 1))
                if ii == nkb - 1:
                    hs = hc % 2
                    c.op("dve", lambda e: e.reciprocal(rc[osl][64:128, :], pO[osl][64:128, :]), reads=[bpO[osl]], writes=[brc[osl]])
                    c.op("dve", lambda e: e.tensor_tensor(ot[hs][:, cq * TS:(cq + 1) * TS], pO[osl][0:64, :], rc[osl][64:128, :], ALU.mult),
                         reads=[bpO[osl], brc[osl]], writes=[bot[hs]])
                    if cq == NS - 1:
                        r0 = hd["row0"]
                        c.dma("pool", g.mixT[r0:r0 + 64, :], ot[hs], bot[hs], reads=[bot[hs]], writes=[g.b_scr])
        if sidegen is not None:
            for _ in sidegen:
                pass
        c.barrier()


def layer_norm(c, y, by, out, bout, lnp, blnp, gi, sc, bsc):
    st, mv, rs = sc
    for h in range(2):
        c.op("dve", lambda e: e.bn_stats(st[:, h, :], y[:, h * 512:(h + 1) * 512]), reads=[by], writes=[bsc])
    c.op("dve", lambda e: e.bn_aggr(mv, st.rearrange("p a b -> p (a b)")), reads=[bsc], writes=[bsc])
    c.op("dve", lambda e: e.tensor_scalar(rs[:, 0:1], mv[:, 1:2], 1e-5, None, ALU.add), reads=[bsc], writes=[bsc])
    c.op("act", lambda e: e.activation(rs[:, 1:2], rs[:, 0:1], AF.Sqrt), reads=[bsc], writes=[bsc])
    c.op("dve", lambda e: e.reciprocal(rs[:, 0:1], rs[:, 1:2]), reads=[bsc], writes=[bsc])
    c.op("dve", lambda e: e.tensor_scalar(out, y, mv[:, 0:1], rs[:, 0:1], ALU.subtract, ALU.mult),
         reads=[by, bsc], writes=[bout])
    c.op("pool", lambda e: e.tensor_tensor(out, out, lnp[:, gi, :], ALU.mult), reads=[bout, blnp], writes=[bout])
    c.op("pool", lambda e: e.tensor_tensor(out, out, lnp[:, gi + 1, :], ALU.add), reads=[bout, blnp], writes=[bout])


def c_weights(g, es, l):
    c = g.c
    cw = G()
    cw.wout = sb(es, g, "c_wout", [128, 8, D], BF16)
    cw.wd = sb(es, g, "c_wd", [128, NJ, D], BF16)
    cw.lnp = sb(es, g, "c_lnp", [128, 4, D], F32)
    cw.bwout, cw.bwd, cw.blnp = Buf("cwout"), Buf("cwd"), Buf("clnp")
    def loader(_es=None):
        wov = g.w_out[l].rearrange("(kc p) n -> p kc n", p=128)
        for kc in range(8):
            c.dma("pool", cw.wout[:, kc, :], wov[:, kc, :], cw.bwout, writes=[cw.bwout])
            if kc % 2 == 1:
                yield
        wdv = g.wd[l].rearrange("(j p) n -> p j n", p=128)
        for j in range(NJ):
            c.dma("pool", cw.wd[:, j, :], wdv[:, j, :], cw.bwd, writes=[cw.bwd])
            if j % 2 == 1:
                yield
        for i_, v_ in enumerate((g.ln1g, g.ln1b, g.ln2g, g.ln2b)):
            c.dma("sp", cw.lnp[:, i_, :], v_[l].partition_broadcast(128), cw.blnp, writes=[cw.blnp])
        yield

    cw.loader = loader
    g.c.persist += [cw.bwout, cw.bwd, cw.blnp]
    return cw


def phase_c(g, l, last, cw):
    c = g.c
    xres = g.x if l == 0 else g.xres1
    dst = g.out if last else g.xres1
    wout, wd, lnp = cw.wout, cw.wd, cw.lnp
    bwout, bwd, blnp = cw.bwout, cw.bwd, cw.blnp
    with ExitStack() as es:
        NWGU = 3
        mx = sb(es, g, "c_mx", [128, 8, TS], BF16)
        xr = [sb(es, g, "c_xr%d" % i, [128, D], F32) for i in range(4)]
        x1s = [sb(es, g, "c_x1s%d" % i, [128, 4, D], F32) for i in range(2)]
        xb1 = [sb(es, g, "c_xb1%d" % i, [128, D], BF16) for i in range(4)]
        o = [sb(es, g, "c_o%d" % i, [128, D], F32) for i in range(2)]
        x1T = sb(es, g, "c_x1T", [128, 8, TS], BF16)
        aT = sb(es, g, "c_aT", [128, NJ, TS], BF16)
        wgu = [sb(es, g, "c_wgu%d" % i, [128, 2, 8, 128], BF16) for i in range(NWGU)]
        sg = [sb(es, g, "c_sg%d" % i, [128, TS], F32) for i in range(2)]
        sc1 = [(sb(es, g, "c_st%d" % i, [128, 2, 6], F32), sb(es, g, "c_mv%d" % i, [128, 2], F32),
                sb(es, g, "c_rs%d" % i, [128, 2], F32)) for i in range(4)]
        sc3 = [(sb(es, g, "c_st3%d" % i, [128, 2, 6], F32), sb(es, g, "c_mv3%d" % i, [128, 2], F32),
                sb(es, g, "c_rs3%d" % i, [128, 2], F32)) for i in range(2)]
        if not last:
            xb3 = [sb(es, g, "c_xb3%d" % i, [128, D], BF16) for i in range(2)]
            xTn = sb(es, g, "c_xTn", [128, 8, TS], BF16)
        PP = [ps(es, g, "c_PP%d" % i, [128, 2, 512]) for i in range(3)]
        pT = [ps(es, g, "c_pT%d" % i, [128, 8, 128], BF16) for i in range(2)]
        B = lambda n: Buf(n)
        bmx, bx1T, baT, bxTn = (B("c%d" % i) for i in range(4))
        bx1s = [[B("x1s%d_%d" % (a_, i)) for i in range(4)] for a_ in range(2)]
        bxr, bxb1, bxb3, bsc1 = ([B("d%d" % i) for i in range(4)] for _ in range(4))
        bo, bod, bsg, bpT, bsc3 = ([B("e%d" % i) for i in range(2)] for _ in range(5))
        bwgu = [B("wgu%d" % i) for i in range(NWGU)]
        bPP = [B("PP%d" % i) for i in range(3)]
        mixT_v = g.mixT.rearrange("(kc p) n -> p kc n", p=128)
        xT_v = g.xT.rearrange("(kc p) n -> p kc n", p=128)
        cnt = dict(pp=0, tp=0, nj=0)

        def nxt(k, n):
            v = cnt[k] % n
            cnt[k] += 1
            return v

        def transposes(src_b, bsrc, dstT, bdst, t):
            ti = nxt("tp", 2)
            for kc in range(8):
                c.op("pe", lambda e: e.transpose(pT[ti][:, kc, :], src_b[:, kc * 128:(kc + 1) * 128], g.ident),
                     reads=[bsrc, g.b_const], writes=[bpT[ti]], signal=(kc == 7))
            cp(c, "act", dstT[:, :, t * 128:(t + 1) * 128], pT[ti], [bpT[ti]], [bdst])

        def ln_stats(yv, byv, scr, bscr):
            st, mv, rs = scr
            for h in range(2):
                c.op("dve", lambda e: e.bn_stats(st[:, h, :], yv[:, h * 512:(h + 1) * 512]), reads=[byv], writes=[bscr])
            c.op("dve", lambda e: e.bn_aggr(mv, st.rearrange("p a b -> p (a b)")), reads=[bscr], writes=[bscr])
            c.op("dve", lambda e: e.tensor_scalar(rs[:, 1:2], mv[:, 1:2], 1e-5, None, ALU.add), reads=[bscr], writes=[bscr])
            c.op("pool", lambda e: e.tensor_tensor(rs[:, 0:1], rs[:, 1:2], g.mhalf[:, 0:1], ALU.pow), reads=[bscr, g.b_mh], writes=[bscr])

        def ln_apply(yv, byv, scr, bscr, out, bout, gi):
            st, mv, rs = scr
            c.op("dve", lambda e: e.tensor_scalar(out, yv, mv[:, 0:1], rs[:, 0:1], ALU.subtract, ALU.mult),
                 reads=[byv, bscr], writes=[bout])
            c.op("pool", lambda e: e.tensor_tensor(out, out, lnp[:, gi, :], ALU.mult), reads=[bout, blnp], writes=[bout])
            c.op("pool", lambda e: e.tensor_tensor(out, out, lnp[:, gi + 1, :], ALU.add), reads=[bout, blnp], writes=[bout])

        def load_mx(s1):
            c.dma("sp", mx, mixT_v[:, :, s1 * TS:(s1 + 1) * TS], bmx, reads=[g.b_scr], writes=[bmx])

        def P1(s1, t):
            tt = s1 * 4 + t
            pp = nxt("pp", 3)
            c.dma("sp", xr[t], xres[tt * 128:(tt + 1) * 128, :], bxr[t], reads=[g.b_scr], writes=[bxr[t]])
            for hf in range(2):
                for kc in range(8):
                    c.op("pe", lambda e: e.matmul(PP[pp][:, hf, :], mx[:, kc, t * 128:(t + 1) * 128], wout[:, kc, hf * 512:(hf + 1) * 512],
                                                  start=(kc == 0), stop=(kc == 7)),
                         reads=[bmx, bwout], writes=[bPP[pp]], signal=(kc == 7))
            c.op("dve", lambda e: e.scalar_tensor_tensor(xr[t], xr[t], DN_ALPHA, PP[pp].rearrange("p a n -> p (a n)"), ALU.mult, ALU.add),
                 reads=[bxr[t], bPP[pp]], writes=[bxr[t]])

        def P2(s1, t):
            ln_stats(xr[t], bxr[t], sc1[t], bsc1[t])

        def P3(s1, t):
            a_ = s1 % 2
            ln_apply(xr[t], bxr[t], sc1[t], bsc1[t], x1s[a_][:, t, :], bx1s[a_][t], 0)
            cp(c, "act", xb1[t], x1s[a_][:, t, :], [bx1s[a_][t]], [bxb1[t]])

        def P4(s1, t):
            transposes(xb1[t], bxb1[t], x1T, bx1T, t)

        def Q1(s, t):
            a_ = s % 2
            pp = nxt("pp", 3)
            for hf in range(2):
                for j in range(NJ):
                    c.op("pe", lambda e: e.matmul(PP[pp][:, hf, :], aT[:, j, t * 128:(t + 1) * 128], wd[:, j, hf * 512:(hf + 1) * 512],
                                                  start=(j == 0), stop=(j == NJ - 1)),
                         reads=[baT, bwd], writes=[bPP[pp]], signal=(j == NJ - 1))
            xv = x1s[a_][:, t, :]
            c.op("dve", lambda e: e.scalar_tensor_tensor(xv, xv, DN_ALPHA, PP[pp].rearrange("p a n -> p (a n)"), ALU.mult, ALU.add),
                 reads=[bx1s[a_][t], bPP[pp]], writes=[bx1s[a_][t]])

        def Q2(s, t):
            a_ = s % 2
            ln_stats(x1s[a_][:, t, :], bx1s[a_][t], sc3[t % 2], bsc3[t % 2])

        def Q3(s, t):
            a_ = s % 2
            tt = s * 4 + t
            i = t % 2
            ln_apply(x1s[a_][:, t, :], bx1s[a_][t], sc3[i], bsc3[i], o[i], bo[i], 2)
            c.dma("pool", dst[tt * 128:(tt + 1) * 128, :], o[i], bod[i], reads=[bo[i]], writes=[g.b_out])
            if not last:
                cp(c, "pool", xb3[t % 2], o[i], [bo[i]], [bxb3[t % 2]])

        def Q4(s, t):
            transposes(xb3[t % 2], bxb3[t % 2], xTn, bxTn, t)
            if t == 3:
                c.dma("act", xT_v[:, :, s * TS:(s + 1) * TS], xTn, bxTn, reads=[bxTn], writes=[g.b_scr])

        def stage2(s, hooks):
            for j in range(NJ):
                wi = nxt("nj", NWGU)
                pp = nxt("pp", 3)
                gi = j % 2
                c.dma("sp", wgu[wi][:, 0], g.wgu[l, 0, j], bwgu[wi], reads=[g.b_scr], writes=[bwgu[wi]])
                c.dma("sp", wgu[wi][:, 1], g.wgu[l, 1, j], bwgu[wi], reads=[g.b_scr], writes=[bwgu[wi]])
                for gu in range(2):
                    for kc in range(8):
                        c.op("pe", lambda e: e.matmul(PP[pp][:, gu, :], wgu[wi][:, gu, kc, :], x1T[:, kc, :],
                                                      start=(kc == 0), stop=(kc == 7)),
                             reads=[bwgu[wi], bx1T], writes=[bPP[pp]], signal=(kc == 7))
                c.op("act", lambda e: e.activation(sg[gi], PP[pp][:, 0, :], AF.Silu), reads=[bPP[pp]], writes=[bsg[gi]])
                c.op("dve", lambda e: e.tensor_tensor(aT[:, j, :], sg[gi], PP[pp][:, 1, :], ALU.mult),
                     reads=[bsg[gi], bPP[pp]], writes=[baT])
                for h_ in hooks.get(j, ()):
                    h_()

        T4 = [0, 1, 2, 3]
        load_mx(0)
        skew([lambda t: P1(0, t), lambda t: P2(0, t), lambda t: P3(0, t), lambda t: P4(0, t)], T4)
        for s in range(NS):
            hooks = {}
            if s > 0 and not last:
                for t in (2, 3):
                    hooks.setdefault(t, []).append(lambda t=t: Q4(s - 1, t))
            if s + 1 < NS:
                hooks.setdefault(12, []).append(lambda: load_mx(s + 1))
            stage2(s, hooks)
            if s + 1 < NS:
                skew([lambda t: P1(s + 1, t), lambda t: P2(s + 1, t), lambda t: P3(s + 1, t)], T4)
            nx = s + 1 < NS
            Q1(s, 0)
            Q1(s, 1)
            Q2(s, 0)
            if nx:
                P4(s + 1, 0)
                P4(s + 1, 1)
            Q1(s, 2)
            Q2(s, 1)
            Q3(s, 0)
            if nx:
                P4(s + 1, 2)
                P4(s + 1, 3)
            Q1(s, 3)
            Q2(s, 2)
            Q3(s, 1)
            if not last:
                Q4(s, 0)
                Q4(s, 1)
            Q2(s, 3)
            Q3(s, 2)
            Q3(s, 3)
        if not last:
            for t in (2, 3):
                Q4(NS - 1, t)
        c.barrier()
        for b_ in (cw.bwout, cw.bwd, cw.blnp):
            g.c.persist.remove(b_)


def w_gen(g, es, layers):
    c = g.c
    st = [sb(es, g, "w_st%d" % i, [128, 8, 512], F32) for i in range(2)]
    wb = [sb(es, g, "w_wb%d" % i, [128, 8, 512], BF16) for i in range(2)]
    bst = [Buf("wst%d" % i) for i in range(2)]
    bwb = [Buf("wwb%d" % i) for i in range(2)]
    n = 0
    for l in layers:
        for gu, w in enumerate((g.wg, g.wu)):
            wv = w[l].rearrange("(kc p) n -> p kc n", p=128)
            for c0 in range(0, DFF, 512):
                cw_ = min(512, DFF - c0)
                i = n % 2
                c.dma("sp", st[i][:, :, 0:cw_], wv[:, :, c0:c0 + cw_], bst[i], writes=[bst[i]])
                cp(c, "pool", wb[i][:, :, 0:cw_], st[i][:, :, 0:cw_], [bst[i]], [bwb[i]])
                for jj in range(cw_ // 128):
                    j = c0 // 128 + jj
                    c.dma("sp", g.wgu[l, gu, j], wb[i][:, :, jj * 128:(jj + 1) * 128], bwb[i],
                          reads=[bwb[i]], writes=[g.b_scr])
                n += 1
                yield


IN_SPECS = [
    ("x", [S, D]), ("w_in", [DEPTH, D, IN_W]), ("w_uq", [DEPTH, 256, 576]), ("w_ukv", [DEPTH, 128, 768]),
    ("qn", [DEPTH, 256]), ("kvn", [DEPTH, 128]), ("d_ggain", [DEPTH, 384]), ("w_out", [DEPTH, D, D]),
    ("wg", [DEPTH, D, DFF]), ("wu", [DEPTH, D, DFF]), ("wd", [DEPTH, DFF, D]),
    ("ln1g", [DEPTH, D]), ("ln1b", [DEPTH, D]), ("ln2g", [DEPTH, D]), ("ln2b", [DEPTH, D]),
    ("d_ident", [128, 128]), ("d_antiI", [128, 128]), ("d_ropeA", [128, NT, 2, 16]), ("d_ropeG", [128, NT, 2, 2, 16]),
    ("d_biasg", [6, LR]), ("d_logm", [6, LR]),
]
SCR_SPECS = [
    ("xT", [D, S], BF16), ("qaT", [6, 96, S], BF16), ("kaT", [6, 96, S], BF16), ("va", [6, 128, NT, 64], BF16),
    ("dqT", [3, 128, S], BF16), ("dkT", [3, 128, S], BF16), ("dv", [6, 128, NT, 64], BF16),
    ("gqT", [2, 128, S], BF16), ("gkT", [2, 128, S], BF16), ("gv", [2, 128, NT, 64], BF16),
    ("mixT", [D, S], BF16), ("xres1", [S, D], F32), ("wgu", [DEPTH, 2, NJ, 128, 8, 128], BF16), ("rtab", [6, LR], BF16),
]


def attn_groups(g):
    mla = [dict(q=g.qaT[h], k=g.kaT[h], kp=96, heads=[dict(p0=0, K=96, v=g.va[h], row0=h * 64, tall=None)]) for h in range(6)]
    dil = [dict(q=g.dqT[j], k=g.dkT[j], kp=128,
                heads=[dict(p0=64 * e, K=64, v=g.dv[2 * j + e], row0=384 + (2 * j + e) * 64, tall=2 * j + e) for e in range(2)])
           for j in range(3)]
    gqa = [dict(q=g.gqT[j], k=g.gkT[j], kp=128,
                heads=[dict(p0=64 * e, K=64, v=g.gv[j], row0=768 + (2 * j + e) * 64, tall=None) for e in range(2)])
           for j in range(2)]
    return mla, dil, gqa


def build(phases=None, dbg=False):
    nc = bass.Bass("TRN2", target_bir_lowering=False)
    g = G()
    g.nc = nc
    for name, shape in IN_SPECS:
        setattr(g, name, nc.dram_tensor(name, shape, F32, kind="ExternalInput").ap())
    for name, shape, dt in SCR_SPECS:
        setattr(g, name, nc.dram_tensor(name, shape, dt, kind=("ExternalOutput" if dbg else "Internal")).ap())
    g.out = nc.dram_tensor("out", [S, D], F32, kind="ExternalOutput").ap()
    g.c = Ctx(nc)
    g.b_scr = Buf("scr")
    g.b_out = Buf("out")
    g.c.persist = [g.b_scr, g.b_out]
    run = (lambda p: True) if phases is None else (lambda p: p in phases)
    with ExitStack() as es:
        phase_consts(g, es)
        g.c.persist += [g.b_const, g.b_mh]
        if run("b0"):
            phase_b0(g)
        mla, dil, gqa = attn_groups(g)
        for l in range(DEPTH):
            with ExitStack() as es_a:
                w1 = a1_weights(g, es_a, l) if run("a1_%d" % l) else None
                aw = a2_weights(g, es_a, l) if run("a2_%d" % l) else None
                if w1 is not None:
                    g.c.persist += w1.bufs
                    w1.load()
                if aw is not None:
                    g.c.persist += aw.bufs
                if l == 0 and run("x0"):
                    phase_x0(g)
                if w1 is not None:
                    phase_a1(g, l, w1, after_loads=(aw.load if aw is not None else None))
                    for b_ in w1.bufs:
                        g.c.persist.remove(b_)
                elif aw is not None:
                    aw.load()
                if aw is not None:
                    phase_a2(g, l, aw)
                    for b_ in aw.bufs:
                        g.c.persist.remove(b_)
            if run("mla_%d" % l):
                phase_attn(g, "ma", mla, 96.0 ** -0.5, False, side=(lambda es_, l=l: w_gen(g, es_, [l])))
            if run("dil_%d" % l):
                phase_attn(g, "md", dil, 0.125, True)
            with ExitStack() as es_c:
                cw = c_weights(g, es_c, l) if run("c_%d" % l) else None
                if run("gqa_%d" % l):
                    phase_attn(g, "mg", gqa, 0.125, False, side=(cw.loader if cw is not None else None))
                elif cw is not None:
                    for _ in cw.loader():
                        pass
                if run("c_%d" % l):
                    phase_c(g, l, l == DEPTH - 1, cw)
        g.c.barrier()
    return nc


def host_inputs(inputs):
    f = lambda a: np.ascontiguousarray(np.asarray(a, dtype=np.float32))
    ropeA, ropeG = _rope_tables()
    biasg, logm = _dil_tables(f(inputs["rel_bias"]))
    gq, gk = f(inputs["gqa_q_norm"]), f(inputs["gqa_k_norm"])
    ggain = np.concatenate([np.tile(gq, (1, 4)), np.tile(gk, (1, 2))], axis=1)
    common = {
        "w_in": f(inputs["w_in"]), "w_uq": f(inputs["mla_w_uq"]), "w_ukv": f(inputs["mla_w_ukv"]),
        "qn": f(inputs["mla_q_norm"]), "kvn": f(inputs["mla_kv_norm"]), "d_ggain": f(ggain),
        "w_out": f(inputs["w_out"]), "wg": f(inputs["ffn_w_gate"]), "wu": f(inputs["ffn_w_up"]), "wd": f(inputs["ffn_w_down"]),
        "ln1g": f(inputs["ln1_g"]), "ln1b": f(inputs["ln1_b"]), "ln2g": f(inputs["ln2_g"]), "ln2b": f(inputs["ln2_b"]),
        "d_ident": np.eye(128, dtype=np.float32), "d_antiI": np.ascontiguousarray(np.eye(128, dtype=np.float32)[::-1]),
        "d_ropeA": ropeA, "d_ropeG": ropeG, "d_biasg": biasg, "d_logm": logm,
    }
    x = f(inputs["x"])
    return [dict(common, x=np.ascontiguousarray(x[b])) for b in range(x.shape[0])]


_NC_CACHE = {}


def kernel(**inputs):
    in_maps = host_inputs(inputs)
    if "nc" not in _NC_CACHE:
        _NC_CACHE["nc"] = build()
    res = run_bass_kernel_spmd(_NC_CACHE["nc"], in_maps, core_ids=list(range(len(in_maps))))
    return np.stack([np.asarray(r["out"], dtype=np.float32) for r in res.results], axis=0)
```
